# Optimizing a Trainium2 kernel written in Bass

```python
import math
import jax, jax.numpy as jnp
from jax import lax
import numpy as np


D_MODEL = 1024
BATCH = 8
SEQ = 2048
DEPTH = 4
DEC_BATCH = 128
DEC_SEQ = 4
PAST_LEN = 16384
PAGE_SIZE = 128

D_SSD = D_MODEL
SSD_HEAD_DIM = 64
SSD_HEADS = D_SSD // SSD_HEAD_DIM
SSD_GROUPS = 2
HEADS_PER_GROUP = SSD_HEADS // SSD_GROUPS
D_STATE = 128
CONV_WIDTH = 4
CONV_DIM = D_SSD + 2 * SSD_GROUPS * D_STATE
SSD_CHUNK = 128
POOL_WIDTH = D_MODEL
POOL_WINDOWS = (2, 4, 8, 16)
POOL_GROUPS = len(POOL_WINDOWS)
POOL_GROUP_DIM = POOL_WIDTH // POOL_GROUPS
POOL_BUF = max(POOL_WINDOWS) - 1
D_MIX = D_SSD + POOL_WIDTH
D_IN_PROJ = D_SSD + CONV_DIM + SSD_HEADS + POOL_WIDTH
D_FF = 2816
PLE_DIM = 256
EPS = 1e-6

kernel_name = "hymba_ssd_multiscale_pool_macaron_decoder_step"


def rms_norm(x, w):
    xf = x.astype(jnp.float32)
    y = xf * lax.rsqrt(jnp.mean(xf * xf, axis=-1, keepdims=True) + EPS)
    return (y * w.astype(jnp.float32)).astype(x.dtype)


def swiglu(u, wg, wu, wd):
    return (jax.nn.silu(u @ wg) * (u @ wu)) @ wd


def causal_dwconv(x_ext, w, b):
    c = x_ext.shape[-1]
    y = lax.conv_general_dilated(x_ext, w[:, None, :].astype(x_ext.dtype), window_strides=(1,), padding='VALID',
                                 dimension_numbers=('NWC', 'WIO', 'NWC'), feature_group_count=c)
    return y + b.astype(x_ext.dtype)


def ssd_scan(xh, dt, a, bm, cm, h0):
    b, l, g, r, p = xh.shape
    n = bm.shape[-1]
    q = math.gcd(l, SSD_CHUNK)
    c = l // q
    f32 = jnp.float32
    dt = dt.astype(f32)
    da = (dt * a).reshape(b, c, q, g, r)
    xdt = (xh.astype(f32) * dt[..., None]).reshape(b, c, q, g, r, p)
    bc = bm.astype(f32).reshape(b, c, q, g, n)
    cc = cm.astype(f32).reshape(b, c, q, g, n)
    a_cum = jnp.cumsum(da, axis=2)
    causal = jnp.tril(jnp.ones((q, q), bool))[None, None, :, :, None, None]
    seg = a_cum[:, :, :, None] - a_cum[:, :, None, :]
    decay_in = jnp.where(causal, jnp.exp(jnp.where(causal, seg, 0.0)), 0.0)
    scores = jnp.einsum('bcqgn,bcsgn->bcqsg', cc, bc)
    w_in_chunk = scores[..., None] * decay_in
    y_diag = jnp.einsum('bcqsgr,bcsgrp->bcqgrp', w_in_chunk, xdt)
    decay_out = jnp.exp(a_cum[:, :, -1:] - a_cum)
    chunk_states = jnp.einsum('bcsgn,bcsgrp->bcgrpn', bc, xdt * decay_out[..., None])
    chunk_decay = jnp.exp(a_cum[:, :, -1])

    def step(h_prev, inp):
        cs, cd = inp
        return h_prev * cd[..., None, None] + cs, h_prev

    h_final, h_prev = lax.scan(step, h0.astype(f32),
                               (jnp.moveaxis(chunk_states, 1, 0), jnp.moveaxis(chunk_decay, 1, 0)))
    h_prev = jnp.moveaxis(h_prev, 0, 1)
    y_off = jnp.einsum('bcqgn,bcgrpn->bcqgrp', cc, h_prev) * jnp.exp(a_cum)[..., None]
    y = (y_diag + y_off).reshape(b, l, g, r, p)
    return y, h_final


def multiscale_pool(x_ext, n_valid, pool_w, pool_scale):
    b, le, ch = x_ext.shape
    l = le - POOL_BUF
    xf = x_ext.astype(jnp.float32)
    cs = jnp.concatenate([jnp.zeros((b, 1, ch), jnp.float32), jnp.cumsum(xf, axis=1)], axis=1)
    t = jnp.arange(l)
    outs = []
    for gi, w in enumerate(POOL_WINDOWS):
        lo, hi = gi * POOL_GROUP_DIM, (gi + 1) * POOL_GROUP_DIM
        s = cs[:, POOL_BUF + 1:, lo:hi] - cs[:, POOL_BUF + 1 - w:POOL_BUF + 1 - w + l, lo:hi]
        cnt = jnp.minimum(n_valid + t + 1, w).astype(jnp.float32)
        outs.append(s / cnt[None, :, None] - xf[:, POOL_BUF:, lo:hi])
    d = jnp.stack(outs, axis=2)
    y = jnp.einsum('blgc,gcd->blgd', d, pool_w.astype(jnp.float32)).reshape(b, l, ch)
    return (y * pool_scale.astype(jnp.float32)).astype(x_ext.dtype)


def mixer(u, conv_buf, h0, pool_buf, n_valid_pool, w_in, conv_w, conv_b, dt_bias, a_log, d_skip,
          ssd_norm_w, pool_w, pool_scale, w_out):
    b, l, _ = u.shape
    proj = u @ w_in
    o1, o2, o3 = D_SSD, D_SSD + CONV_DIM, D_SSD + CONV_DIM + SSD_HEADS
    z, xbc, dt, xp = proj[..., :o1], proj[..., o1:o2], proj[..., o2:o3], proj[..., o3:]
    xbc_ext = jnp.concatenate([conv_buf.astype(xbc.dtype), xbc], axis=1)
    new_conv = xbc_ext[:, -(CONV_WIDTH - 1):]
    xbc_c = jax.nn.silu(causal_dwconv(xbc_ext, conv_w, conv_b))
    xs = xbc_c[..., :D_SSD].reshape(b, l, SSD_GROUPS, HEADS_PER_GROUP, SSD_HEAD_DIM)
    gn = SSD_GROUPS * D_STATE
    bm = xbc_c[..., D_SSD:D_SSD + gn].reshape(b, l, SSD_GROUPS, D_STATE)
    cm = xbc_c[..., D_SSD + gn:].reshape(b, l, SSD_GROUPS, D_STATE)
    dtp = jax.nn.softplus((dt + dt_bias).astype(jnp.float32)).reshape(b, l, SSD_GROUPS, HEADS_PER_GROUP)
    a = -jnp.exp(a_log.astype(jnp.float32)).reshape(SSD_GROUPS, HEADS_PER_GROUP)
    h0g = h0.reshape(b, SSD_GROUPS, HEADS_PER_GROUP, SSD_HEAD_DIM, D_STATE)
    y, h = ssd_scan(xs, dtp, a, bm, cm, h0g)
    y = y + d_skip.astype(jnp.float32).reshape(SSD_GROUPS, HEADS_PER_GROUP)[..., None] * xs.astype(jnp.float32)
    y = y.reshape(b, l, D_SSD).astype(u.dtype)
    y = rms_norm(y * jax.nn.silu(z), ssd_norm_w)
    pool_ext = jnp.concatenate([pool_buf.astype(xp.dtype), xp], axis=1)
    new_pool = pool_ext[:, -POOL_BUF:]
    yp = multiscale_pool(pool_ext, n_valid_pool, pool_w, pool_scale)
    out = jnp.concatenate([y, yp], axis=-1) @ w_out
    new_h = h.reshape(b, SSD_HEADS, SSD_HEAD_DIM, D_STATE).astype(h0.dtype)
    return out, new_conv, new_h, new_pool


def trunk(x, p, ssm0, conv0, pool0, n_valid_pool, w_in, conv_w, conv_b, dt_bias, a_log, d_skip, ssd_norm_w,
          pool_w, pool_scale, w_out, norm_ffn1, ffn1_gate, ffn1_up, ffn1_down, norm_mix, norm_ffn2,
          ffn2_gate, ffn2_up, ffn2_down, norm_ple, ple_gate, ple_proj, final_norm):
    ssm_out, conv_out, pool_out = [], [], []
    for i in range(DEPTH):
        x = x + 0.5 * swiglu(rms_norm(x, norm_ffn1[i]), ffn1_gate[i], ffn1_up[i], ffn1_down[i])
        m, nc, nh, npool = mixer(rms_norm(x, norm_mix[i]), conv0[i], ssm0[i], pool0[i], n_valid_pool,
                                 w_in[i], conv_w[i], conv_b[i], dt_bias[i], a_log[i], d_skip[i],
                                 ssd_norm_w[i], pool_w[i], pool_scale[i], w_out[i])
        x = x + m
        x = x + 0.5 * swiglu(rms_norm(x, norm_ffn2[i]), ffn2_gate[i], ffn2_up[i], ffn2_down[i])
        gate = jax.nn.sigmoid(rms_norm(x, norm_ple[i]) @ ple_gate[i])
        x = x + gate * (p[i].astype(x.dtype) @ ple_proj[i])
        ssm_out.append(nh)
        conv_out.append(nc)
        pool_out.append(npool)
    y = rms_norm(x, final_norm)
    return y, jnp.stack(ssm_out), jnp.stack(conv_out), jnp.stack(pool_out)


def setup_inputs(seed: int = 0) -> dict:
    key = jax.random.key(seed)
    ks = jax.random.split(key, 32)
    f32 = jnp.float32

    def nrm(k, shape, fan_in):
        return jax.random.normal(k, shape, f32) * (fan_in ** -0.5)

    def gain(k, shape):
        return 1.0 + 0.02 * jax.random.normal(k, shape, f32)

    dt0 = jnp.exp(jax.random.uniform(ks[10], (DEPTH, SSD_HEADS), f32) * (math.log(0.1) - math.log(0.001))
                  + math.log(0.001))
    return {
        "x_prompt": jax.random.normal(ks[0], (BATCH, SEQ, D_MODEL), f32),
        "x_sample": jax.random.normal(ks[1], (DEC_BATCH, DEC_SEQ, D_MODEL), f32),
        "p_prompt": jax.random.normal(ks[2], (DEPTH, BATCH, SEQ, PLE_DIM), f32),
        "p_sample": jax.random.normal(ks[3], (DEPTH, DEC_BATCH, DEC_SEQ, PLE_DIM), f32),
        "state_ssm": 0.1 * jax.random.normal(ks[4], (DEPTH, DEC_BATCH, SSD_HEADS, SSD_HEAD_DIM, D_STATE), f32),
        "state_conv": jax.random.normal(ks[5], (DEPTH, DEC_BATCH, CONV_WIDTH - 1, CONV_DIM), f32),
        "state_pool": jax.random.normal(ks[6], (DEPTH, DEC_BATCH, POOL_BUF, POOL_WIDTH), f32),
        "w_in": nrm(ks[7], (DEPTH, D_MODEL, D_IN_PROJ), D_MODEL),
        "conv_w": nrm(ks[8], (DEPTH, CONV_WIDTH, CONV_DIM), CONV_WIDTH),
        "conv_b": 0.01 * jax.random.normal(ks[9], (DEPTH, CONV_DIM), f32),
        "dt_bias": dt0 + jnp.log(-jnp.expm1(-dt0)),
        "a_log": jnp.log(jax.random.uniform(ks[11], (DEPTH, SSD_HEADS), f32, 1.0, 16.0)),
        "d_skip": gain(ks[12], (DEPTH, SSD_HEADS)),
        "ssd_norm_w": gain(ks[13], (DEPTH, D_SSD)),
        "pool_w": nrm(ks[14], (DEPTH, POOL_GROUPS, POOL_GROUP_DIM, POOL_GROUP_DIM), POOL_GROUP_DIM),
        "pool_scale": gain(ks[15], (DEPTH, POOL_WIDTH)),
        "w_out": nrm(ks[16], (DEPTH, D_MIX, D_MODEL), D_MIX),
        "norm_ffn1": gain(ks[17], (DEPTH, D_MODEL)),
        "ffn1_gate": nrm(ks[18], (DEPTH, D_MODEL, D_FF), D_MODEL),
        "ffn1_up": nrm(ks[19], (DEPTH, D_MODEL, D_FF), D_MODEL),
        "ffn1_down": nrm(ks[20], (DEPTH, D_FF, D_MODEL), D_FF),
        "norm_mix": gain(ks[21], (DEPTH, D_MODEL)),
        "norm_ffn2": gain(ks[22], (DEPTH, D_MODEL)),
        "ffn2_gate": nrm(ks[23], (DEPTH, D_MODEL, D_FF), D_MODEL),
        "ffn2_up": nrm(ks[24], (DEPTH, D_MODEL, D_FF), D_MODEL),
        "ffn2_down": nrm(ks[25], (DEPTH, D_FF, D_MODEL), D_FF),
        "norm_ple": gain(ks[26], (DEPTH, D_MODEL)),
        "ple_gate": nrm(ks[27], (DEPTH, D_MODEL, D_MODEL), D_MODEL),
        "ple_proj": nrm(ks[28], (DEPTH, PLE_DIM, D_MODEL), PLE_DIM),
        "final_norm": gain(ks[29], (D_MODEL,)),
    }


def reference(x_prompt, x_sample, p_prompt, p_sample, state_ssm, state_conv, state_pool, w_in, conv_w, conv_b,
              dt_bias, a_log, d_skip, ssd_norm_w, pool_w, pool_scale, w_out, norm_ffn1, ffn1_gate, ffn1_up,
              ffn1_down, norm_mix, norm_ffn2, ffn2_gate, ffn2_up, ffn2_down, norm_ple, ple_gate, ple_proj,
              final_norm):
    dt_ = x_prompt.dtype
    b = x_prompt.shape[0]
    ssm0 = jnp.zeros((DEPTH, b, SSD_HEADS, SSD_HEAD_DIM, D_STATE), dt_)
    conv0 = jnp.zeros((DEPTH, b, CONV_WIDTH - 1, CONV_DIM), dt_)
    pool0 = jnp.zeros((DEPTH, b, POOL_BUF, POOL_WIDTH), dt_)
    weights = (w_in, conv_w, conv_b, dt_bias, a_log, d_skip, ssd_norm_w, pool_w, pool_scale, w_out,
               norm_ffn1, ffn1_gate, ffn1_up, ffn1_down, norm_mix, norm_ffn2, ffn2_gate, ffn2_up, ffn2_down,
               norm_ple, ple_gate, ple_proj, final_norm)
    y_prompt, ssm_prompt, conv_prompt, pool_prompt = trunk(x_prompt, p_prompt, ssm0, conv0, pool0, 0, *weights)
    y_sample, ssm_sample, conv_sample, pool_sample = trunk(x_sample, p_sample, state_ssm, state_conv, state_pool,
                                                           min(PAST_LEN, POOL_BUF), *weights)
    return (y_prompt, y_sample, ssm_prompt, conv_prompt, pool_prompt, ssm_sample, conv_sample, pool_sample)
```

```python
import os
from contextlib import ExitStack
from math import prod

import numpy as np
import concourse.bass as bass
import concourse.mybir as mybir
from concourse.bass_utils import run_bass_kernel_spmd

F32 = mybir.dt.float32
BF16 = mybir.dt.bfloat16
AF = mybir.ActivationFunctionType
ALU = mybir.AluOpType
AX = mybir.AxisListType

NCORE = 8
L = 4
D = 1024
NPR = 2048
NSM = 64
T = NPR + NSM
DFF = 2816
DIN = 3600
CONV = 1536
EPSV = 1e-6
TILES = [(0, 512), (512, 512), (1024, 512), (1536, 512), (2048, 64)]
PL = 142
NPRM = L * PL + 8
NCST = 656
ENGS = ("pe", "act", "dve", "pool", "sp")


class Sched:
    def __init__(self):
        self.ops = []
        self.last_writer = {}
        self.readers = {}
        self.chan_last = {}
        self.fence_deps = set()
        self.fenced = set()
        self.arena_touch = {}

    def add(self, eng, fn, reads=(), writes=(), chan=None):
        idx = len(self.ops)
        if idx >= int(os.environ.get("MK_MAXOPS", 10 ** 9)) and not getattr(self, "nolimit", False):
            self.nolimit = True
            print("MAXOPS stop at", idx, eng, reads, writes)
            raise _Stop()
        deps = set()
        touches = False
        if eng != "pe":
            writes = list(writes) + [k for k in reads if k[0] == "ps" and k not in writes]
        for k in list(reads) + list(writes):
            if k[0] == "A":
                touches = True
                if k not in self.fenced:
                    deps.update(self.fence_deps)
                    self.fenced.add(k)
        for k in reads:
            if k in self.last_writer:
                deps.add(self.last_writer[k])
        for k in writes:
            if k in self.last_writer:
                deps.add(self.last_writer[k])
            deps.update(self.readers.get(k, ()))
        if chan is not None and chan in self.chan_last:
            deps.add(self.chan_last[chan])
        deps.discard(idx)
        for k in reads:
            self.readers.setdefault(k, []).append(idx)
        for k in writes:
            self.last_writer[k] = idx
            self.readers[k] = []
        if chan is not None:
            self.chan_last[chan] = idx
        if touches:
            self.arena_touch[chan if chan is not None else eng] = idx
        self.ops.append(dict(eng=eng, fn=fn, deps=deps, chan=chan, signal=False))
        return idx

    def fence(self):
        allp = set(self.fence_deps) | set(self.arena_touch.values())
        best = {}
        for i in allp:
            o = self.ops[i]
            k = o["chan"] if o["chan"] is not None else o["eng"]
            if k not in best or best[k] < i:
                best[k] = i
        self.fence_deps = set(best.values())
        self.fenced = set()
        self.arena_touch = {}
        for k in [k for k in self.last_writer if k[0] == "A"]:
            del self.last_writer[k]
        for k in [k for k in self.readers if k[0] == "A"]:
            del self.readers[k]

    def emit(self, nc, es):
        ops = self.ops
        for o in ops:
            nd = set()
            for d in o["deps"]:
                p = ops[d]
                if o["eng"] == "pe" and p["eng"] == "pe" and p["chan"] is None and o["chan"] is None:
                    continue
                nd.add(d)
                p["signal"] = True
            o["deps"] = nd
        chans = sorted({o["chan"] for o in ops if o["chan"] is not None})
        sems = {}
        for e in ENGS:
            sems[e] = es.enter_context(nc.semaphore("s_" + e))
        for c in chans:
            sems[c] = es.enter_context(nc.semaphore("c_" + c))
        cnt = {k: 0 for k in sems}
        for o in ops:
            if o["chan"] is not None:
                cnt[o["chan"]] += 16
                o["tick"] = (o["chan"], cnt[o["chan"]])
            elif o["signal"]:
                cnt[o["eng"]] += 1
                o["tick"] = (o["eng"], cnt[o["eng"]])
        block = es.enter_context(nc.Block())

        def run(engname, e):
            waited = {}
            for o in ops:
                if o["eng"] != engname:
                    continue
                need = {}
                for d in o["deps"]:
                    s, v = ops[d]["tick"]
                    if need.get(s, 0) < v:
                        need[s] = v
                for s, v in need.items():
                    if waited.get(s, 0) < v:
                        e.wait_ge(sems[s], v)
                        waited[s] = v
                ins = o["fn"](e)
                if o["chan"] is not None:
                    ins.then_inc(sems[o["chan"]], 16)
                elif o["signal"]:
                    ins.then_inc(sems[o["eng"]], 1)
            if engname == "sp":
                for s, v in cnt.items():
                    if v > 0 and waited.get(s, 0) < v:
                        e.wait_ge(sems[s], v)

        @block.tensor
        def _(e):
            run("pe", e)

        @block.scalar
        def _(e):
            run("act", e)

        @block.vector
        def _(e):
            run("dve", e)

        @block.gpsimd
        def _(e):
            run("pool", e)

        @block.sync
        def _(e):
            run("sp", e)


class _Stop(Exception):
    pass


def build_nc(nlayers=L):
    stop_at = int(os.environ.get("MK_STOP", 999))
    stage = [0]

    def ckpt():
        stage[0] += 1
        if stage[0] >= stop_at:
            raise _Stop()

    sub_at = int(os.environ.get("MK_SUB", 999))

    def sub(k):
        if k >= sub_at:
            raise _Stop()

    nc = bass.Bass("TRN2", target_bir_lowering=False)
    S = Sched()
    es = ExitStack()

    def din(name, shape):
        return nc.dram_tensor(name, list(shape), F32, kind="ExternalInput").ap()

    def dout(name, shape):
        return nc.dram_tensor(name, list(shape), F32, kind="ExternalOutput").ap()

    xin = din("xin", [T, D])
    pin = din("pin", [L, T, 256])
    sssm = din("sssm", [L, 16, 1024, 128])
    sconv = din("sconv", [L, 48, CONV])
    spool = din("spool", [L, 240, 1024])
    w_in = din("w_in", [L, D, DIN])
    w_out = din("w_out", [L, 2048, D])
    pool_w = din("pool_w", [L, 4, 256, 256])
    fg = [din("ffn1_gate", [L, D, DFF]), din("ffn2_gate", [L, D, DFF])]
    fu = [din("ffn1_up", [L, D, DFF]), din("ffn2_up", [L, D, DFF])]
    fd = [din("ffn1_down", [L, DFF, D]), din("ffn2_down", [L, DFF, D])]
    ple_gate = din("ple_gate", [L, D, D])
    ple_proj = din("ple_proj", [L, 256, D])
    prm_d = din("prm", [128, NPRM])
    cst_d = din("cst", [128, NCST])
    e16_d = din("e16", [16, 1024])

    y_o = dout("y", [T, D])
    ssm_p_o = dout("ssm_p", [L, 1024, 128])
    conv_p_o = dout("conv_p", [L, 3, CONV])
    pool_p_o = dout("pool_p", [L, 15, 1024])
    ssm_s_o = dout("ssm_s", [L, 16, 1024, 128])
    conv_s_o = dout("conv_s", [L, 48, CONV])
    pool_s_o = dout("pool_s", [L, 240, 1024])

    with es:
        def sb(name, shape, dt=F32):
            return es.enter_context(nc.sbuf_tensor(name, list(shape), dt))

        X = sb("X", [128, 8, T])
        RING = sb("RING", [128, 2, 6144], BF16)
        ARENA_B = 92928
        ARENA = sb("ARENA", [128, ARENA_B // 4])
        HST = sb("HST", [128, 1024])
        HB = sb("HB", [128, 1024], BF16)
        HISTC = sb("HISTC", [128, 12, 3])
        HISTP = sb("HISTP", [128, 8, 15])
        PRM = sb("PRM", [128, NPRM])
        CST = sb("CST", [128, NCST])
        IDB = sb("IDB", [128, 128], BF16)
        ONESB = sb("ONESB", [128, 128], BF16)
        ONESF = sb("ONESF", [128, 128])
        EPS = sb("EPS", [128, 1])
        A_B = sb("A_B", [128, L, 16])
        ACOL = sb("ACOL", [16, L])
        PS = es.enter_context(nc.psum_tensor("PS", [128, 8, 512], F32))

        IDF = CST[:, 0:128]
        TRI = CST[:, 128:256]
        TRISEG = CST[:, 256:320]
        BLK = CST[:, 320:384]
        MASKCOL = CST[:, 384:400]
        RC = CST[:, 400:464]
        STRICT = CST[:, 464:592]
        STRICTSEG = CST[:, 592:656]

        def av(off_b, shape, dt=F32):
            nel = prod(shape[1:])
            nb = nel * (4 if dt == F32 else 2)
            assert off_b % 4 == 0 and nb % 4 == 0 and off_b + nb <= ARENA_B, (off_b, shape)
            v = ARENA[0:shape[0], off_b // 4: (off_b + nb) // 4]
            if dt == BF16:
                v = v.bitcast(BF16)
            if len(shape) == 3:
                v = v.rearrange("p (a b) -> p a b", a=shape[1])
            elif len(shape) == 4:
                v = v.rearrange("p (a b c) -> p a b c", a=shape[1], b=shape[2])
            return v

        U = av(0, [128, 8, T], BF16)
        HH = av(33792, [128, 2, 2, T], BF16)
        SG = av(50688, [128, 2, 512], BF16)
        SQ = av(62976, [128, 8, 512], BF16)
        RS = av(71168, [128, 512])
        PT = av(73216, [128, 2, T], BF16)
        SGT = av(81664, [128, 512])
        TMP2 = av(83712, [128, 512])
        PSTG = av(85760, [128, 2, 256])
        XSTG = av(0, [128, 2, 1024])
        US = av(0, [128, 8, 1024], BF16)
        MIX = av(16384, [128, 12, 1024], BF16)
        ZB = av(40960, [128, 8, 1024], BF16)
        DTT = av(57344, [16, 1024])
        SC = 61440
        PA = [av(SC, [128, 1040]), av(SC + 4160, [128, 1040])]
        PP1 = av(SC + 8320, [128, 1040])
        PP2 = av(SC + 12480, [128, 1040])
        POUT = av(SC + 16640, [128, 1024])
        PSTGC = av(SC + 20736, [128, 2, 256])
        PASTP = av(SC + 22784, [128, 2, 240])
        SNEW = av(SC + 24704, [128, 2, 64])
        TMP16 = av(SC + 25216, [128, 16])
        RAW = [av(SC, [128, 1028]), av(SC + 4112, [128, 1028])]
        ACC = av(SC + 8224, [128, 1024])
        CSTG = av(SC + 12320, [128, 1536])
        PASTT = av(SC + 18464, [128, 12, 48])
        SELC = av(SC + 20768, [128, 2, 48])
        BIG = av(SC, [128, 1024])
        WW = av(SC + 4096, [128, 1024], BF16)
        XDT = av(SC + 6144, [128, 1024], BF16)
        XDTD = av(SC + 8192, [128, 1024], BF16)
        XSD = av(SC + 10240, [128, 1024], BF16)
        YT = av(SC + 12288, [128, 1024])
        GN = av(SC + 16384, [128, 1024], BF16)
        BTOK = av(SC + 18432, [128, 256], BF16)
        SM = av(SC + 18944, [128, 2, 128])
        SML = av(SC + 19968, [128, 16, 16])
        H0X = av(SC + 20992, [128, 1024])
        CMS = av(SC + 25088, [128, 2, 2, 64], BF16)
        BMS = av(SC + 25600, [128, 2, 256], BF16)
        CDF = av(SC + 26624, [128, 8, 16])
        E16 = av(SC + 27136, [16, 1024])
        SETS = [
            (BIG, WW, XDT, XDTD, XSD, YT, GN, BTOK, SM, SML),
            (av(0, [128, 1024]), av(4096, [128, 1024], BF16), av(6144, [128, 1024], BF16), av(8192, [128, 1024], BF16),
             av(10240, [128, 1024], BF16), av(12288, [128, 1024]), av(SC + 20992, [128, 1024], BF16), av(SC + 23040, [128, 256], BF16),
             av(SC + 23552, [128, 2, 128]), av(SC + 24576, [128, 16, 16])),
        ]

        bank_cur = [0]

        held = set()

        def alloc(n=1, hold=False):
            c = bank_cur[0]
            for _ in range(32):
                if n > 1 and c % n:
                    c += n - c % n
                if c + n > 8:
                    c = 0
                if all((c + i) not in held for i in range(n)):
                    break
                c += 1
            else:
                raise RuntimeError("no free psum banks")
            bank_cur[0] = (c + n) % 8
            if hold:
                for i in range(n):
                    held.add(c + i)
            return c

        def release(b, n=1):
            for i in range(n):
                held.discard(b + i)

        def pk(b, n=1):
            return [("ps", b + i) for i in range(n)]

        ring_cur = [0]
        dma_flip = [0]

        def P(l, off, n=1):
            return PRM[:, l * PL + off: l * PL + off + n]

        def mm(out, pairs):
            def fn(e):
                n = len(pairs)
                for i, (a, b) in enumerate(pairs):
                    m = e.matmul(out, a, b, start=(i == 0), stop=(i == n - 1))
                return m
            return fn

        def mms(groups):
            def fn(e):
                for out, pairs in groups:
                    n = len(pairs)
                    for i, (a, b) in enumerate(pairs):
                        m = e.matmul(out, a, b, start=(i == 0), stop=(i == n - 1))
                return m
            return fn

        def act(out, in_, func, **kw):
            return lambda e: e.activation(out=out, in_=in_, func=func, **kw)

        def tt(out, a, b, op):
            return lambda e: e.tensor_tensor(out=out, in0=a, in1=b, op=op)

        def ts(out, a, s1, s2, op0, op1):
            return lambda e: e.tensor_scalar(out=out, in0=a, scalar1=s1, scalar2=s2, op0=op0, op1=op1)

        def stt(out, a, s, b, op0, op1):
            return lambda e: e.scalar_tensor_tensor(out=out, in0=a, scalar=s, in1=b, op0=op0, op1=op1)

        def cp(out, in_):
            return lambda e: e.tensor_copy(out, in_)

        def ms(ap, v):
            return lambda e: e.memset(ap, v)

        def dma(out, in_):
            return lambda e: e.dma_start(out=out, in_=in_)

        def xk(j, t):
            return ("x", j, t)

        S.add("sp", dma(PRM[:], prm_d[:, :]), writes=[("prm",)], chan="ld_prm")
        S.add("sp", dma(CST[:], cst_d[:, :]), writes=[("cst",)], chan="ld_cst")
        S.add("dve", cp(IDB[:], IDF), reads=[("cst",)], writes=[("idb",)])
        S.add("dve", ms(ONESB[:], 1.0 / 1024.0), writes=[("onesb",)])
        S.add("dve", ms(ONESF[:], 1.0), writes=[("onesf",)])
        S.add("dve", ms(EPS[:], EPSV), writes=[("eps",)])
        for l in range(L):
            S.add("act", act(A_B[:, l, :], P(l, 108, 16), AF.Exp), reads=[("prm",)], writes=[("ab0", l)])
            S.add("dve", ts(A_B[:, l, :], A_B[:, l, :], -1.0, 0.0, ALU.mult, ALU.add), reads=[("ab0", l)], writes=[("ab",)])
            S.add("act", act(ACOL[:, l:l + 1], PRM[0:16, l * PL + 141: l * PL + 142], AF.Exp), reads=[("prm",)], writes=[("ac0", l)])
            S.add("dve", ts(ACOL[:, l:l + 1], ACOL[:, l:l + 1], -1.0, 0.0, ALU.mult, ALU.add), reads=[("ac0", l)], writes=[("acol",)])
        CK = [("cst",), ("idb",), ("onesb",), ("onesf",), ("eps",), ("ab",), ("acol",), ("prm",)]

        for blk in range(17):
            r0 = blk * 128
            n = 128 if blk < 16 else 64
            t = blk // 4
            sl = blk % 2
            S.add("sp", dma(XSTG[0:n, sl, :], xin[r0:r0 + n, :]), writes=[("A", "xstg", sl)], chan="ld_x%d" % sl)
            for hb in range(2):
                b = alloc()
                S.add("pe", mms([(PS[:, b, jj * 128: jj * 128 + n],
                                  [(XSTG[0:n, sl, (hb * 4 + jj) * 128:(hb * 4 + jj + 1) * 128], IDF[0:n, 0:n])]) for jj in range(4)]),
                      reads=[("A", "xstg", sl)] + CK, writes=pk(b))
                src = PS[:, b, :].rearrange("p (j q) -> p j q", j=4)[:, :, 0:n]
                dst = X[:, hb * 4:hb * 4 + 4, r0:r0 + n]
                if (blk + hb) % 2 == 0:
                    S.add("act", act(dst, src, AF.Copy), reads=pk(b), writes=[xk(hb * 4 + jj, t) for jj in range(4)])
                else:
                    S.add("dve", cp(dst, src), reads=pk(b), writes=[xk(hb * 4 + jj, t) for jj in range(4)])
        S.fence()

        def norm(tiles, wl, woff, dst_fn, dkey_fn):
            for t in tiles:
                t0, n = TILES[t]
                S.add("act", act(SQ[:, :, 0:n], X[:, :, t0:t0 + n], AF.Square),
                      reads=[xk(j, t) for j in range(8)], writes=[("A", "sq")])
                b = alloc()
                S.add("pe", mm(PS[:, b, 0:n], [(ONESB[:], SQ[:, k, 0:n]) for k in range(8)]),
                      reads=[("A", "sq")] + CK, writes=pk(b))
                S.add("act", act(RS[:, 0:n], PS[:, b, 0:n], AF.Ln, bias=EPS[:, 0:1]), reads=pk(b) + CK, writes=[("A", "rs")])
                S.add("act", act(RS[:, 0:n], RS[:, 0:n], AF.Exp, scale=-0.5), reads=[("A", "rs")], writes=[("A", "rs")])
                for k in range(8):
                    wcol = PRM[:, wl * PL + woff + k: wl * PL + woff + k + 1] if wl is not None else PRM[:, L * PL + k: L * PL + k + 1]
                    S.add("dve", stt(dst_fn(k, t, n), X[:, k, t0:t0 + n], wcol, RS[:, 0:n], ALU.mult, ALU.mult),
                          reads=[xk(k, t), ("A", "rs")] + CK, writes=[dkey_fn(k, t)])

        def u_full(k, t, n):
            return U[:, k, TILES[t][0]:TILES[t][0] + n]

        def ukey(k, t):
            return ("A", "u", k, t)

        def ring_next():
            s = ring_cur[0]
            ring_cur[0] = 1 - s
            return s

        def wload(slot, part, view, src):
            S.add("pool", dma(view, src), writes=[("ring", slot, part)], chan="r%d%s" % (slot, part))

        def ffn(l, which):
            S.fence()
            norm(range(5), l, 0 if which == 0 else 16, u_full, ukey)
            NG = DFF // 256
            gd, ud, dd = fg[which], fu[which], fd[which]
            slots = {}

            def load(g):
                s = ring_next()
                slots[g] = s
                wload(s, "gu0", RING[:, s, 0:2048].rearrange("p (k n) -> p k n", k=8),
                      gd[l, :, g * 256:(g + 1) * 256].rearrange("(k p) n -> p k n", p=128))
                wload(s, "gu1", RING[:, s, 2048:4096].rearrange("p (k n) -> p k n", k=8),
                      ud[l, :, g * 256:(g + 1) * 256].rearrange("(k p) n -> p k n", p=128))
                wload(s, "d", RING[:, s, 4096:6144].rearrange("p (f n) -> p f n", f=2),
                      dd[l, g * 256:(g + 1) * 256, :].rearrange("(f p) n -> p f n", p=128))

            def gu(g):
                s = slots[g]
                hs = g % 2
                wg = RING[:, s, 0:2048].rearrange("p (k n) -> p k n", k=8)
                wu = RING[:, s, 2048:4096].rearrange("p (k n) -> p k n", k=8)
                for f in range(2):
                    for t in range(5):
                        t0, n = TILES[t]
                        b = alloc(2)
                        S.add("pe", mms([(PS[:, b, 0:n], [(wg[:, k, f * 128:(f + 1) * 128], U[:, k, t0:t0 + n]) for k in range(8)]),
                                         (PS[:, b + 1, 0:n], [(wu[:, k, f * 128:(f + 1) * 128], U[:, k, t0:t0 + n]) for k in range(8)])]),
                              reads=[("ring", s, "gu0"), ("ring", s, "gu1")] + [ukey(k, t) for k in range(8)], writes=pk(b, 2))
                        sgs = (f * 5 + t) % 2
                        S.add("act", act(SG[:, sgs, 0:n], PS[:, b, 0:n], AF.Silu), reads=pk(b), writes=[("A", "sg", sgs)])
                        S.add("dve", tt(HH[:, hs, f, t0:t0 + n], SG[:, sgs, 0:n], PS[:, b + 1, 0:n], ALU.mult),
                              reads=[("A", "sg", sgs)] + pk(b + 1), writes=[("A", "h", hs, f, t)])

            def down(g):
                s = slots[g]
                hs = g % 2
                wd = RING[:, s, 4096:6144].rearrange("p (f n) -> p f n", f=2)
                for j in range(8):
                    for t in range(5):
                        t0, n = TILES[t]
                        b = alloc()
                        S.add("pe", mm(PS[:, b, 0:n], [(wd[:, f, j * 128:(j + 1) * 128], HH[:, hs, f, t0:t0 + n]) for f in range(2)]),
                              reads=[("ring", s, "d")] + [("A", "h", hs, f, t) for f in range(2)], writes=pk(b))
                        S.add("dve", stt(X[:, j, t0:t0 + n], PS[:, b, 0:n], 0.5, X[:, j, t0:t0 + n], ALU.mult, ALU.add),
                              reads=pk(b) + [xk(j, t)], writes=[xk(j, t)])

            load(0)
            load(1)
            gu(0)
            for g in range(NG):
                if g + 1 < NG:
                    gu(g + 1)
                down(g)
                if g + 2 < NG:
                    load(g + 2)

        def colblock_load(s, src2d, col0, ncols, kc, part="gu0", off=0):
            view = RING[:, s, off:off + kc * ncols].rearrange("p (k n) -> p k n", k=kc)
            wload(s, part, view, src2d[:, col0:col0 + ncols].rearrange("(k p) n -> p k n", p=128))
            return s, view

        def mixer(l, seg):
            kind, sidx, s0, ns, tiles = seg
            smp = kind == "S"
            nseq, ntok = (16, 4) if smp else (1, 1024)
            ltile = [(TILES[t][0] - s0, TILES[t][1], t) for t in tiles]

            def us(k, t, n):
                return US[:, k, TILES[t][0] - s0: TILES[t][0] - s0 + n]

            def uskey(k, t):
                return ("A", "us", k, t)

            S.fence()
            norm(tiles, l, 8, us, uskey)
            ckpt()
            S.fence()
            if smp:
                S.add("sp", dma(pool_s_o[l].rearrange("(s r) f -> s r f", r=15)[:, 0:11, :],
                                spool[l].rearrange("(s r) f -> s r f", r=15)[:, 4:15, :]), chan="st_pool_d2d")
            for cb in range(4):
                wwin = 2 << cb
                s = ring_next()
                _, wv = colblock_load(s, w_in[l], 2576 + cb * 256, 256, 8, "gu0", 0)
                wload(s, "gu1", RING[:, s, 2048:2560].rearrange("p (k n) -> p k n", k=2),
                      pool_w[l, cb].rearrange("(k p) n -> p k n", p=128))
                pwv = RING[:, s, 2048:2560].rearrange("p (k n) -> p k n", k=2)
                for m in range(2):
                    c = cb * 2 + m
                    A = PA[m]
                    A3 = A[:, 0:nseq * (16 + ntok)].rearrange("p (s q) -> p s q", s=nseq)
                    akey = ("A", "pa", m)
                    if smp:
                        S.add("sp", dma(PSTGC[0:120, :, 0:128],
                                        spool[l, :, c * 128:(c + 1) * 128].rearrange("(a p) f -> p a f", p=120)),
                              writes=[("A", "pstgc")], chan="ld_sp")
                        b = alloc()
                        S.add("pe", mms([(PS[:, b, a * 120:(a + 1) * 120], [(PSTGC[0:120, a, 0:128], IDF[0:120, 0:120])]) for a in range(2)]),
                              reads=[("A", "pstgc")] + CK, writes=pk(b))
                        S.add("dve", cp(A3[:, :, 1:16], PS[:, b, 0:240].rearrange("p (s r) -> p s r", r=15)), reads=pk(b), writes=[akey])
                    elif sidx == 0:
                        S.add("dve", ms(A[:, 0:16], 0.0), writes=[akey])
                    else:
                        S.add("dve", cp(A[:, 1:16], HISTP[:, c, :]), reads=[("histp", c)], writes=[akey])
                    for (lc, n, t) in ltile:
                        b = alloc()
                        S.add("pe", mm(PS[:, b, 0:n], [(wv[:, k, m * 128:(m + 1) * 128], us(k, t, n)) for k in range(8)]),
                              reads=[("ring", s, "gu0")] + [uskey(k, t) for k in range(8)], writes=pk(b))
                        if smp:
                            S.add("act", act(A3[:, :, 16:20], PS[:, b, 0:64].rearrange("p (s q) -> p s q", q=4), AF.Copy), reads=pk(b), writes=[akey])
                            S.add("act", act(SNEW[:, m, :], PS[:, b, 0:64], AF.Copy), reads=pk(b), writes=[("A", "snew", m)])
                        else:
                            S.add("act", act(A[:, 16 + lc:16 + lc + n], PS[:, b, 0:n], AF.Copy), reads=pk(b), writes=[akey])
                    LL = 16 + ntok
                    if smp:
                        if m == 0 and cb % 2 == 0:
                            pob = alloc(1, hold=True)
                        S.add("pe", mm(PS[0:64, pob, (c % 4) * 128:(c % 4 + 1) * 128], [(SNEW[:, m, :], IDF)]),
                              reads=[("A", "snew", m)] + CK, writes=pk(pob))
                    elif sidx == 0:
                        S.add("dve", cp(HISTP[:, c, :], A[:, LL - 15:LL]), reads=[akey], writes=[("histp", c)])
                    else:
                        if m == 0 and cb % 2 == 0:
                            pob = alloc(1, hold=True)
                        S.add("pe", mm(PS[0:15, pob, (c % 4) * 128:(c % 4 + 1) * 128], [(A[:, LL - 15:LL], IDF)]),
                              reads=[akey] + CK, writes=pk(pob))
                    if (smp or sidx == 1) and c % 4 == 3:
                        nr = 64 if smp else 15
                        hh = c // 4
                        S.add("act", act(POUT[0:nr, hh * 512:(hh + 1) * 512], PS[0:nr, pob, :], AF.Copy), reads=pk(pob), writes=[("A", "pout", hh)])
                        release(pob)
                        if c == 7:
                            if smp:
                                for sq in range(16):
                                    S.add("sp", dma(pool_s_o[l, sq * 15 + 11: sq * 15 + 15, :], POUT[sq * 4: sq * 4 + 4, :]),
                                          reads=[("A", "pout", 0), ("A", "pout", 1)], chan="st_pool%d" % (sq % 4))
                            else:
                                S.add("sp", dma(pool_p_o[l], POUT[0:15, :]), reads=[("A", "pout", 0), ("A", "pout", 1)], chan="st_pool0")
                    cur, curkey = A3, akey
                    bufs = [(PP1, ("A", "pp1")), (PP2, ("A", "pp2"))]
                    sh = 1
                    lvl = 0
                    while sh < wwin:
                        lo = 2 * sh - 1
                        nb_, nk = bufs[lvl % 2]
                        nb3 = nb_[:, 0:nseq * LL].rearrange("p (s q) -> p s q", s=nseq)
                        S.add("dve", tt(nb3[:, :, lo:LL], cur[:, :, lo:LL], cur[:, :, lo - sh:LL - sh], ALU.add), reads=[curkey], writes=[nk])
                        cur, curkey = nb3, nk
                        sh *= 2
                        lvl += 1
                    dst = MIX[:, c, 0:ns].rearrange("p (s q) -> p s q", s=nseq)
                    mkeys = [("A", "mix", c, t) for (_, _, t) in ltile]
                    S.add("dve", stt(dst, cur[:, :, 16:LL], 1.0 / wwin, A3[:, :, 16:LL], ALU.mult, ALU.subtract),
                          reads=[curkey, akey], writes=mkeys)
                    if (not smp) and sidx == 0:
                        S.add("dve", tt(TMP16[:, :], cur[:, 0, 16:32], RC[:, cb * 16:(cb + 1) * 16], ALU.mult), reads=[curkey] + CK, writes=[("A", "tmp16")])
                        S.add("dve", tt(MIX[:, c, 0:16], TMP16[:, :], A[:, 16:32], ALU.subtract), reads=[("A", "tmp16"), akey], writes=mkeys[0:1])
                for (lc, n, t) in ltile:
                    b = alloc(2)
                    S.add("pe", mms([(PS[:, b + m, 0:n], [(pwv[:, k, m * 128:(m + 1) * 128], MIX[:, cb * 2 + k, lc:lc + n]) for k in range(2)]) for m in range(2)]),
                          reads=[("ring", s, "gu1")] + [("A", "mix", cb * 2 + k, t) for k in range(2)], writes=pk(b, 2))
                    for m in range(2):
                        c = cb * 2 + m
                        S.add("act", act(MIX[:, c, lc:lc + n], PS[:, b + m, 0:n], AF.Copy, scale=P(l, 40 + c)),
                              reads=[("ps", b + m)] + CK, writes=[("A", "mix", c, t)])
            for jb in range(4):
                s = ring_next()
                _, wv = colblock_load(s, w_out[l, 1024:2048], jb * 256, 256, 8, "gu0", 0)
                for m in range(2):
                    j = jb * 2 + m
                    for (lc, n, t) in ltile:
                        b = alloc()
                        S.add("pe", mm(PS[:, b, 0:n], [(wv[:, k, m * 128:(m + 1) * 128], MIX[:, k, lc:lc + n]) for k in range(8)]),
                              reads=[("ring", s, "gu0")] + [("A", "mix", k, t) for k in range(8)], writes=pk(b))
                        g0 = TILES[t][0]
                        S.add("dve", tt(X[:, j, g0:g0 + n], X[:, j, g0:g0 + n], PS[:, b, 0:n], ALU.add), reads=pk(b) + [xk(j, t)], writes=[xk(j, t)])
            ckpt()
            S.fence()
            if smp:
                S.add("sp", dma(CSTG[0:48, :], sconv[l]), writes=[("A", "cstg")], chan="ld_sc")
                for q4 in range(3):
                    b = alloc()
                    S.add("pe", mms([(PS[:, b, jj * 48:(jj + 1) * 48], [(CSTG[0:48, (q4 * 4 + jj) * 128:(q4 * 4 + jj + 1) * 128], IDF[0:48, 0:48])]) for jj in range(4)]),
                          reads=[("A", "cstg")] + CK, writes=pk(b))
                    S.add("dve", cp(PASTT[:, q4 * 4:q4 * 4 + 4, :], PS[:, b, 0:192].rearrange("p (j q) -> p j q", j=4)), reads=pk(b), writes=[("A", "pastt", q4)])
            LR = 3 + ntok
            for cb in range(6):
                s = ring_next()
                _, wv = colblock_load(s, w_in[l], 1024 + cb * 256, 256, 8, "gu0", 0)
                for m in range(2):
                    c = cb * 2 + m
                    R = RAW[m]
                    R3 = R[:, 0:nseq * LR].rearrange("p (s q) -> p s q", s=nseq)
                    rkey = ("A", "raw", m)
                    if smp:
                        S.add("dve", cp(R3[:, :, 0:3], PASTT[:, c, :].rearrange("p (s r) -> p s r", r=3)), reads=[("A", "pastt", c // 4)], writes=[rkey])
                    elif sidx == 0:
                        S.add("dve", ms(R[:, 0:3], 0.0), writes=[rkey])
                    else:
                        S.add("dve", cp(R[:, 0:3], HISTC[:, c, :]), reads=[("histc", c)], writes=[rkey])
                    for (lc, n, t) in ltile:
                        b = alloc()
                        S.add("pe", mm(PS[:, b, 0:n], [(wv[:, k, m * 128:(m + 1) * 128], us(k, t, n)) for k in range(8)]),
                              reads=[("ring", s, "gu0")] + [uskey(k, t) for k in range(8)], writes=pk(b))
                        if smp:
                            S.add("act", act(R3[:, :, 3:7], PS[:, b, 0:64].rearrange("p (s q) -> p s q", q=4), AF.Copy), reads=pk(b), writes=[rkey])
                        else:
                            S.add("act", act(R[:, 3 + lc:3 + lc + n], PS[:, b, 0:n], AF.Copy), reads=pk(b), writes=[rkey])
                    if smp:
                        S.add("dve", cp(SELC[:, m, :].rearrange("p (s r) -> p s r", r=3), R3[:, :, 4:7]), reads=[rkey], writes=[("A", "selc", m)])
                        if c % 4 == 0:
                            cob = alloc(1, hold=True)
                        S.add("pe", mm(PS[0:48, cob, (c % 4) * 128:(c % 4 + 1) * 128], [(SELC[:, m, :], IDF)]), reads=[("A", "selc", m)] + CK, writes=pk(cob))
                    elif sidx == 0:
                        S.add("dve", cp(HISTC[:, c, :], R[:, LR - 3:LR]), reads=[rkey], writes=[("histc", c)])
                    else:
                        if c % 4 == 0:
                            cob = alloc(1, hold=True)
                        S.add("pe", mm(PS[0:3, cob, (c % 4) * 128:(c % 4 + 1) * 128], [(R[:, LR - 3:LR], IDF)]), reads=[rkey] + CK, writes=pk(cob))
                    if (smp or sidx == 1) and c % 4 == 3:
                        nr = 48 if smp else 3
                        q4 = c // 4
                        S.add("act", act(CSTG[0:nr, q4 * 512:(q4 + 1) * 512], PS[0:nr, cob, :], AF.Copy), reads=pk(cob), writes=[("A", "cstg")])
                        release(cob)
                        if c == 11:
                            S.add("sp", dma((conv_s_o if smp else conv_p_o)[l], CSTG[0:nr, :]), reads=[("A", "cstg")], chan="st_conv")
                    A3 = ACC[:, 0:ns].rearrange("p (s q) -> p s q", s=nseq)
                    S.add("dve", ts(A3, R3[:, :, 0:ntok], P(l, 48 + c * 4), P(l, 96 + c), ALU.mult, ALU.add), reads=[rkey] + CK, writes=[("A", "acc")])
                    for jt in range(1, 4):
                        S.add("dve", stt(A3, R3[:, :, jt:jt + ntok], P(l, 48 + c * 4 + jt), A3, ALU.mult, ALU.add), reads=[rkey, ("A", "acc")] + CK, writes=[("A", "acc")])
                    S.add("act", act(MIX[:, c, 0:ns], ACC[:, 0:ns], AF.Silu), reads=[("A", "acc")], writes=[("A", "mix", c, t) for (_, _, t) in ltile])
            s = ring_next()
            _, wv = colblock_load(s, w_in[l], 2560, 16, 8, "gu0", 0)
            for (lc, n, t) in ltile:
                b = alloc()
                S.add("pe", mm(PS[0:16, b, 0:n], [(wv[:, k, :], us(k, t, n)) for k in range(8)]),
                      reads=[("ring", s, "gu0")] + [uskey(k, t) for k in range(8)], writes=pk(b))
                S.add("act", act(DTT[:, lc:lc + n], PS[0:16, b, 0:n], AF.Exp, bias=PRM[0:16, l * PL + 140:l * PL + 141]), reads=pk(b) + CK, writes=[("A", "dtt", t)])
                S.add("act", act(DTT[:, lc:lc + n], DTT[:, lc:lc + n], AF.Ln, bias=1.0), reads=[("A", "dtt", t)], writes=[("A", "dtt", t)])
            for cb in range(4):
                s = ring_next()
                _, wv = colblock_load(s, w_in[l], cb * 256, 256, 8, "gu0", 0)
                for m in range(2):
                    c = cb * 2 + m
                    for (lc, n, t) in ltile:
                        b = alloc()
                        S.add("pe", mm(PS[:, b, 0:n], [(wv[:, k, m * 128:(m + 1) * 128], us(k, t, n)) for k in range(8)]),
                              reads=[("ring", s, "gu0")] + [uskey(k, t) for k in range(8)], writes=pk(b))
                        S.add("act", act(ZB[:, c, lc:lc + n], PS[:, b, 0:n], AF.Silu), reads=pk(b), writes=[("A", "zb", c, t)])
            ckpt()
            S.fence()
            AB = A_B[:, l, :]
            DB = P(l, 124, 16)
            if (not smp) and sidx == 0:
                S.add("dve", ms(HST[:], 0.0), writes=[("hst",)])
                S.add("dve", ms(HB[:], 0.0), writes=[("hb",)])
            if smp:
                S.add("sp", dma(E16[:, :], e16_d[:, :]), writes=[("A", "e16")], chan="ld_e16")
            nq = 64 if smp else 128
            nch = 1 if smp else min(8, int(os.environ.get("MK_NCH", 8)))
            TR = TRISEG[0:64, 0:64] if smp else TRI
            def chunk_gen(ci):
                q0 = ci * 128
                par = 0 if smp else ci % 2
                sfx = str(par)
                BIG, WW, XDT, XDTD, XSD, YT, GN, BTOK, SM, SML = SETS[par]
                DTTOK, DA, CUMCOL, EXPA, CDB, DOUT, SSQ, RSTD, TOTT = (SML[:, i, :] for i in range(9))
                tg = tiles[q0 // 512] if not smp else tiles[0]
                mk = lambda c: ("A", "mix", c, tg)
                bx = alloc(2)
                S.add("pe", mms([(PS[0:nq, bx + jj // 4, (jj % 4) * 128:(jj % 4 + 1) * 128], [(MIX[:, jj, q0:q0 + nq], IDB[:])]) for jj in range(8)]),
                      reads=[mk(c) for c in range(8)] + CK, writes=pk(bx, 2))
                bb = alloc()
                S.add("pe", mms([(PS[0:nq, bb, g * 128:(g + 1) * 128], [(MIX[:, 8 + g, q0:q0 + nq], IDB[:])]) for g in range(2)]
                                + [(PS[0:nq, bb, 256:272], [(DTT[0:16, q0:q0 + nq], IDF[0:16, 0:16])])]),
                      reads=[mk(8), mk(9), ("A", "dtt", tg)] + CK, writes=pk(bb))
                S.add("dve", cp(DTTOK[0:nq, :], PS[0:nq, bb, 256:272]), reads=pk(bb), writes=[("A", "dttok" + sfx)])
                S.add("dve", tt(DA[0:nq, :], PS[0:nq, bb, 256:272], AB[0:nq, :], ALU.mult), reads=pk(bb) + CK, writes=[("A", "da" + sfx)])
                S.add("act", act(BTOK[0:nq, :], PS[0:nq, bb, 0:256], AF.Copy), reads=pk(bb), writes=[("A", "btok" + sfx)])
                sub(1)
                bc = alloc()
                grp = [(PS[0:nq, bc, 0:16], [(TR[0:nq, 0:nq], DA[0:nq, :])])]
                if smp:
                    grp.append((PS[0:nq, bc, 16:32], [(BLK[0:64, 0:64], DA[0:nq, :])]))
                else:
                    grp.append((PS[0:nq, bc, 32:48], [(ONESF[0:nq, 0:nq], DA[0:nq, :])]))
                S.add("pe", mms(grp), reads=[("A", "da" + sfx)] + CK, writes=pk(bc))
                S.add("dve", cp(CUMCOL[0:nq, :], PS[0:nq, bc, 0:16]), reads=pk(bc), writes=[("A", "cumcol" + sfx)])
                S.add("act", act(EXPA[0:nq, :], PS[0:nq, bc, 0:16], AF.Exp), reads=pk(bc), writes=[("A", "expa" + sfx)])
                if not smp:
                    S.add("act", act(CDB[:, :], PS[:, bc, 32:48], AF.Exp), reads=pk(bc), writes=[("A", "cdb" + sfx, 0), ("A", "cdb" + sfx, 1)])
                if smp:
                    S.add("dve", tt(DOUT[0:nq, :], PS[0:nq, bc, 16:32], CUMCOL[0:nq, :], ALU.subtract), reads=pk(bc) + [("A", "cumcol" + sfx)], writes=[("A", "dout" + sfx)])
                    S.add("act", act(DOUT[0:nq, :], DOUT[0:nq, :], AF.Exp), reads=[("A", "dout" + sfx)], writes=[("A", "dout" + sfx)])
                sub(2)
                xs3 = PS[0:nq, bx:bx + 2, :].rearrange("p a (r q) -> p (a r) q", q=64)
                S.add("dve", tt(XDT[0:nq, :].rearrange("p (r q) -> p r q", q=64), xs3, DTTOK[0:nq, :].unsqueeze(2).broadcast_to([nq, 16, 64]), ALU.mult),
                      reads=pk(bx, 2) + [("A", "dttok" + sfx)], writes=[("A", "xdt" + sfx)])
                S.add("dve", tt(XSD[0:nq, :].rearrange("p (r q) -> p r q", q=64), xs3, DB[0:nq, :].unsqueeze(2).broadcast_to([nq, 16, 64]), ALU.mult),
                      reads=pk(bx, 2) + CK, writes=[("A", "xsd" + sfx)])
                sub(3)
                bs = alloc()
                S.add("pe", mms([(PS[0:nq, bs, g * 128:g * 128 + nq], [(MIX[:, 8 + g, q0:q0 + nq], MIX[:, 10 + g, q0:q0 + nq])]) for g in range(2)]),
                      reads=[mk(8), mk(9), mk(10), mk(11)], writes=pk(bs))
                S.add("dve", tt(SM[0:nq, :, 0:nq], PS[0:nq, bs, 0:256].rearrange("p (g q) -> p g q", g=2)[:, :, 0:nq],
                                TR[0:nq, 0:nq].unsqueeze(1).broadcast_to([nq, 2, nq]), ALU.mult), reads=pk(bs) + CK, writes=[("A", "sm" + sfx)])
                sub(4)
                by = alloc(2, hold=True)
                for g in range(2):
                    B3 = BIG[0:nq, 0:8 * nq].rearrange("p (r q) -> p r q", q=nq)
                    W3 = WW[0:nq, 0:8 * nq].rearrange("p (r q) -> p r q", q=nq)
                    S.add("pool", tt(B3, TR[0:nq, 0:nq].unsqueeze(1).broadcast_to([nq, 8, nq]),
                                    DA[0:nq, 8 * g:8 * g + 8].unsqueeze(2).broadcast_to([nq, 8, nq]), ALU.mult),
                          reads=[("A", "da" + sfx)] + CK, writes=[("A", "big" + sfx)])
                    nbk = (8 * nq) // 512
                    br = alloc(nbk)
                    SL = STRICTSEG[0:64, 0:64] if smp else STRICT
                    S.add("pe", mms([(PS[0:nq, br + i, :], [(SL[0:nq, 0:nq], BIG[0:nq, i * 512:(i + 1) * 512])]) for i in range(nbk)]),
                          reads=[("A", "big" + sfx)] + CK, writes=pk(br, nbk))
                    if nbk == 2:
                        cr3 = PS[0:nq, br:br + 2, :].rearrange("p a (r q) -> p (a r) q", q=nq)
                    else:
                        cr3 = PS[0:nq, br, :].rearrange("p (r q) -> p r q", q=nq)
                    S.add("act", act(B3, cr3, AF.Exp), reads=pk(br, nbk), writes=[("A", "big" + sfx)])
                    if not smp:
                        S.add("dve", cp(DOUT[0:nq, 8 * g:8 * g + 8], B3[:, :, nq - 1]), reads=[("A", "big" + sfx)], writes=[("A", "dout" + sfx)])
                    S.add("pool", tt(W3, B3, SM[0:nq, g, 0:nq].unsqueeze(1).broadcast_to([nq, 8, nq]), ALU.mult),
                          reads=[("A", "big" + sfx), ("A", "sm" + sfx)], writes=[("A", "ww" + sfx)])

                    def ydiag(g=g, W3=W3, by=by, nq=nq, XSD=XSD, XDT=XDT):
                        def fn(e):
                            e.matmul(PS[0:nq, by + g, :], IDB[0:nq, 0:nq], XSD[0:nq, g * 512:(g + 1) * 512], start=True, stop=False)
                            for r in range(8):
                                m_ = e.matmul(PS[0:nq, by + g, r * 64:(r + 1) * 64], W3[:, r, :],
                                              XDT[0:nq, (8 * g + r) * 64:(8 * g + r + 1) * 64], start=False, stop=(r == 7))
                            return m_
                        return fn
                    S.add("pe", ydiag(), reads=[("A", "ww" + sfx), ("A", "xdt" + sfx), ("A", "xsd" + sfx)] + CK, writes=pk(by + g))
                sub(5)
                S.add("pool", tt(XDTD[0:nq, :].rearrange("p (r q) -> p r q", q=64), XDT[0:nq, :].rearrange("p (r q) -> p r q", q=64),
                                DOUT[0:nq, :].unsqueeze(2).broadcast_to([nq, 16, 64]), ALU.mult), reads=[("A", "xdt" + sfx), ("A", "dout" + sfx)], writes=[("A", "xdtd" + sfx)])
                bo = alloc(2, hold=True)
                if not smp:
                    S.add("pe", mms([(PS[0:nq, bo + g, :], [(MIX[:, 10 + g, q0:q0 + nq], HB[:, g * 512:(g + 1) * 512])]) for g in range(2)]),
                          reads=[mk(10), mk(11), ("hb",)], writes=pk(bo, 2))
                else:
                    S.add("dve", ts(SML[0:16, 10:14, :].rearrange("p a b -> p (a b)"), DTT[0:16, 0:64], ACOL[:, l:l + 1], 0.0, ALU.mult, ALU.add),
                          reads=[("A", "dtt", tg)] + CK, writes=[("A", "dat")])
                    S.add("dve", lambda e: e.tensor_reduce(out=TOTT[0:16, :], in_=SML[0:16, 10:14, :].rearrange("p a b -> p (a b)").rearrange("p (s q) -> p s q", q=4),
                                                            axis=AX.X, op=ALU.add), reads=[("A", "dat")], writes=[("A", "tott")])
                    bcd = alloc()
                    S.add("pe", mms([(PS[:, bcd, j * 16:(j + 1) * 16], [(E16[0:16, j * 128:(j + 1) * 128], TOTT[0:16, :])]) for j in range(8)]),
                          reads=[("A", "e16"), ("A", "tott")], writes=pk(bcd))
                    S.add("act", act(CDF[:, :, :], PS[:, bcd, 0:128].rearrange("p (j s) -> p j s", j=8), AF.Exp), reads=pk(bcd), writes=[("A", "cdf")])
                    hbufs = [(HST, ("hst",)), (H0X, ("A", "h0x"))]
                    for sq in range(16):
                        H0, hkey = hbufs[sq % 2]
                        H03 = H0[:, :].rearrange("p (j n) -> p j n", j=8)
                        S.add("sp", dma(H03, sssm[l, sq].rearrange("(j p) n -> p j n", p=128)), writes=[hkey], chan="ld_h%d" % (sq % 2))
                        bt = alloc(2)
                        S.add("pe", mms([(PS[:, bt + j // 4, (j % 4) * 128:(j % 4 + 1) * 128], [(H03[:, j, :], IDF)]) for j in range(8)]),
                              reads=[hkey] + CK, writes=pk(bt, 2))
                        S.add("act", act(HB[:, :].rearrange("p (a b) -> p a b", a=2), PS[:, bt:bt + 2, :], AF.Copy), reads=pk(bt, 2), writes=[("hb",)])
                        cs_ = sq % 2
                        S.add("dve", ms(CMS[:, cs_, :, :], 0.0), writes=[("A", "cms", cs_)])
                        S.add("dve", cp(CMS[:, cs_, :, 4 * sq:4 * sq + 4], MIX[:, 10:12, 4 * sq:4 * sq + 4]), reads=[mk(10), mk(11)], writes=[("A", "cms", cs_)])
                        S.add("pe", (lambda sq=sq, cs_=cs_, bo=bo: (lambda e: [e.matmul(PS[0:64, bo + g, :], CMS[:, cs_, g, :], HB[:, g * 512:(g + 1) * 512],
                                                                                    start=(sq == 0), stop=(sq == 15)) for g in range(2)][-1]))(),
                              reads=[("A", "cms", cs_), ("hb",)], writes=pk(bo, 2))
                        S.add("dve", ts(BMS[0:64, cs_, :], BTOK[0:64, :], MASKCOL[0:64, sq:sq + 1], 0.0, ALU.mult, ALU.add),
                              reads=[("A", "btok" + sfx)] + CK, writes=[("A", "bms", cs_)])
                        bn = alloc(2)
                        S.add("pe", mms([(PS[:, bn + j // 4, (j % 4) * 128:(j % 4 + 1) * 128],
                                          [(XDTD[0:64, j * 128:(j + 1) * 128], BMS[0:64, cs_, (j // 4) * 128:(j // 4 + 1) * 128])]) for j in range(8)]),
                              reads=[("A", "xdtd" + sfx), ("A", "bms", cs_)], writes=pk(bn, 2))
                        S.add("dve", tt(H03, H03, CDF[:, :, sq].unsqueeze(2).broadcast_to([128, 8, 128]), ALU.mult), reads=[hkey, ("A", "cdf")], writes=[hkey])
                        S.add("dve", tt(H03, H03, PS[:, bn:bn + 2, :].rearrange("p a (j n) -> p (a j) n", n=128), ALU.add), reads=[hkey] + pk(bn, 2), writes=[hkey])
                        S.add("sp", dma(ssm_s_o[l, sq].rearrange("(j p) n -> p j n", p=128), H03), reads=[hkey], chan="st_h%d" % (sq % 2))
                if not smp:
                    bsp = alloc(2)
                    S.add("pe", mms([(PS[:, bsp + g, :], [(BTOK[:, g * 128:(g + 1) * 128], XDTD[:, g * 512:(g + 1) * 512])]) for g in range(2)]),
                          reads=[("A", "btok" + sfx), ("A", "xdtd" + sfx)], writes=pk(bsp, 2))
                    S.add("pool", tt(HST[:, :].rearrange("p (r q) -> p r q", q=64), HST[:, :].rearrange("p (r q) -> p r q", q=64),
                                    CDB[:, :].unsqueeze(2).broadcast_to([128, 16, 64]), ALU.mult), reads=[("hst",), ("A", "cdb" + sfx, 0), ("A", "cdb" + sfx, 1)], writes=[("hst",)])
                    S.add("dve", tt(HST[:, :].rearrange("p (a b) -> p a b", a=2), HST[:, :].rearrange("p (a b) -> p a b", a=2), PS[:, bsp:bsp + 2, :], ALU.add),
                          reads=[("hst",)] + pk(bsp, 2), writes=[("hst",)])
                    S.add("act", act(HB[:], HST[:], AF.Copy), reads=[("hst",)], writes=[("hb",)])
                sub(6)
                S.add("dve", tt(YT[0:nq, :].rearrange("p (r q) -> p r q", q=64), PS[0:nq, bo:bo + 2, :].rearrange("p a (r q) -> p (a r) q", q=64),
                                EXPA[0:nq, :].unsqueeze(2).broadcast_to([nq, 16, 64]), ALU.mult), reads=pk(bo, 2) + [("A", "expa" + sfx)], writes=[("A", "yt" + sfx)])
                release(bo, 2)
                S.add("dve", tt(YT[0:nq, :].rearrange("p (a b) -> p a b", a=2), YT[0:nq, :].rearrange("p (a b) -> p a b", a=2), PS[0:nq, by:by + 2, :], ALU.add),
                      reads=pk(by, 2) + [("A", "yt" + sfx)], writes=[("A", "yt" + sfx)])
                release(by, 2)
                yield
                bz = alloc(2)
                S.add("pe", mms([(PS[0:nq, bz + jj // 4, (jj % 4) * 128:(jj % 4 + 1) * 128], [(ZB[:, jj, q0:q0 + nq], IDB[:])]) for jj in range(8)]),
                      reads=[("A", "zb", c, tg) for c in range(8)] + CK, writes=pk(bz, 2))
                S.add("dve", tt(YT[0:nq, :].rearrange("p (a b) -> p a b", a=2), YT[0:nq, :].rearrange("p (a b) -> p a b", a=2), PS[0:nq, bz:bz + 2, :], ALU.mult),
                      reads=pk(bz, 2) + [("A", "yt" + sfx)], writes=[("A", "yt" + sfx)])
                S.add("dve", ms(SSQ[0:nq, 0:1], 0.0), writes=[("A", "ssq" + sfx)])
                S.add("act", act(GN[0:nq, :], YT[0:nq, :], AF.Square, accum_out=SSQ[0:nq, 0:1]), reads=[("A", "yt" + sfx), ("A", "ssq" + sfx)], writes=[("A", "gn" + sfx), ("A", "ssq" + sfx)])
                S.add("act", act(RSTD[0:nq, 0:1], SSQ[0:nq, 0:1], AF.Ln, bias=EPS[0:nq, 0:1], scale=1.0 / 1024.0), reads=[("A", "ssq" + sfx)] + CK, writes=[("A", "rstd" + sfx)])
                S.add("act", act(RSTD[0:nq, 0:1], RSTD[0:nq, 0:1], AF.Exp, scale=-0.5), reads=[("A", "rstd" + sfx)], writes=[("A", "rstd" + sfx)])
                S.add("act", act(GN[0:nq, :], YT[0:nq, :], AF.Copy, scale=RSTD[0:nq, 0:1]), reads=[("A", "yt" + sfx), ("A", "rstd" + sfx)], writes=[("A", "gn" + sfx)])
                sub(7)
                bk = alloc(2)
                S.add("pe", mms([(PS[:, bk + jj // 4, (jj % 4) * 128:(jj % 4) * 128 + nq], [(GN[0:nq, jj * 128:(jj + 1) * 128], IDB[0:nq, 0:nq])]) for jj in range(8)]),
                      reads=[("A", "gn" + sfx)] + CK, writes=pk(bk, 2))
                for a in range(2):
                    S.add("dve", tt(MIX[:, 4 * a:4 * a + 4, q0:q0 + nq], PS[:, bk + a, :].rearrange("p (j q) -> p j q", j=4)[:, :, 0:nq],
                                    P(l, 32 + 4 * a, 4).unsqueeze(2).broadcast_to([128, 4, nq]), ALU.mult),
                          reads=pk(bk + a) + CK, writes=[mk(c) for c in range(4 * a, 4 * a + 4)])
            prev = None
            for ci in range(nch):
                gcur = chunk_gen(ci)
                next(gcur)
                if prev is not None:
                    for _ in prev:
                        pass
                prev = gcur
            for _ in prev:
                pass
            if (not smp) and sidx == 1:
                YT = SETS[0][5]
                sfx = "0"
                bf = alloc(2)
                S.add("pe", mms([(PS[:, bf + j // 4, (j % 4) * 128:(j % 4 + 1) * 128], [(HST[:, j * 128:(j + 1) * 128], IDF)]) for j in range(8)]),
                      reads=[("hst",)] + CK, writes=pk(bf, 2))
                S.add("act", act(YT[:, :].rearrange("p (a b) -> p a b", a=2), PS[:, bf:bf + 2, :], AF.Copy), reads=pk(bf, 2), writes=[("A", "yt" + sfx)])
                S.add("sp", dma(ssm_p_o[l].rearrange("(j p) n -> p j n", p=128), YT[:, :].rearrange("p (j n) -> p j n", j=8)), reads=[("A", "yt" + sfx)], chan="st_ssmp")
            ckpt()
            for jb in range(4):
                s = ring_next()
                _, wv = colblock_load(s, w_out[l, 0:1024], jb * 256, 256, 8, "gu0", 0)
                for m in range(2):
                    j = jb * 2 + m
                    for (lc, n, t) in ltile:
                        b = alloc()
                        S.add("pe", mm(PS[:, b, 0:n], [(wv[:, k, m * 128:(m + 1) * 128], MIX[:, k, lc:lc + n]) for k in range(8)]),
                              reads=[("ring", s, "gu0")] + [("A", "mix", k, t) for k in range(8)], writes=pk(b))
                        g0 = TILES[t][0]
                        S.add("dve", tt(X[:, j, g0:g0 + n], X[:, j, g0:g0 + n], PS[:, b, 0:n], ALU.add), reads=pk(b) + [xk(j, t)], writes=[xk(j, t)])

        def ple(l):
            S.fence()
            for blk in range(17):
                r0 = blk * 128
                n = 128 if blk < 16 else 64
                t = blk // 4
                sl = blk % 2
                S.add("sp", dma(PSTG[0:n, sl, :], pin[l, r0:r0 + n, :]), writes=[("A", "pstg", sl)], chan="ld_p%d" % sl)
                b = alloc()
                S.add("pe", mms([(PS[:, b, k * 128:k * 128 + n], [(PSTG[0:n, sl, k * 128:(k + 1) * 128], IDF[0:n, 0:n])]) for k in range(2)]),
                      reads=[("A", "pstg", sl)] + CK, writes=pk(b))
                S.add("act", act(PT[:, :, r0:r0 + n], PS[:, b, 0:256].rearrange("p (k q) -> p k q", k=2)[:, :, 0:n], AF.Copy), reads=pk(b), writes=[("A", "pt", blk)])
            norm(range(5), l, 24, u_full, ukey)
            for jb in range(4):
                s = ring_next()
                _, wv = colblock_load(s, ple_gate[l], jb * 256, 256, 8, "gu0", 0)
                wload(s, "gu1", RING[:, s, 2048:2560].rearrange("p (k n) -> p k n", k=2),
                      ple_proj[l][:, jb * 256:(jb + 1) * 256].rearrange("(k p) n -> p k n", p=128))
                pv = RING[:, s, 2048:2560].rearrange("p (k n) -> p k n", k=2)
                for m in range(2):
                    j = jb * 2 + m
                    for t in range(5):
                        t0, n = TILES[t]
                        b = alloc(2)
                        S.add("pe", mms([(PS[:, b, 0:n], [(wv[:, k, m * 128:(m + 1) * 128], U[:, k, t0:t0 + n]) for k in range(8)]),
                                         (PS[:, b + 1, 0:n], [(pv[:, k, m * 128:(m + 1) * 128], PT[:, k, t0:t0 + n]) for k in range(2)])]),
                              reads=[("ring", s, "gu0"), ("ring", s, "gu1")] + [ukey(k, t) for k in range(8)] + [("A", "pt", bb_) for bb_ in range(17)], writes=pk(b, 2))
                        S.add("act", act(SGT[:, 0:n], PS[:, b, 0:n], AF.Sigmoid), reads=pk(b), writes=[("A", "sgt")])
                        S.add("dve", tt(TMP2[:, 0:n], SGT[:, 0:n], PS[:, b + 1, 0:n], ALU.mult), reads=[("A", "sgt")] + pk(b + 1), writes=[("A", "tmp2")])
                        S.add("dve", tt(X[:, j, t0:t0 + n], X[:, j, t0:t0 + n], TMP2[:, 0:n], ALU.add), reads=[("A", "tmp2"), xk(j, t)], writes=[xk(j, t)])

        SEGS = [("P", 0, 0, 1024, [0, 1]), ("P", 1, 1024, 1024, [2, 3]), ("S", 2, 2048, 64, [4])]
        try:
            ckpt()
            for l in range(nlayers):
                ffn(l, 0)
                ckpt()
                for seg in SEGS:
                    mixer(l, seg)
                    ckpt()
                ffn(l, 1)
                ckpt()
                ple(l)
                ckpt()
        except _Stop:
            pass
        S.fence()
        norm(range(5), None, 0, lambda k, t, n: X[:, k, TILES[t][0]:TILES[t][0] + n], lambda k, t: xk(k, t))
        S.fence()
        for blk in range(17):
            r0 = blk * 128
            n = 128 if blk < 16 else 64
            t = blk // 4
            sl = blk % 2
            b = alloc(2)
            S.add("pe", mms([(PS[0:n, b + j // 4, (j % 4) * 128:(j % 4 + 1) * 128], [(X[:, j, r0:r0 + n], IDF)]) for j in range(8)]),
                  reads=[xk(j, t) for j in range(8)] + CK, writes=pk(b, 2))
            if blk % 2 == 0:
                S.add("act", act(XSTG[0:n, sl, :].rearrange("p (a b) -> p a b", a=2), PS[0:n, b:b + 2, :], AF.Copy), reads=pk(b, 2), writes=[("A", "xstg", sl)])
            else:
                S.add("dve", cp(XSTG[0:n, sl, :].rearrange("p (a b) -> p a b", a=2), PS[0:n, b:b + 2, :]), reads=pk(b, 2), writes=[("A", "xstg", sl)])
            S.add("sp", dma(y_o[r0:r0 + n, :], XSTG[0:n, sl, :]), reads=[("A", "xstg", sl)], chan="st_y%d" % sl)
        S.emit(nc, es)
    return nc


def _host_consts():
    cst = np.zeros((128, NCST), np.float32)
    cst[:, 0:128] = np.eye(128, dtype=np.float32)
    s = np.arange(128)[:, None]
    q = np.arange(128)[None, :]
    cst[:, 128:256] = (s <= q).astype(np.float32)
    s6 = np.arange(64)[:, None]
    q6 = np.arange(64)[None, :]
    cst[0:64, 256:320] = ((s6 <= q6) & (s6 // 4 == q6 // 4)).astype(np.float32)
    cst[0:64, 320:384] = (s6 // 4 == q6 // 4).astype(np.float32)
    cst[0:64, 384:400] = (s6 // 4 == np.arange(16)[None, :]).astype(np.float32)
    for gi, w in enumerate((2, 4, 8, 16)):
        cst[:, 400 + gi * 16: 400 + (gi + 1) * 16] = (1.0 / np.minimum(np.arange(16) + 1, w)).astype(np.float32)[None, :]
    cst[:, 464:592] = (s > q).astype(np.float32)
    cst[0:64, 592:656] = ((s6 > q6) & (s6 // 4 == q6 // 4)).astype(np.float32)
    e16 = np.zeros((16, 1024), np.float32)
    for r in range(16):
        e16[r, r * 64:(r + 1) * 64] = 1.0
    return cst, e16


def _host_params(inp):
    prm = np.zeros((128, NPRM), np.float32)

    def cols(v):
        return np.asarray(v, np.float32).reshape(8, 128).T

    for l in range(L):
        b = l * PL
        prm[:, b + 0:b + 8] = cols(inp["norm_ffn1"][l])
        prm[:, b + 8:b + 16] = cols(inp["norm_mix"][l])
        prm[:, b + 16:b + 24] = cols(inp["norm_ffn2"][l])
        prm[:, b + 24:b + 32] = cols(inp["norm_ple"][l])
        prm[:, b + 32:b + 40] = cols(inp["ssd_norm_w"][l])
        prm[:, b + 40:b + 48] = cols(inp["pool_scale"][l])
        cw = np.asarray(inp["conv_w"][l], np.float32)
        prm[:, b + 48:b + 96] = cw.reshape(4, 12, 128).transpose(2, 1, 0).reshape(128, 48)
        prm[:, b + 96:b + 108] = np.asarray(inp["conv_b"][l], np.float32).reshape(12, 128).T
        prm[:, b + 108:b + 124] = np.asarray(inp["a_log"][l], np.float32)[None, :]
        prm[:, b + 124:b + 140] = np.asarray(inp["d_skip"][l], np.float32)[None, :]
        prm[0:16, b + 140] = np.asarray(inp["dt_bias"][l], np.float32)
        prm[0:16, b + 141] = np.asarray(inp["a_log"][l], np.float32)
    prm[:, L * PL:L * PL + 8] = cols(inp["final_norm"])
    return prm


_NC_CACHE = {}


def kernel(**inp):
    inp = {k: np.asarray(v) for k, v in inp.items()}
    nlayers = int(os.environ.get("MK_LAYERS", L))
    if nlayers not in _NC_CACHE:
        _NC_CACHE[nlayers] = build_nc(nlayers)
    nc = _NC_CACHE[nlayers]
    cst, e16 = _host_consts()
    prm = _host_params(inp)
    shared = {k: np.ascontiguousarray(inp[k], dtype=np.float32) for k in
              ("w_in", "w_out", "pool_w", "ffn1_gate", "ffn2_gate", "ffn1_up", "ffn2_up", "ffn1_down", "ffn2_down", "ple_gate", "ple_proj")}
    in_maps = []
    for c in range(NCORE):
        sl = slice(16 * c, 16 * c + 16)
        m = dict(shared)
        m["xin"] = np.ascontiguousarray(np.concatenate([inp["x_prompt"][c], inp["x_sample"][sl].reshape(NSM, D)], axis=0), dtype=np.float32)
        m["pin"] = np.ascontiguousarray(np.concatenate([inp["p_prompt"][:, c], inp["p_sample"][:, sl].reshape(L, NSM, 256)], axis=1), dtype=np.float32)
        m["sssm"] = np.ascontiguousarray(inp["state_ssm"][:, sl].reshape(L, 16, 1024, 128), dtype=np.float32)
        m["sconv"] = np.ascontiguousarray(inp["state_conv"][:, sl].reshape(L, 48, CONV), dtype=np.float32)
        m["spool"] = np.ascontiguousarray(inp["state_pool"][:, sl].reshape(L, 240, 1024), dtype=np.float32)
        m["prm"] = prm
        m["cst"] = cst
        m["e16"] = e16
        in_maps.append(m)
    ncr = int(os.environ.get("MK_CORES", NCORE))
    res = run_bass_kernel_spmd(nc, in_maps[:ncr], core_ids=list(range(ncr)))
    R = list(res.results) + [res.results[0]] * (NCORE - ncr)
    y_prompt = np.stack([R[c]["y"][0:NPR] for c in range(NCORE)], axis=0)
    y_sample = np.concatenate([R[c]["y"][NPR:].reshape(16, 4, D) for c in range(NCORE)], axis=0)
    ssm_prompt = np.stack([R[c]["ssm_p"].reshape(L, 16, 64, 128) for c in range(NCORE)], axis=1)
    conv_prompt = np.stack([R[c]["conv_p"] for c in range(NCORE)], axis=1)
    pool_prompt = np.stack([R[c]["pool_p"] for c in range(NCORE)], axis=1)
    ssm_sample = np.concatenate([R[c]["ssm_s"].reshape(L, 16, 16, 64, 128) for c in range(NCORE)], axis=1)
    conv_sample = np.concatenate([R[c]["conv_s"].reshape(L, 16, 3, CONV) for c in range(NCORE)], axis=1)
    pool_sample = np.concatenate([R[c]["pool_s"].reshape(L, 16, 15, 1024) for c in range(NCORE)], axis=1)
    f = lambda a: np.ascontiguousarray(a, dtype=np.float32)
    return (f(y_prompt), f(y_sample), f(ssm_prompt), f(conv_prompt), f(pool_prompt), f(ssm_sample), f(conv_sample), f(pool_sample))
```

```python
import os
from contextlib import ExitStack
from math import prod

import numpy as np
import concourse.bass as bass
import concourse.mybir as mybir
from concourse.bass_utils import run_bass_kernel_spmd

F32 = mybir.dt.float32
BF16 = mybir.dt.bfloat16
AF = mybir.ActivationFunctionType
ALU = mybir.AluOpType
AX = mybir.AxisListType

NCORE = 8
L = 4
D = 1024
NPR = 2048
NSM = 64
T = NPR + NSM
DFF = 2816
DIN = 3600
CONV = 1536
EPSV = 1e-6
TILES = [(0, 512), (512, 512), (1024, 512), (1536, 512), (2048, 64)]
PL = 142
NPRM = L * PL + 8
NCST = 656
ENGS = ("pe", "act", "dve", "pool", "sp")


class Sched:
    def __init__(self):
        self.ops = []
        self.last_writer = {}
        self.readers = {}
        self.chan_last = {}
        self.fence_deps = set()
        self.fenced = set()
        self.arena_touch = {}

    def add(self, eng, fn, reads=(), writes=(), chan=None):
        idx = len(self.ops)
        if idx >= int(os.environ.get("MK_MAXOPS", 10 ** 9)) and not getattr(self, "nolimit", False):
            self.nolimit = True
            print("MAXOPS stop at", idx, eng, reads, writes)
            raise _Stop()
        deps = set()
        touches = False
        if eng != "pe":
            writes = list(writes) + [k for k in reads if k[0] == "ps" and k not in writes]
        for k in list(reads) + list(writes):
            if k[0] == "A":
                touches = True
                if k not in self.fenced:
                    deps.update(self.fence_deps)
                    self.fenced.add(k)
        for k in reads:
            if k in self.last_writer:
                deps.add(self.last_writer[k])
        for k in writes:
            if k in self.last_writer:
                deps.add(self.last_writer[k])
            deps.update(self.readers.get(k, ()))
        if chan is not None and chan in self.chan_last:
            deps.add(self.chan_last[chan])
        deps.discard(idx)
        for k in reads:
            self.readers.setdefault(k, []).append(idx)
        for k in writes:
            self.last_writer[k] = idx
            self.readers[k] = []
        if chan is not None:
            self.chan_last[chan] = idx
        if touches:
            self.arena_touch[chan if chan is not None else eng] = idx
        self.ops.append(dict(eng=eng, fn=fn, deps=deps, chan=chan, signal=False))
        return idx

    def fence(self):
        allp = set(self.fence_deps) | set(self.arena_touch.values())
        best = {}
        for i in allp:
            o = self.ops[i]
            k = o["chan"] if o["chan"] is not None else o["eng"]
            if k not in best or best[k] < i:
                best[k] = i
        self.fence_deps = set(best.values())
        self.fenced = set()
        self.arena_touch = {}
        for k in [k for k in self.last_writer if k[0] == "A"]:
            del self.last_writer[k]
        for k in [k for k in self.readers if k[0] == "A"]:
            del self.readers[k]

    def emit(self, nc, es):
        ops = self.ops
        for o in ops:
            nd = set()
            for d in o["deps"]:
                p = ops[d]
                if o["eng"] == "pe" and p["eng"] == "pe" and p["chan"] is None and o["chan"] is None:
                    continue
                nd.add(d)
                p["signal"] = True
            o["deps"] = nd
        chans = sorted({o["chan"] for o in ops if o["chan"] is not None})
        sems = {}
        for e in ENGS:
            sems[e] = es.enter_context(nc.semaphore("s_" + e))
        for c in chans:
            sems[c] = es.enter_context(nc.semaphore("c_" + c))
        cnt = {k: 0 for k in sems}
        for o in ops:
            if o["chan"] is not None:
                cnt[o["chan"]] += 16
                o["tick"] = (o["chan"], cnt[o["chan"]])
            elif o["signal"]:
                cnt[o["eng"]] += 1
                o["tick"] = (o["eng"], cnt[o["eng"]])
        block = es.enter_context(nc.Block())

        def run(engname, e):
            waited = {}
            for o in ops:
                if o["eng"] != engname:
                    continue
                need = {}
                for d in o["deps"]:
                    s, v = ops[d]["tick"]
                    if need.get(s, 0) < v:
                        need[s] = v
                for s, v in need.items():
                    if waited.get(s, 0) < v:
                        e.wait_ge(sems[s], v)
                        waited[s] = v
                ins = o["fn"](e)
                if o["chan"] is not None:
                    ins.then_inc(sems[o["chan"]], 16)
                elif o["signal"]:
                    ins.then_inc(sems[o["eng"]], 1)
            if engname == "sp":
                for s, v in cnt.items():
                    if v > 0 and waited.get(s, 0) < v:
                        e.wait_ge(sems[s], v)

        @block.tensor
        def _(e):
            run("pe", e)

        @block.scalar
        def _(e):
            run("act", e)

        @block.vector
        def _(e):
            run("dve", e)

        @block.gpsimd
        def _(e):
            run("pool", e)

        @block.sync
        def _(e):
            run("sp", e)


class _Stop(Exception):
    pass


def build_nc(nlayers=L):
    stop_at = int(os.environ.get("MK_STOP", 999))
    stage = [0]

    def ckpt():
        stage[0] += 1
        if stage[0] >= stop_at:
            raise _Stop()

    sub_at = int(os.environ.get("MK_SUB", 999))

    def sub(k):
        if k >= sub_at:
            raise _Stop()

    nc = bass.Bass("TRN2", target_bir_lowering=False)
    S = Sched()
    es = ExitStack()

    def din(name, shape):
        return nc.dram_tensor(name, list(shape), F32, kind="ExternalInput").ap()

    def dout(name, shape):
        return nc.dram_tensor(name, list(shape), F32, kind="ExternalOutput").ap()

    xin = din("xin", [T, D])
    pin = din("pin", [L, T, 256])
    sssm = din("sssm", [L, 16, 1024, 128])
    sconv = din("sconv", [L, 48, CONV])
    spool = din("spool", [L, 240, 1024])
    w_in = din("w_in", [L, D, DIN])
    w_out = din("w_out", [L, 2048, D])
    pool_w = din("pool_w", [L, 4, 256, 256])
    fg = [din("ffn1_gate", [L, D, DFF]), din("ffn2_gate", [L, D, DFF])]
    fu = [din("ffn1_up", [L, D, DFF]), din("ffn2_up", [L, D, DFF])]
    fd = [din("ffn1_down", [L, DFF, D]), din("ffn2_down", [L, DFF, D])]
    ple_gate = din("ple_gate", [L, D, D])
    ple_proj = din("ple_proj", [L, 256, D])
    prm_d = din("prm", [128, NPRM])
    cst_d = din("cst", [128, NCST])
    e16_d = din("e16", [16, 1024])

    y_o = dout("y", [T, D])
    ssm_p_o = dout("ssm_p", [L, 1024, 128])
    conv_p_o = dout("conv_p", [L, 3, CONV])
    pool_p_o = dout("pool_p", [L, 15, 1024])
    ssm_s_o = dout("ssm_s", [L, 16, 1024, 128])
    conv_s_o = dout("conv_s", [L, 48, CONV])
    pool_s_o = dout("pool_s", [L, 240, 1024])

    with es:
        def sb(name, shape, dt=F32):
            return es.enter_context(nc.sbuf_tensor(name, list(shape), dt))

        X = sb("X", [128, 8, T])
        RING = sb("RING", [128, 2, 6144], BF16)
        ARENA_B = 92928
        ARENA = sb("ARENA", [128, ARENA_B // 4])
        HST = sb("HST", [128, 1024])
        HB = sb("HB", [128, 1024], BF16)
        HISTC = sb("HISTC", [128, 12, 3])
        HISTP = sb("HISTP", [128, 8, 15])
        PRM = sb("PRM", [128, NPRM])
        CST = sb("CST", [128, NCST])
        IDB = sb("IDB", [128, 128], BF16)
        ONESB = sb("ONESB", [128, 128], BF16)
        ONESF = sb("ONESF", [128, 128])
        EPS = sb("EPS", [128, 1])
        A_B = sb("A_B", [128, L, 16])
        ACOL = sb("ACOL", [16, L])
        PS = es.enter_context(nc.psum_tensor("PS", [128, 8, 512], F32))

        IDF = CST[:, 0:128]
        TRI = CST[:, 128:256]
        TRISEG = CST[:, 256:320]
        BLK = CST[:, 320:384]
        MASKCOL = CST[:, 384:400]
        RC = CST[:, 400:464]
        STRICT = CST[:, 464:592]
        STRICTSEG = CST[:, 592:656]

        def av(off_b, shape, dt=F32):
            nel = prod(shape[1:])
            nb = nel * (4 if dt == F32 else 2)
            assert off_b % 4 == 0 and nb % 4 == 0 and off_b + nb <= ARENA_B, (off_b, shape)
            v = ARENA[0:shape[0], off_b // 4: (off_b + nb) // 4]
            if dt == BF16:
                v = v.bitcast(BF16)
            if len(shape) == 3:
                v = v.rearrange("p (a b) -> p a b", a=shape[1])
            elif len(shape) == 4:
                v = v.rearrange("p (a b c) -> p a b c", a=shape[1], b=shape[2])
            return v

        U = av(0, [128, 8, T], BF16)
        HH = av(33792, [128, 2, 2, T], BF16)
        SG = av(50688, [128, 2, 512], BF16)
        SQ = av(62976, [128, 8, 512], BF16)
        RS = av(71168, [128, 512])
        PT = av(73216, [128, 2, T], BF16)
        SGT = av(81664, [128, 512])
        TMP2 = av(83712, [128, 512])
        PSTG = av(85760, [128, 2, 256])
        XSTG = av(0, [128, 2, 1024])
        US = av(0, [128, 8, 1024], BF16)
        MIX = av(16384, [128, 12, 1024], BF16)
        ZB = av(40960, [128, 8, 1024], BF16)
        DTT = av(57344, [16, 1024])
        SC = 61440
        PA = [av(SC, [128, 1040]), av(SC + 4160, [128, 1040])]
        PP1 = av(SC + 8320, [128, 1040])
        PP2 = av(SC + 12480, [128, 1040])
        POUT = av(SC + 16640, [128, 1024])
        PSTGC = av(SC + 20736, [128, 2, 256])
        PASTP = av(SC + 22784, [128, 2, 240])
        SNEW = av(SC + 24704, [128, 2, 64])
        TMP16 = av(SC + 25216, [128, 16])
        RAW = [av(SC, [128, 1028]), av(SC + 4112, [128, 1028])]
        ACC = av(SC + 8224, [128, 1024])
        CSTG = av(SC + 12320, [128, 1536])
        PASTT = av(SC + 18464, [128, 12, 48])
        SELC = av(SC + 20768, [128, 2, 48])
        BIG = av(SC, [128, 1024])
        WW = av(SC + 4096, [128, 1024], BF16)
        XDT = av(SC + 6144, [128, 1024], BF16)
        XDTD = av(SC + 8192, [128, 1024], BF16)
        XSD = av(SC + 10240, [128, 1024], BF16)
        YT = av(SC + 12288, [128, 1024])
        GN = av(SC + 16384, [128, 1024], BF16)
        BTOK = av(SC + 18432, [128, 256], BF16)
        SM = av(SC + 18944, [128, 2, 128])
        SML = av(SC + 19968, [128, 16, 16])
        H0X = av(SC + 20992, [128, 1024])
        CMS = av(SC + 25088, [128, 2, 2, 64], BF16)
        BMS = av(SC + 25600, [128, 2, 256], BF16)
        CDF = av(SC + 26624, [128, 8, 16])
        E16 = av(SC + 27136, [16, 1024])
        SETS = [
            (BIG, WW, XDT, XDTD, XSD, YT, GN, BTOK, SM, SML),
            (av(0, [128, 1024]), av(4096, [128, 1024], BF16), av(6144, [128, 1024], BF16), av(8192, [128, 1024], BF16),
             av(10240, [128, 1024], BF16), av(12288, [128, 1024]), av(SC + 20992, [128, 1024], BF16), av(SC + 23040, [128, 256], BF16),
             av(SC + 23552, [128, 2, 128]), av(SC + 24576, [128, 16, 16])),
        ]

        bank_cur = [0]

        held = set()

        def alloc(n=1, hold=False):
            c = bank_cur[0]
            for _ in range(32):
                if n > 1 and c % n:
                    c += n - c % n
                if c + n > 8:
                    c = 0
                if all((c + i) not in held for i in range(n)):
                    break
                c += 1
            else:
                raise RuntimeError("no free psum banks")
            bank_cur[0] = (c + n) % 8
            if hold:
                for i in range(n):
                    held.add(c + i)
            return c

        def release(b, n=1):
            for i in range(n):
                held.discard(b + i)

        def pk(b, n=1):
            return [("ps", b + i) for i in range(n)]

        ring_cur = [0]
        dma_flip = [0]

        def P(l, off, n=1):
            return PRM[:, l * PL + off: l * PL + off + n]

        def mm(out, pairs):
            def fn(e):
                n = len(pairs)
                for i, (a, b) in enumerate(pairs):
                    m = e.matmul(out, a, b, start=(i == 0), stop=(i == n - 1))
                return m
            return fn

        def mms(groups):
            def fn(e):
                for out, pairs in groups:
                    n = len(pairs)
                    for i, (a, b) in enumerate(pairs):
                        m = e.matmul(out, a, b, start=(i == 0), stop=(i == n - 1))
                return m
            return fn

        def act(out, in_, func, **kw):
            return lambda e: e.activation(out=out, in_=in_, func=func, **kw)

        def tt(out, a, b, op):
            return lambda e: e.tensor_tensor(out=out, in0=a, in1=b, op=op)

        def ts(out, a, s1, s2, op0, op1):
            return lambda e: e.tensor_scalar(out=out, in0=a, scalar1=s1, scalar2=s2, op0=op0, op1=op1)

        def stt(out, a, s, b, op0, op1):
            return lambda e: e.scalar_tensor_tensor(out=out, in0=a, scalar=s, in1=b, op0=op0, op1=op1)

        def cp(out, in_):
            return lambda e: e.tensor_copy(out, in_)

        def ms(ap, v):
            return lambda e: e.memset(ap, v)

        def dma(out, in_):
            return lambda e: e.dma_start(out=out, in_=in_)

        def xk(j, t):
            return ("x", j, t)

        S.add("sp", dma(PRM[:], prm_d[:, :]), writes=[("prm",)], chan="ld_prm")
        S.add("sp", dma(CST[:], cst_d[:, :]), writes=[("cst",)], chan="ld_cst")
        S.add("dve", cp(IDB[:], IDF), reads=[("cst",)], writes=[("idb",)])
        S.add("dve", ms(ONESB[:], 1.0 / 1024.0), writes=[("onesb",)])
        S.add("dve", ms(ONESF[:], 1.0), writes=[("onesf",)])
        S.add("dve", ms(EPS[:], EPSV), writes=[("eps",)])
        for l in range(L):
            S.add("act", act(A_B[:, l, :], P(l, 108, 16), AF.Exp), reads=[("prm",)], writes=[("ab0", l)])
            S.add("dve", ts(A_B[:, l, :], A_B[:, l, :], -1.0, 0.0, ALU.mult, ALU.add), reads=[("ab0", l)], writes=[("ab",)])
            S.add("act", act(ACOL[:, l:l + 1], PRM[0:16, l * PL + 141: l * PL + 142], AF.Exp), reads=[("prm",)], writes=[("ac0", l)])
            S.add("dve", ts(ACOL[:, l:l + 1], ACOL[:, l:l + 1], -1.0, 0.0, ALU.mult, ALU.add), reads=[("ac0", l)], writes=[("acol",)])
        CK = [("cst",), ("idb",), ("onesb",), ("onesf",), ("eps",), ("ab",), ("acol",), ("prm",)]

        for blk in range(17):
            r0 = blk * 128
            n = 128 if blk < 16 else 64
            t = blk // 4
            sl = blk % 2
            S.add("sp", dma(XSTG[0:n, sl, :], xin[r0:r0 + n, :]), writes=[("A", "xstg", sl)], chan="ld_x%d" % sl)
            for hb in range(2):
                b = alloc()
                S.add("pe", mms([(PS[:, b, jj * 128: jj * 128 + n],
                                  [(XSTG[0:n, sl, (hb * 4 + jj) * 128:(hb * 4 + jj + 1) * 128], IDF[0:n, 0:n])]) for jj in range(4)]),
                      reads=[("A", "xstg", sl)] + CK, writes=pk(b))
                src = PS[:, b, :].rearrange("p (j q) -> p j q", j=4)[:, :, 0:n]
                dst = X[:, hb * 4:hb * 4 + 4, r0:r0 + n]
                if (blk + hb) % 2 == 0:
                    S.add("act", act(dst, src, AF.Copy), reads=pk(b), writes=[xk(hb * 4 + jj, t) for jj in range(4)])
                else:
                    S.add("dve", cp(dst, src), reads=pk(b), writes=[xk(hb * 4 + jj, t) for jj in range(4)])
        S.fence()

        def norm(tiles, wl, woff, dst_fn, dkey_fn):
            for t in tiles:
                t0, n = TILES[t]
                S.add("act", act(SQ[:, :, 0:n], X[:, :, t0:t0 + n], AF.Square),
                      reads=[xk(j, t) for j in range(8)], writes=[("A", "sq")])
                b = alloc()
                S.add("pe", mm(PS[:, b, 0:n], [(ONESB[:], SQ[:, k, 0:n]) for k in range(8)]),
                      reads=[("A", "sq")] + CK, writes=pk(b))
                S.add("act", act(RS[:, 0:n], PS[:, b, 0:n], AF.Ln, bias=EPS[:, 0:1]), reads=pk(b) + CK, writes=[("A", "rs")])
                S.add("act", act(RS[:, 0:n], RS[:, 0:n], AF.Exp, scale=-0.5), reads=[("A", "rs")], writes=[("A", "rs")])
                for k in range(8):
                    wcol = PRM[:, wl * PL + woff + k: wl * PL + woff + k + 1] if wl is not None else PRM[:, L * PL + k: L * PL + k + 1]
                    S.add("dve", stt(dst_fn(k, t, n), X[:, k, t0:t0 + n], wcol, RS[:, 0:n], ALU.mult, ALU.mult),
                          reads=[xk(k, t), ("A", "rs")] + CK, writes=[dkey_fn(k, t)])

        def u_full(k, t, n):
            return U[:, k, TILES[t][0]:TILES[t][0] + n]

        def ukey(k, t):
            return ("A", "u", k, t)

        def ring_next():
            s = ring_cur[0]
            ring_cur[0] = 1 - s
            return s

        def wload(slot, part, view, src):
            S.add("pool", dma(view, src), writes=[("ring", slot, part)], chan="r%d%s" % (slot, part))

        def ffn(l, which):
            S.fence()
            norm(range(5), l, 0 if which == 0 else 16, u_full, ukey)
            NG = DFF // 256
            gd, ud, dd = fg[which], fu[which], fd[which]
            slots = {}

            def load(g):
                s = ring_next()
                slots[g] = s
                wload(s, "gu0", RING[:, s, 0:2048].rearrange("p (k n) -> p k n", k=8),
                      gd[l, :, g * 256:(g + 1) * 256].rearrange("(k p) n -> p k n", p=128))
                wload(s, "gu1", RING[:, s, 2048:4096].rearrange("p (k n) -> p k n", k=8),
                      ud[l, :, g * 256:(g + 1) * 256].rearrange("(k p) n -> p k n", p=128))
                wload(s, "d", RING[:, s, 4096:6144].rearrange("p (f n) -> p f n", f=2),
                      dd[l, g * 256:(g + 1) * 256, :].rearrange("(f p) n -> p f n", p=128))

            def gu(g):
                s = slots[g]
                hs = g % 2
                wg = RING[:, s, 0:2048].rearrange("p (k n) -> p k n", k=8)
                wu = RING[:, s, 2048:4096].rearrange("p (k n) -> p k n", k=8)
                for f in range(2):
                    for t in range(5):
                        t0, n = TILES[t]
                        b = alloc(2)
                        S.add("pe", mms([(PS[:, b, 0:n], [(wg[:, k, f * 128:(f + 1) * 128], U[:, k, t0:t0 + n]) for k in range(8)]),
                                         (PS[:, b + 1, 0:n], [(wu[:, k, f * 128:(f + 1) * 128], U[:, k, t0:t0 + n]) for k in range(8)])]),
                              reads=[("ring", s, "gu0"), ("ring", s, "gu1")] + [ukey(k, t) for k in range(8)], writes=pk(b, 2))
                        sgs = (f * 5 + t) % 2
                        S.add("act", act(SG[:, sgs, 0:n], PS[:, b, 0:n], AF.Silu), reads=pk(b), writes=[("A", "sg", sgs)])
                        S.add("dve", tt(HH[:, hs, f, t0:t0 + n], SG[:, sgs, 0:n], PS[:, b + 1, 0:n], ALU.mult),
                              reads=[("A", "sg", sgs)] + pk(b + 1), writes=[("A", "h", hs, f, t)])

            def down(g):
                s = slots[g]
                hs = g % 2
                wd = RING[:, s, 4096:6144].rearrange("p (f n) -> p f n", f=2)
                for j in range(8):
                    for t in range(5):
                        t0, n = TILES[t]
                        b = alloc()
                        S.add("pe", mm(PS[:, b, 0:n], [(wd[:, f, j * 128:(j + 1) * 128], HH[:, hs, f, t0:t0 + n]) for f in range(2)]),
                              reads=[("ring", s, "d")] + [("A", "h", hs, f, t) for f in range(2)], writes=pk(b))
                        S.add("dve", stt(X[:, j, t0:t0 + n], PS[:, b, 0:n], 0.5, X[:, j, t0:t0 + n], ALU.mult, ALU.add),
                              reads=pk(b) + [xk(j, t)], writes=[xk(j, t)])

            load(0)
            load(1)
            gu(0)
            for g in range(NG):
                if g + 1 < NG:
                    gu(g + 1)
                down(g)
                if g + 2 < NG:
                    load(g + 2)

        def colblock_load(s, src2d, col0, ncols, kc, part="gu0", off=0):
            view = RING[:, s, off:off + kc * ncols].rearrange("p (k n) -> p k n", k=kc)
            wload(s, part, view, src2d[:, col0:col0 + ncols].rearrange("(k p) n -> p k n", p=128))
            return s, view

        def mixer(l, seg):
            kind, sidx, s0, ns, tiles = seg
            smp = kind == "S"
            nseq, ntok = (16, 4) if smp else (1, 1024)
            ltile = [(TILES[t][0] - s0, TILES[t][1], t) for t in tiles]

            def us(k, t, n):
                return US[:, k, TILES[t][0] - s0: TILES[t][0] - s0 + n]

            def uskey(k, t):
                return ("A", "us", k, t)

            S.fence()
            norm(tiles, l, 8, us, uskey)
            ckpt()
            S.fence()
            if smp:
                S.add("sp", dma(pool_s_o[l].rearrange("(s r) f -> s r f", r=15)[:, 0:11, :],
                                spool[l].rearrange("(s r) f -> s r f", r=15)[:, 4:15, :]), chan="st_pool_d2d")
            for cb in range(4):
                wwin = 2 << cb
                s = ring_next()
                _, wv = colblock_load(s, w_in[l], 2576 + cb * 256, 256, 8, "gu0", 0)
                wload(s, "gu1", RING[:, s, 2048:2560].rearrange("p (k n) -> p k n", k=2),
                      pool_w[l, cb].rearrange("(k p) n -> p k n", p=128))
                pwv = RING[:, s, 2048:2560].rearrange("p (k n) -> p k n", k=2)
                for m in range(2):
                    c = cb * 2 + m
                    A = PA[m]
                    A3 = A[:, 0:nseq * (16 + ntok)].rearrange("p (s q) -> p s q", s=nseq)
                    akey = ("A", "pa", m)
                    if smp:
                        S.add("sp", dma(PSTGC[0:120, :, 0:128],
                                        spool[l, :, c * 128:(c + 1) * 128].rearrange("(a p) f -> p a f", p=120)),
                              writes=[("A", "pstgc")], chan="ld_sp")
                        b = alloc()
                        S.add("pe", mms([(PS[:, b, a * 120:(a + 1) * 120], [(PSTGC[0:120, a, 0:128], IDF[0:120, 0:120])]) for a in range(2)]),
                              reads=[("A", "pstgc")] + CK, writes=pk(b))
                        S.add("dve", cp(A3[:, :, 1:16], PS[:, b, 0:240].rearrange("p (s r) -> p s r", r=15)), reads=pk(b), writes=[akey])
                    elif sidx == 0:
                        S.add("dve", ms(A[:, 0:16], 0.0), writes=[akey])
                    else:
                        S.add("dve", cp(A[:, 1:16], HISTP[:, c, :]), reads=[("histp", c)], writes=[akey])
                    for (lc, n, t) in ltile:
                        b = alloc()
                        S.add("pe", mm(PS[:, b, 0:n], [(wv[:, k, m * 128:(m + 1) * 128], us(k, t, n)) for k in range(8)]),
                              reads=[("ring", s, "gu0")] + [uskey(k, t) for k in range(8)], writes=pk(b))
                        if smp:
                            S.add("act", act(A3[:, :, 16:20], PS[:, b, 0:64].rearrange("p (s q) -> p s q", q=4), AF.Copy), reads=pk(b), writes=[akey])
                            S.add("act", act(SNEW[:, m, :], PS[:, b, 0:64], AF.Copy), reads=pk(b), writes=[("A", "snew", m)])
                        else:
                            S.add("act", act(A[:, 16 + lc:16 + lc + n], PS[:, b, 0:n], AF.Copy), reads=pk(b), writes=[akey])
                    LL = 16 + ntok
                    if smp:
                        if m == 0 and cb % 2 == 0:
                            pob = alloc(1, hold=True)
                        S.add("pe", mm(PS[0:64, pob, (c % 4) * 128:(c % 4 + 1) * 128], [(SNEW[:, m, :], IDF)]),
                              reads=[("A", "snew", m)] + CK, writes=pk(pob))
                    elif sidx == 0:
                        S.add("dve", cp(HISTP[:, c, :], A[:, LL - 15:LL]), reads=[akey], writes=[("histp", c)])
                    else:
                        if m == 0 and cb % 2 == 0:
                            pob = alloc(1, hold=True)
                        S.add("pe", mm(PS[0:15, pob, (c % 4) * 128:(c % 4 + 1) * 128], [(A[:, LL - 15:LL], IDF)]),
                              reads=[akey] + CK, writes=pk(pob))
                    if (smp or sidx == 1) and c % 4 == 3:
                        nr = 64 if smp else 15
                        hh = c // 4
                        S.add("act", act(POUT[0:nr, hh * 512:(hh + 1) * 512], PS[0:nr, pob, :], AF.Copy), reads=pk(pob), writes=[("A", "pout", hh)])
                        release(pob)
                        if c == 7:
                            if smp:
                                for sq in range(16):
                                    S.add("sp", dma(pool_s_o[l, sq * 15 + 11: sq * 15 + 15, :], POUT[sq * 4: sq * 4 + 4, :]),
                                          reads=[("A", "pout", 0), ("A", "pout", 1)], chan="st_pool%d" % (sq % 4))
                            else:
                                S.add("sp", dma(pool_p_o[l], POUT[0:15, :]), reads=[("A", "pout", 0), ("A", "pout", 1)], chan="st_pool0")
                    cur, curkey = A3, akey
                    bufs = [(PP1, ("A", "pp1")), (PP2, ("A", "pp2"))]
                    sh = 1
                    lvl = 0
                    while sh < wwin:
                        lo = 2 * sh - 1
                        nb_, nk = bufs[lvl % 2]
                        nb3 = nb_[:, 0:nseq * LL].rearrange("p (s q) -> p s q", s=nseq)
                        S.add("dve", tt(nb3[:, :, lo:LL], cur[:, :, lo:LL], cur[:, :, lo - sh:LL - sh], ALU.add), reads=[curkey], writes=[nk])
                        cur, curkey = nb3, nk
                        sh *= 2
                        lvl += 1
                    dst = MIX[:, c, 0:ns].rearrange("p (s q) -> p s q", s=nseq)
                    mkeys = [("A", "mix", c, t) for (_, _, t) in ltile]
                    S.add("dve", stt(dst, cur[:, :, 16:LL], 1.0 / wwin, A3[:, :, 16:LL], ALU.mult, ALU.subtract),
                          reads=[curkey, akey], writes=mkeys)
                    if (not smp) and sidx == 0:
                        S.add("dve", tt(TMP16[:, :], cur[:, 0, 16:32], RC[:, cb * 16:(cb + 1) * 16], ALU.mult), reads=[curkey] + CK, writes=[("A", "tmp16")])
                        S.add("dve", tt(MIX[:, c, 0:16], TMP16[:, :], A[:, 16:32], ALU.subtract), reads=[("A", "tmp16"), akey], writes=mkeys[0:1])
                for (lc, n, t) in ltile:
                    b = alloc(2)
                    S.add("pe", mms([(PS[:, b + m, 0:n], [(pwv[:, k, m * 128:(m + 1) * 128], MIX[:, cb * 2 + k, lc:lc + n]) for k in range(2)]) for m in range(2)]),
                          reads=[("ring", s, "gu1")] + [("A", "mix", cb * 2 + k, t) for k in range(2)], writes=pk(b, 2))
                    for m in range(2):
                        c = cb * 2 + m
                        S.add("act", act(MIX[:, c, lc:lc + n], PS[:, b + m, 0:n], AF.Copy, scale=P(l, 40 + c)),
                              reads=[("ps", b + m)] + CK, writes=[("A", "mix", c, t)])
            for jb in range(4):
                s = ring_next()
                _, wv = colblock_load(s, w_out[l, 1024:2048], jb * 256, 256, 8, "gu0", 0)
                for m in range(2):
                    j = jb * 2 + m
                    for (lc, n, t) in ltile:
                        b = alloc()
                        S.add("pe", mm(PS[:, b, 0:n], [(wv[:, k, m * 128:(m + 1) * 128], MIX[:, k, lc:lc + n]) for k in range(8)]),
                              reads=[("ring", s, "gu0")] + [("A", "mix", k, t) for k in range(8)], writes=pk(b))
                        g0 = TILES[t][0]
                        S.add("dve", tt(X[:, j, g0:g0 + n], X[:, j, g0:g0 + n], PS[:, b, 0:n], ALU.add), reads=pk(b) + [xk(j, t)], writes=[xk(j, t)])
            ckpt()
            S.fence()
            if smp:
                S.add("sp", dma(CSTG[0:48, :], sconv[l]), writes=[("A", "cstg")], chan="ld_sc")
                for q4 in range(3):
                    b = alloc()
                    S.add("pe", mms([(PS[:, b, jj * 48:(jj + 1) * 48], [(CSTG[0:48, (q4 * 4 + jj) * 128:(q4 * 4 + jj + 1) * 128], IDF[0:48, 0:48])]) for jj in range(4)]),
                          reads=[("A", "cstg")] + CK, writes=pk(b))
                    S.add("dve", cp(PASTT[:, q4 * 4:q4 * 4 + 4, :], PS[:, b, 0:192].rearrange("p (j q) -> p j q", j=4)), reads=pk(b), writes=[("A", "pastt", q4)])
            LR = 3 + ntok
            for cb in range(6):
                s = ring_next()
                _, wv = colblock_load(s, w_in[l], 1024 + cb * 256, 256, 8, "gu0", 0)
                for m in range(2):
                    c = cb * 2 + m
                    R = RAW[m]
                    R3 = R[:, 0:nseq * LR].rearrange("p (s q) -> p s q", s=nseq)
                    rkey = ("A", "raw", m)
                    if smp:
                        S.add("dve", cp(R3[:, :, 0:3], PASTT[:, c, :].rearrange("p (s r) -> p s r", r=3)), reads=[("A", "pastt", c // 4)], writes=[rkey])
                    elif sidx == 0:
                        S.add("dve", ms(R[:, 0:3], 0.0), writes=[rkey])
                    else:
                        S.add("dve", cp(R[:, 0:3], HISTC[:, c, :]), reads=[("histc", c)], writes=[rkey])
                    for (lc, n, t) in ltile:
                        b = alloc()
                        S.add("pe", mm(PS[:, b, 0:n], [(wv[:, k, m * 128:(m + 1) * 128], us(k, t, n)) for k in range(8)]),
                              reads=[("ring", s, "gu0")] + [uskey(k, t) for k in range(8)], writes=pk(b))
                        if smp:
                            S.add("act", act(R3[:, :, 3:7], PS[:, b, 0:64].rearrange("p (s q) -> p s q", q=4), AF.Copy), reads=pk(b), writes=[rkey])
                        else:
                            S.add("act", act(R[:, 3 + lc:3 + lc + n], PS[:, b, 0:n], AF.Copy), reads=pk(b), writes=[rkey])
                    if smp:
                        S.add("dve", cp(SELC[:, m, :].rearrange("p (s r) -> p s r", r=3), R3[:, :, 4:7]), reads=[rkey], writes=[("A", "selc", m)])
                        if c % 4 == 0:
                            cob = alloc(1, hold=True)
                        S.add("pe", mm(PS[0:48, cob, (c % 4) * 128:(c % 4 + 1) * 128], [(SELC[:, m, :], IDF)]), reads=[("A", "selc", m)] + CK, writes=pk(cob))
                    elif sidx == 0:
                        S.add("dve", cp(HISTC[:, c, :], R[:, LR - 3:LR]), reads=[rkey], writes=[("histc", c)])
                    else:
                        if c % 4 == 0:
                            cob = alloc(1, hold=True)
                        S.add("pe", mm(PS[0:3, cob, (c % 4) * 128:(c % 4 + 1) * 128], [(R[:, LR - 3:LR], IDF)]), reads=[rkey] + CK, writes=pk(cob))
                    if (smp or sidx == 1) and c % 4 == 3:
                        nr = 48 if smp else 3
                        q4 = c // 4
                        S.add("act", act(CSTG[0:nr, q4 * 512:(q4 + 1) * 512], PS[0:nr, cob, :], AF.Copy), reads=pk(cob), writes=[("A", "cstg")])
                        release(cob)
                        if c == 11:
                            S.add("sp", dma((conv_s_o if smp else conv_p_o)[l], CSTG[0:nr, :]), reads=[("A", "cstg")], chan="st_conv")
                    A3 = ACC[:, 0:ns].rearrange("p (s q) -> p s q", s=nseq)
                    S.add("dve", ts(A3, R3[:, :, 0:ntok], P(l, 48 + c * 4), P(l, 96 + c), ALU.mult, ALU.add), reads=[rkey] + CK, writes=[("A", "acc")])
                    for jt in range(1, 4):
                        S.add("dve", stt(A3, R3[:, :, jt:jt + ntok], P(l, 48 + c * 4 + jt), A3, ALU.mult, ALU.add), reads=[rkey, ("A", "acc")] + CK, writes=[("A", "acc")])
                    S.add("act", act(MIX[:, c, 0:ns], ACC[:, 0:ns], AF.Silu), reads=[("A", "acc")], writes=[("A", "mix", c, t) for (_, _, t) in ltile])
            s = ring_next()
            _, wv = colblock_load(s, w_in[l], 2560, 16, 8, "gu0", 0)
            for (lc, n, t) in ltile:
                b = alloc()
                S.add("pe", mm(PS[0:16, b, 0:n], [(wv[:, k, :], us(k, t, n)) for k in range(8)]),
                      reads=[("ring", s, "gu0")] + [uskey(k, t) for k in range(8)], writes=pk(b))
                S.add("act", act(DTT[:, lc:lc + n], PS[0:16, b, 0:n], AF.Exp, bias=PRM[0:16, l * PL + 140:l * PL + 141]), reads=pk(b) + CK, writes=[("A", "dtt", t)])
                S.add("act", act(DTT[:, lc:lc + n], DTT[:, lc:lc + n], AF.Ln, bias=1.0), reads=[("A", "dtt", t)], writes=[("A", "dtt", t)])
            for cb in range(4):
                s = ring_next()
                _, wv = colblock_load(s, w_in[l], cb * 256, 256, 8, "gu0", 0)
                for m in range(2):
                    c = cb * 2 + m
                    for (lc, n, t) in ltile:
                        b = alloc()
                        S.add("pe", mm(PS[:, b, 0:n], [(wv[:, k, m * 128:(m + 1) * 128], us(k, t, n)) for k in range(8)]),
                              reads=[("ring", s, "gu0")] + [uskey(k, t) for k in range(8)], writes=pk(b))
                        S.add("act", act(ZB[:, c, lc:lc + n], PS[:, b, 0:n], AF.Silu), reads=pk(b), writes=[("A", "zb", c, t)])
            ckpt()
            S.fence()
            AB = A_B[:, l, :]
            DB = P(l, 124, 16)
            if (not smp) and sidx == 0:
                S.add("dve", ms(HST[:], 0.0), writes=[("hst",)])
                S.add("dve", ms(HB[:], 0.0), writes=[("hb",)])
            if smp:
                S.add("sp", dma(E16[:, :], e16_d[:, :]), writes=[("A", "e16")], chan="ld_e16")
            nq = 64 if smp else 128
            nch = 1 if smp else min(8, int(os.environ.get("MK_NCH", 8)))
            TR = TRISEG[0:64, 0:64] if smp else TRI
            def chunk_gen(ci):
                q0 = ci * 128
                par = 0 if smp else ci % 2
                sfx = str(par)
                BIG, WW, XDT, XDTD, XSD, YT, GN, BTOK, SM, SML = SETS[par]
                DTTOK, DA, CUMCOL, EXPA, CDB, DOUT, SSQ, RSTD, TOTT = (SML[:, i, :] for i in range(9))
                tg = tiles[q0 // 512] if not smp else tiles[0]
                mk = lambda c: ("A", "mix", c, tg)
                bx = alloc(2)
                S.add("pe", mms([(PS[0:nq, bx + jj // 4, (jj % 4) * 128:(jj % 4 + 1) * 128], [(MIX[:, jj, q0:q0 + nq], IDB[:])]) for jj in range(8)]),
                      reads=[mk(c) for c in range(8)] + CK, writes=pk(bx, 2))
                bb = alloc()
                S.add("pe", mms([(PS[0:nq, bb, g * 128:(g + 1) * 128], [(MIX[:, 8 + g, q0:q0 + nq], IDB[:])]) for g in range(2)]
                                + [(PS[0:nq, bb, 256:272], [(DTT[0:16, q0:q0 + nq], IDF[0:16, 0:16])])]),
                      reads=[mk(8), mk(9), ("A", "dtt", tg)] + CK, writes=pk(bb))
                S.add("dve", cp(DTTOK[0:nq, :], PS[0:nq, bb, 256:272]), reads=pk(bb), writes=[("A", "dttok" + sfx)])
                S.add("dve", tt(DA[0:nq, :], PS[0:nq, bb, 256:272], AB[0:nq, :], ALU.mult), reads=pk(bb) + CK, writes=[("A", "da" + sfx)])
                S.add("act", act(BTOK[0:nq, :], PS[0:nq, bb, 0:256], AF.Copy), reads=pk(bb), writes=[("A", "btok" + sfx)])
                sub(1)
                bc = alloc()
                grp = [(PS[0:nq, bc, 0:16], [(TR[0:nq, 0:nq], DA[0:nq, :])])]
                if smp:
                    grp.append((PS[0:nq, bc, 16:32], [(BLK[0:64, 0:64], DA[0:nq, :])]))
                else:
                    grp.append((PS[0:nq, bc, 32:48], [(ONESF[0:nq, 0:nq], DA[0:nq, :])]))
                S.add("pe", mms(grp), reads=[("A", "da" + sfx)] + CK, writes=pk(bc))
                S.add("dve", cp(CUMCOL[0:nq, :], PS[0:nq, bc, 0:16]), reads=pk(bc), writes=[("A", "cumcol" + sfx)])
                S.add("act", act(EXPA[0:nq, :], PS[0:nq, bc, 0:16], AF.Exp), reads=pk(bc), writes=[("A", "expa" + sfx)])
                if not smp:
                    S.add("act", act(CDB[:, :], PS[:, bc, 32:48], AF.Exp), reads=pk(bc), writes=[("A", "cdb" + sfx, 0), ("A", "cdb" + sfx, 1)])
                if smp:
                    S.add("dve", tt(DOUT[0:nq, :], PS[0:nq, bc, 16:32], CUMCOL[0:nq, :], ALU.subtract), reads=pk(bc) + [("A", "cumcol" + sfx)], writes=[("A", "dout" + sfx)])
                    S.add("act", act(DOUT[0:nq, :], DOUT[0:nq, :], AF.Exp), reads=[("A", "dout" + sfx)], writes=[("A", "dout" + sfx)])
                sub(2)
                xs3 = PS[0:nq, bx:bx + 2, :].rearrange("p a (r q) -> p (a r) q", q=64)
                S.add("dve", tt(XDT[0:nq, :].rearrange("p (r q) -> p r q", q=64), xs3, DTTOK[0:nq, :].unsqueeze(2).broadcast_to([nq, 16, 64]), ALU.mult),
                      reads=pk(bx, 2) + [("A", "dttok" + sfx)], writes=[("A", "xdt" + sfx)])
                S.add("dve", tt(XSD[0:nq, :].rearrange("p (r q) -> p r q", q=64), xs3, DB[0:nq, :].unsqueeze(2).broadcast_to([nq, 16, 64]), ALU.mult),
                      reads=pk(bx, 2) + CK, writes=[("A", "xsd" + sfx)])
                sub(3)
                bs = alloc()
                S.add("pe", mms([(PS[0:nq, bs, g * 128:g * 128 + nq], [(MIX[:, 8 + g, q0:q0 + nq], MIX[:, 10 + g, q0:q0 + nq])]) for g in range(2)]),
                      reads=[mk(8), mk(9), mk(10), mk(11)], writes=pk(bs))
                S.add("dve", tt(SM[0:nq, :, 0:nq], PS[0:nq, bs, 0:256].rearrange("p (g q) -> p g q", g=2)[:, :, 0:nq],
                                TR[0:nq, 0:nq].unsqueeze(1).broadcast_to([nq, 2, nq]), ALU.mult), reads=pk(bs) + CK, writes=[("A", "sm" + sfx)])
                sub(4)
                by = alloc(2, hold=True)
                for g in range(2):
                    BIGg, WWg = SETS[g][0], SETS[g][1]
                    gs = "g%d" % g
                    B3 = BIGg[0:nq, 0:8 * nq].rearrange("p (r q) -> p r q", q=nq)
                    W3 = WWg[0:nq, 0:8 * nq].rearrange("p (r q) -> p r q", q=nq)
                    S.add("pool", tt(B3, TR[0:nq, 0:nq].unsqueeze(1).broadcast_to([nq, 8, nq]),
                                    DA[0:nq, 8 * g:8 * g + 8].unsqueeze(2).broadcast_to([nq, 8, nq]), ALU.mult),
                          reads=[("A", "da" + sfx)] + CK, writes=[("A", "big" + gs)])
                    nbk = (8 * nq) // 512
                    br = alloc(nbk)
                    SL = STRICTSEG[0:64, 0:64] if smp else STRICT
                    S.add("pe", mms([(PS[0:nq, br + i, :], [(SL[0:nq, 0:nq], BIGg[0:nq, i * 512:(i + 1) * 512])]) for i in range(nbk)]),
                          reads=[("A", "big" + gs)] + CK, writes=pk(br, nbk))
                    if nbk == 2:
                        cr3 = PS[0:nq, br:br + 2, :].rearrange("p a (r q) -> p (a r) q", q=nq)
                    else:
                        cr3 = PS[0:nq, br, :].rearrange("p (r q) -> p r q", q=nq)
                    S.add("act", act(B3, cr3, AF.Exp), reads=pk(br, nbk), writes=[("A", "big" + gs)])
                    if not smp:
                        S.add("dve", cp(DOUT[0:nq, 8 * g:8 * g + 8], B3[:, :, nq - 1]), reads=[("A", "big" + gs)], writes=[("A", "dout" + sfx)])
                    S.add("dve", tt(W3, B3, SM[0:nq, g, 0:nq].unsqueeze(1).broadcast_to([nq, 8, nq]), ALU.mult),
                          reads=[("A", "big" + gs), ("A", "sm" + sfx)], writes=[("A", "ww" + gs)])

                    def ydiag(g=g, W3=W3, by=by, nq=nq, XSD=XSD, XDT=XDT):
                        def fn(e):
                            e.matmul(PS[0:nq, by + g, :], IDB[0:nq, 0:nq], XSD[0:nq, g * 512:(g + 1) * 512], start=True, stop=False)
                            for r in range(8):
                                m_ = e.matmul(PS[0:nq, by + g, r * 64:(r + 1) * 64], W3[:, r, :],
                                              XDT[0:nq, (8 * g + r) * 64:(8 * g + r + 1) * 64], start=False, stop=(r == 7))
                            return m_
                        return fn
                    S.add("pe", ydiag(), reads=[("A", "ww" + gs), ("A", "xdt" + sfx), ("A", "xsd" + sfx)] + CK, writes=pk(by + g))
                sub(5)
                S.add("pool", tt(XDTD[0:nq, :].rearrange("p (r q) -> p r q", q=64), XDT[0:nq, :].rearrange("p (r q) -> p r q", q=64),
                                DOUT[0:nq, :].unsqueeze(2).broadcast_to([nq, 16, 64]), ALU.mult), reads=[("A", "xdt" + sfx), ("A", "dout" + sfx)], writes=[("A", "xdtd" + sfx)])
                bo = alloc(2, hold=True)
                if not smp:
                    S.add("pe", mms([(PS[0:nq, bo + g, :], [(MIX[:, 10 + g, q0:q0 + nq], HB[:, g * 512:(g + 1) * 512])]) for g in range(2)]),
                          reads=[mk(10), mk(11), ("hb",)], writes=pk(bo, 2))
                else:
                    S.add("dve", ts(SML[0:16, 10:14, :].rearrange("p a b -> p (a b)"), DTT[0:16, 0:64], ACOL[:, l:l + 1], 0.0, ALU.mult, ALU.add),
                          reads=[("A", "dtt", tg)] + CK, writes=[("A", "dat")])
                    S.add("dve", lambda e: e.tensor_reduce(out=TOTT[0:16, :], in_=SML[0:16, 10:14, :].rearrange("p a b -> p (a b)").rearrange("p (s q) -> p s q", q=4),
                                                            axis=AX.X, op=ALU.add), reads=[("A", "dat")], writes=[("A", "tott")])
                    bcd = alloc()
                    S.add("pe", mms([(PS[:, bcd, j * 16:(j + 1) * 16], [(E16[0:16, j * 128:(j + 1) * 128], TOTT[0:16, :])]) for j in range(8)]),
                          reads=[("A", "e16"), ("A", "tott")], writes=pk(bcd))
                    S.add("act", act(CDF[:, :, :], PS[:, bcd, 0:128].rearrange("p (j s) -> p j s", j=8), AF.Exp), reads=pk(bcd), writes=[("A", "cdf")])
                    hbufs = [(HST, ("hst",)), (H0X, ("A", "h0x")), (av(8192, [128, 1024]), ("A", "h0y")), (av(12288, [128, 1024]), ("A", "h0z"))]

                    def hload(sq_):
                        H0_, hkey_ = hbufs[sq_ % 4]
                        S.add("sp", dma(H0_[:, :].rearrange("p (j n) -> p j n", j=8), sssm[l, sq_].rearrange("(j p) n -> p j n", p=128)),
                              writes=[hkey_], chan="ld_h%d" % (sq_ % 4))
                    hload(0)
                    hload(1)
                    for sq in range(16):
                        H0, hkey = hbufs[sq % 4]
                        H03 = H0[:, :].rearrange("p (j n) -> p j n", j=8)
                        if sq + 2 < 16:
                            hload(sq + 2)
                        bt = alloc(2)
                        S.add("pe", mms([(PS[:, bt + j // 4, (j % 4) * 128:(j % 4 + 1) * 128], [(H03[:, j, :], IDF)]) for j in range(8)]),
                              reads=[hkey] + CK, writes=pk(bt, 2))
                        S.add("act", act(HB[:, :].rearrange("p (a b) -> p a b", a=2), PS[:, bt:bt + 2, :], AF.Copy), reads=pk(bt, 2), writes=[("hb",)])
                        cs_ = sq % 2
                        S.add("dve", ms(CMS[:, cs_, :, :], 0.0), writes=[("A", "cms", cs_)])
                        S.add("dve", cp(CMS[:, cs_, :, 4 * sq:4 * sq + 4], MIX[:, 10:12, 4 * sq:4 * sq + 4]), reads=[mk(10), mk(11)], writes=[("A", "cms", cs_)])
                        S.add("pe", (lambda sq=sq, cs_=cs_, bo=bo: (lambda e: [e.matmul(PS[0:64, bo + g, :], CMS[:, cs_, g, :], HB[:, g * 512:(g + 1) * 512],
                                                                                    start=(sq == 0), stop=(sq == 15)) for g in range(2)][-1]))(),
                              reads=[("A", "cms", cs_), ("hb",)], writes=pk(bo, 2))
                        S.add("dve", ts(BMS[0:64, cs_, :], BTOK[0:64, :], MASKCOL[0:64, sq:sq + 1], 0.0, ALU.mult, ALU.add),
                              reads=[("A", "btok" + sfx)] + CK, writes=[("A", "bms", cs_)])
                        bn = alloc(2)
                        S.add("pe", mms([(PS[:, bn + j // 4, (j % 4) * 128:(j % 4 + 1) * 128],
                                          [(XDTD[0:64, j * 128:(j + 1) * 128], BMS[0:64, cs_, (j // 4) * 128:(j // 4 + 1) * 128])]) for j in range(8)]),
                              reads=[("A", "xdtd" + sfx), ("A", "bms", cs_)], writes=pk(bn, 2))
                        S.add("dve", tt(H03, H03, CDF[:, :, sq].unsqueeze(2).broadcast_to([128, 8, 128]), ALU.mult), reads=[hkey, ("A", "cdf")], writes=[hkey])
                        S.add("dve", tt(H03, H03, PS[:, bn:bn + 2, :].rearrange("p a (j n) -> p (a j) n", n=128), ALU.add), reads=[hkey] + pk(bn, 2), writes=[hkey])
                        S.add("sp", dma(ssm_s_o[l, sq].rearrange("(j p) n -> p j n", p=128), H03), reads=[hkey], chan="st_h%d" % (sq % 4))
                if not smp:
                    bsp = alloc(2)
                    S.add("pe", mms([(PS[:, bsp + g, :], [(BTOK[:, g * 128:(g + 1) * 128], XDTD[:, g * 512:(g + 1) * 512])]) for g in range(2)]),
                          reads=[("A", "btok" + sfx), ("A", "xdtd" + sfx)], writes=pk(bsp, 2))
                    S.add("pool", tt(HST[:, :].rearrange("p (r q) -> p r q", q=64), HST[:, :].rearrange("p (r q) -> p r q", q=64),
                                    CDB[:, :].unsqueeze(2).broadcast_to([128, 16, 64]), ALU.mult), reads=[("hst",), ("A", "cdb" + sfx, 0), ("A", "cdb" + sfx, 1)], writes=[("hst",)])
                    S.add("dve", tt(HST[:, :].rearrange("p (a b) -> p a b", a=2), HST[:, :].rearrange("p (a b) -> p a b", a=2), PS[:, bsp:bsp + 2, :], ALU.add),
                          reads=[("hst",)] + pk(bsp, 2), writes=[("hst",)])
                    S.add("act", act(HB[:], HST[:], AF.Copy), reads=[("hst",)], writes=[("hb",)])
                sub(6)
                S.add("dve", tt(YT[0:nq, :].rearrange("p (r q) -> p r q", q=64), PS[0:nq, bo:bo + 2, :].rearrange("p a (r q) -> p (a r) q", q=64),
                                EXPA[0:nq, :].unsqueeze(2).broadcast_to([nq, 16, 64]), ALU.mult), reads=pk(bo, 2) + [("A", "expa" + sfx)], writes=[("A", "yt" + sfx)])
                release(bo, 2)
                S.add("dve", tt(YT[0:nq, :].rearrange("p (a b) -> p a b", a=2), YT[0:nq, :].rearrange("p (a b) -> p a b", a=2), PS[0:nq, by:by + 2, :], ALU.add),
                      reads=pk(by, 2) + [("A", "yt" + sfx)], writes=[("A", "yt" + sfx)])
                release(by, 2)
                yield
                bz = alloc(2)
                S.add("pe", mms([(PS[0:nq, bz + jj // 4, (jj % 4) * 128:(jj % 4 + 1) * 128], [(ZB[:, jj, q0:q0 + nq], IDB[:])]) for jj in range(8)]),
                      reads=[("A", "zb", c, tg) for c in range(8)] + CK, writes=pk(bz, 2))
                S.add("dve", tt(YT[0:nq, :].rearrange("p (a b) -> p a b", a=2), YT[0:nq, :].rearrange("p (a b) -> p a b", a=2), PS[0:nq, bz:bz + 2, :], ALU.mult),
                      reads=pk(bz, 2) + [("A", "yt" + sfx)], writes=[("A", "yt" + sfx)])
                S.add("dve", ms(SSQ[0:nq, 0:1], 0.0), writes=[("A", "ssq" + sfx)])
                S.add("act", act(GN[0:nq, :], YT[0:nq, :], AF.Square, accum_out=SSQ[0:nq, 0:1]), reads=[("A", "yt" + sfx), ("A", "ssq" + sfx)], writes=[("A", "gn" + sfx), ("A", "ssq" + sfx)])
                S.add("act", act(RSTD[0:nq, 0:1], SSQ[0:nq, 0:1], AF.Ln, bias=EPS[0:nq, 0:1], scale=1.0 / 1024.0), reads=[("A", "ssq" + sfx)] + CK, writes=[("A", "rstd" + sfx)])
                S.add("act", act(RSTD[0:nq, 0:1], RSTD[0:nq, 0:1], AF.Exp, scale=-0.5), reads=[("A", "rstd" + sfx)], writes=[("A", "rstd" + sfx)])
                S.add("act", act(GN[0:nq, :], YT[0:nq, :], AF.Copy, scale=RSTD[0:nq, 0:1]), reads=[("A", "yt" + sfx), ("A", "rstd" + sfx)], writes=[("A", "gn" + sfx)])
                sub(7)
                bk = alloc(2)
                S.add("pe", mms([(PS[:, bk + jj // 4, (jj % 4) * 128:(jj % 4) * 128 + nq], [(GN[0:nq, jj * 128:(jj + 1) * 128], IDB[0:nq, 0:nq])]) for jj in range(8)]),
                      reads=[("A", "gn" + sfx)] + CK, writes=pk(bk, 2))
                for a in range(2):
                    S.add("dve", tt(MIX[:, 4 * a:4 * a + 4, q0:q0 + nq], PS[:, bk + a, :].rearrange("p (j q) -> p j q", j=4)[:, :, 0:nq],
                                    P(l, 32 + 4 * a, 4).unsqueeze(2).broadcast_to([128, 4, nq]), ALU.mult),
                          reads=pk(bk + a) + CK, writes=[mk(c) for c in range(4 * a, 4 * a + 4)])
            prev = None
            for ci in range(nch):
                gcur = chunk_gen(ci)
                next(gcur)
                if prev is not None:
                    for _ in prev:
                        pass
                prev = gcur
            for _ in prev:
                pass
            if (not smp) and sidx == 1:
                YT = SETS[0][5]
                sfx = "0"
                bf = alloc(2)
                S.add("pe", mms([(PS[:, bf + j // 4, (j % 4) * 128:(j % 4 + 1) * 128], [(HST[:, j * 128:(j + 1) * 128], IDF)]) for j in range(8)]),
                      reads=[("hst",)] + CK, writes=pk(bf, 2))
                S.add("act", act(YT[:, :].rearrange("p (a b) -> p a b", a=2), PS[:, bf:bf + 2, :], AF.Copy), reads=pk(bf, 2), writes=[("A", "yt" + sfx)])
                S.add("sp", dma(ssm_p_o[l].rearrange("(j p) n -> p j n", p=128), YT[:, :].rearrange("p (j n) -> p j n", j=8)), reads=[("A", "yt" + sfx)], chan="st_ssmp")
            ckpt()
            for jb in range(4):
                s = ring_next()
                _, wv = colblock_load(s, w_out[l, 0:1024], jb * 256, 256, 8, "gu0", 0)
                for m in range(2):
                    j = jb * 2 + m
                    for (lc, n, t) in ltile:
                        b = alloc()
                        S.add("pe", mm(PS[:, b, 0:n], [(wv[:, k, m * 128:(m + 1) * 128], MIX[:, k, lc:lc + n]) for k in range(8)]),
                              reads=[("ring", s, "gu0")] + [("A", "mix", k, t) for k in range(8)], writes=pk(b))
                        g0 = TILES[t][0]
                        S.add("dve", tt(X[:, j, g0:g0 + n], X[:, j, g0:g0 + n], PS[:, b, 0:n], ALU.add), reads=pk(b) + [xk(j, t)], writes=[xk(j, t)])

        def ple(l):
            S.fence()
            for blk in range(17):
                r0 = blk * 128
                n = 128 if blk < 16 else 64
                t = blk // 4
                sl = blk % 2
                S.add("sp", dma(PSTG[0:n, sl, :], pin[l, r0:r0 + n, :]), writes=[("A", "pstg", sl)], chan="ld_p%d" % sl)
                b = alloc()
                S.add("pe", mms([(PS[:, b, k * 128:k * 128 + n], [(PSTG[0:n, sl, k * 128:(k + 1) * 128], IDF[0:n, 0:n])]) for k in range(2)]),
                      reads=[("A", "pstg", sl)] + CK, writes=pk(b))
                S.add("act", act(PT[:, :, r0:r0 + n], PS[:, b, 0:256].rearrange("p (k q) -> p k q", k=2)[:, :, 0:n], AF.Copy), reads=pk(b), writes=[("A", "pt", blk)])
            norm(range(5), l, 24, u_full, ukey)
            for jb in range(4):
                s = ring_next()
                _, wv = colblock_load(s, ple_gate[l], jb * 256, 256, 8, "gu0", 0)
                wload(s, "gu1", RING[:, s, 2048:2560].rearrange("p (k n) -> p k n", k=2),
                      ple_proj[l][:, jb * 256:(jb + 1) * 256].rearrange("(k p) n -> p k n", p=128))
                pv = RING[:, s, 2048:2560].rearrange("p (k n) -> p k n", k=2)
                for m in range(2):
                    j = jb * 2 + m
                    for t in range(5):
                        t0, n = TILES[t]
                        b = alloc(2)
                        S.add("pe", mms([(PS[:, b, 0:n], [(wv[:, k, m * 128:(m + 1) * 128], U[:, k, t0:t0 + n]) for k in range(8)]),
                                         (PS[:, b + 1, 0:n], [(pv[:, k, m * 128:(m + 1) * 128], PT[:, k, t0:t0 + n]) for k in range(2)])]),
                              reads=[("ring", s, "gu0"), ("ring", s, "gu1")] + [ukey(k, t) for k in range(8)] + [("A", "pt", bb_) for bb_ in range(17)], writes=pk(b, 2))
                        S.add("act", act(SGT[:, 0:n], PS[:, b, 0:n], AF.Sigmoid), reads=pk(b), writes=[("A", "sgt")])
                        S.add("dve", tt(TMP2[:, 0:n], SGT[:, 0:n], PS[:, b + 1, 0:n], ALU.mult), reads=[("A", "sgt")] + pk(b + 1), writes=[("A", "tmp2")])
                        S.add("dve", tt(X[:, j, t0:t0 + n], X[:, j, t0:t0 + n], TMP2[:, 0:n], ALU.add), reads=[("A", "tmp2"), xk(j, t)], writes=[xk(j, t)])

        SEGS = [("P", 0, 0, 1024, [0, 1]), ("P", 1, 1024, 1024, [2, 3]), ("S", 2, 2048, 64, [4])]
        try:
            ckpt()
            for l in range(nlayers):
                ffn(l, 0)
                ckpt()
                for seg in SEGS:
                    mixer(l, seg)
                    ckpt()
                ffn(l, 1)
                ckpt()
                ple(l)
                ckpt()
        except _Stop:
            pass
        S.fence()
        norm(range(5), None, 0, lambda k, t, n: X[:, k, TILES[t][0]:TILES[t][0] + n], lambda k, t: xk(k, t))
        S.fence()
        for blk in range(17):
            r0 = blk * 128
            n = 128 if blk < 16 else 64
            t = blk // 4
            sl = blk % 2
            b = alloc(2)
            S.add("pe", mms([(PS[0:n, b + j // 4, (j % 4) * 128:(j % 4 + 1) * 128], [(X[:, j, r0:r0 + n], IDF)]) for j in range(8)]),
                  reads=[xk(j, t) for j in range(8)] + CK, writes=pk(b, 2))
            if blk % 2 == 0:
                S.add("act", act(XSTG[0:n, sl, :].rearrange("p (a b) -> p a b", a=2), PS[0:n, b:b + 2, :], AF.Copy), reads=pk(b, 2), writes=[("A", "xstg", sl)])
            else:
                S.add("dve", cp(XSTG[0:n, sl, :].rearrange("p (a b) -> p a b", a=2), PS[0:n, b:b + 2, :]), reads=pk(b, 2), writes=[("A", "xstg", sl)])
            S.add("sp", dma(y_o[r0:r0 + n, :], XSTG[0:n, sl, :]), reads=[("A", "xstg", sl)], chan="st_y%d" % sl)
        S.emit(nc, es)
    return nc


def _host_consts():
    cst = np.zeros((128, NCST), np.float32)
    cst[:, 0:128] = np.eye(128, dtype=np.float32)
    s = np.arange(128)[:, None]
    q = np.arange(128)[None, :]
    cst[:, 128:256] = (s <= q).astype(np.float32)
    s6 = np.arange(64)[:, None]
    q6 = np.arange(64)[None, :]
    cst[0:64, 256:320] = ((s6 <= q6) & (s6 // 4 == q6 // 4)).astype(np.float32)
    cst[0:64, 320:384] = (s6 // 4 == q6 // 4).astype(np.float32)
    cst[0:64, 384:400] = (s6 // 4 == np.arange(16)[None, :]).astype(np.float32)
    for gi, w in enumerate((2, 4, 8, 16)):
        cst[:, 400 + gi * 16: 400 + (gi + 1) * 16] = (1.0 / np.minimum(np.arange(16) + 1, w)).astype(np.float32)[None, :]
    cst[:, 464:592] = (s > q).astype(np.float32)
    cst[0:64, 592:656] = ((s6 > q6) & (s6 // 4 == q6 // 4)).astype(np.float32)
    e16 = np.zeros((16, 1024), np.float32)
    for r in range(16):
        e16[r, r * 64:(r + 1) * 64] = 1.0
    return cst, e16


def _host_params(inp):
    prm = np.zeros((128, NPRM), np.float32)

    def cols(v):
        return np.asarray(v, np.float32).reshape(8, 128).T

    for l in range(L):
        b = l * PL
        prm[:, b + 0:b + 8] = cols(inp["norm_ffn1"][l])
        prm[:, b + 8:b + 16] = cols(inp["norm_mix"][l])
        prm[:, b + 16:b + 24] = cols(inp["norm_ffn2"][l])
        prm[:, b + 24:b + 32] = cols(inp["norm_ple"][l])
        prm[:, b + 32:b + 40] = cols(inp["ssd_norm_w"][l])
        prm[:, b + 40:b + 48] = cols(inp["pool_scale"][l])
        cw = np.asarray(inp["conv_w"][l], np.float32)
        prm[:, b + 48:b + 96] = cw.reshape(4, 12, 128).transpose(2, 1, 0).reshape(128, 48)
        prm[:, b + 96:b + 108] = np.asarray(inp["conv_b"][l], np.float32).reshape(12, 128).T
        prm[:, b + 108:b + 124] = np.asarray(inp["a_log"][l], np.float32)[None, :]
        prm[:, b + 124:b + 140] = np.asarray(inp["d_skip"][l], np.float32)[None, :]
        prm[0:16, b + 140] = np.asarray(inp["dt_bias"][l], np.float32)
        prm[0:16, b + 141] = np.asarray(inp["a_log"][l], np.float32)
    prm[:, L * PL:L * PL + 8] = cols(inp["final_norm"])
    return prm


_NC_CACHE = {}


def kernel(**inp):
    inp = {k: np.asarray(v) for k, v in inp.items()}
    nlayers = int(os.environ.get("MK_LAYERS", L))
    if nlayers not in _NC_CACHE:
        _NC_CACHE[nlayers] = build_nc(nlayers)
    nc = _NC_CACHE[nlayers]
    cst, e16 = _host_consts()
    prm = _host_params(inp)
    shared = {k: np.ascontiguousarray(inp[k], dtype=np.float32) for k in
              ("w_in", "w_out", "pool_w", "ffn1_gate", "ffn2_gate", "ffn1_up", "ffn2_up", "ffn1_down", "ffn2_down", "ple_gate", "ple_proj")}
    in_maps = []
    for c in range(NCORE):
        sl = slice(16 * c, 16 * c + 16)
        m = dict(shared)
        m["xin"] = np.ascontiguousarray(np.concatenate([inp["x_prompt"][c], inp["x_sample"][sl].reshape(NSM, D)], axis=0), dtype=np.float32)
        m["pin"] = np.ascontiguousarray(np.concatenate([inp["p_prompt"][:, c], inp["p_sample"][:, sl].reshape(L, NSM, 256)], axis=1), dtype=np.float32)
        m["sssm"] = np.ascontiguousarray(inp["state_ssm"][:, sl].reshape(L, 16, 1024, 128), dtype=np.float32)
        m["sconv"] = np.ascontiguousarray(inp["state_conv"][:, sl].reshape(L, 48, CONV), dtype=np.float32)
        m["spool"] = np.ascontiguousarray(inp["state_pool"][:, sl].reshape(L, 240, 1024), dtype=np.float32)
        m["prm"] = prm
        m["cst"] = cst
        m["e16"] = e16
        in_maps.append(m)
    ncr = int(os.environ.get("MK_CORES", NCORE))
    res = run_bass_kernel_spmd(nc, in_maps[:ncr], core_ids=list(range(ncr)))
    R = list(res.results) + [res.results[0]] * (NCORE - ncr)
    y_prompt = np.stack([R[c]["y"][0:NPR] for c in range(NCORE)], axis=0)
    y_sample = np.concatenate([R[c]["y"][NPR:].reshape(16, 4, D) for c in range(NCORE)], axis=0)
    ssm_prompt = np.stack([R[c]["ssm_p"].reshape(L, 16, 64, 128) for c in range(NCORE)], axis=1)
    conv_prompt = np.stack([R[c]["conv_p"] for c in range(NCORE)], axis=1)
    pool_prompt = np.stack([R[c]["pool_p"] for c in range(NCORE)], axis=1)
    ssm_sample = np.concatenate([R[c]["ssm_s"].reshape(L, 16, 16, 64, 128) for c in range(NCORE)], axis=1)
    conv_sample = np.concatenate([R[c]["conv_s"].reshape(L, 16, 3, CONV) for c in range(NCORE)], axis=1)
    pool_sample = np.concatenate([R[c]["pool_s"].reshape(L, 16, 15, 1024) for c in range(NCORE)], axis=1)
    f = lambda a: np.ascontiguousarray(a, dtype=np.float32)
    return (f(y_prompt), f(y_sample), f(ssm_prompt), f(conv_prompt), f(pool_prompt), f(ssm_sample), f(conv_sample), f(pool_sample))
```

```python
import os
from contextlib import ExitStack
from math import prod

import numpy as np
import concourse.bass as bass
import concourse.mybir as mybir
from concourse.bass_utils import run_bass_kernel_spmd

F32 = mybir.dt.float32
BF16 = mybir.dt.bfloat16
AF = mybir.ActivationFunctionType
ALU = mybir.AluOpType
AX = mybir.AxisListType

NCORE = 8
L = 4
D = 1024
NPR = 2048
NSM = 64
T = NPR + NSM
DFF = 2816
DIN = 3600
CONV = 1536
EPSV = 1e-6
TILES = [(0, 512), (512, 512), (1024, 512), (1536, 512), (2048, 64)]
PL = 142
NPRM = L * PL + 8
NCST = 656
ENGS = ("pe", "act", "dve", "pool", "sp")


class Sched:
    def __init__(self):
        self.ops = []
        self.last_writer = {}
        self.readers = {}
        self.chan_last = {}
        self.fence_deps = set()
        self.fenced = set()
        self.arena_touch = {}

    def add(self, eng, fn, reads=(), writes=(), chan=None):
        idx = len(self.ops)
        if idx >= int(os.environ.get("MK_MAXOPS", 10 ** 9)) and not getattr(self, "nolimit", False):
            self.nolimit = True
            print("MAXOPS stop at", idx, eng, reads, writes)
            raise _Stop()
        deps = set()
        touches = False
        if eng != "pe":
            writes = list(writes) + [k for k in reads if k[0] == "ps" and k not in writes]
        for k in list(reads) + list(writes):
            if k[0] == "A":
                touches = True
                if k not in self.fenced:
                    deps.update(self.fence_deps)
                    self.fenced.add(k)
        for k in reads:
            if k in self.last_writer:
                deps.add(self.last_writer[k])
        for k in writes:
            if k in self.last_writer:
                deps.add(self.last_writer[k])
            deps.update(self.readers.get(k, ()))
        if chan is not None and chan in self.chan_last:
            deps.add(self.chan_last[chan])
        deps.discard(idx)
        for k in reads:
            self.readers.setdefault(k, []).append(idx)
        for k in writes:
            self.last_writer[k] = idx
            self.readers[k] = []
        if chan is not None:
            self.chan_last[chan] = idx
        if touches:
            self.arena_touch[chan if chan is not None else eng] = idx
        self.ops.append(dict(eng=eng, fn=fn, deps=deps, chan=chan, signal=False))
        return idx

    def fence(self):
        allp = set(self.fence_deps) | set(self.arena_touch.values())
        best = {}
        for i in allp:
            o = self.ops[i]
            k = o["chan"] if o["chan"] is not None else o["eng"]
            if k not in best or best[k] < i:
                best[k] = i
        self.fence_deps = set(best.values())
        self.fenced = set()
        self.arena_touch = {}
        for k in [k for k in self.last_writer if k[0] == "A"]:
            del self.last_writer[k]
        for k in [k for k in self.readers if k[0] == "A"]:
            del self.readers[k]

    def emit(self, nc, es):
        ops = self.ops
        for o in ops:
            nd = set()
            for d in o["deps"]:
                p = ops[d]
                if o["eng"] == "pe" and p["eng"] == "pe" and p["chan"] is None and o["chan"] is None:
                    continue
                nd.add(d)
                p["signal"] = True
            o["deps"] = nd
        chans = sorted({o["chan"] for o in ops if o["chan"] is not None})
        sems = {}
        for e in ENGS:
            sems[e] = es.enter_context(nc.semaphore("s_" + e))
        for c in chans:
            sems[c] = es.enter_context(nc.semaphore("c_" + c))
        cnt = {k: 0 for k in sems}
        for o in ops:
            if o["chan"] is not None:
                cnt[o["chan"]] += 16
                o["tick"] = (o["chan"], cnt[o["chan"]])
            elif o["signal"]:
                cnt[o["eng"]] += 1
                o["tick"] = (o["eng"], cnt[o["eng"]])
        block = es.enter_context(nc.Block())

        def run(engname, e):
            waited = {}
            for o in ops:
                if o["eng"] != engname:
                    continue
                need = {}
                for d in o["deps"]:
                    s, v = ops[d]["tick"]
                    if need.get(s, 0) < v:
                        need[s] = v
                for s, v in need.items():
                    if waited.get(s, 0) < v:
                        e.wait_ge(sems[s], v)
                        waited[s] = v
                ins = o["fn"](e)
                if o["chan"] is not None:
                    ins.then_inc(sems[o["chan"]], 16)
                elif o["signal"]:
                    ins.then_inc(sems[o["eng"]], 1)
            if engname == "sp":
                for s, v in cnt.items():
                    if v > 0 and waited.get(s, 0) < v:
                        e.wait_ge(sems[s], v)

        @block.tensor
        def _(e):
            run("pe", e)

        @block.scalar
        def _(e):
            run("act", e)

        @block.vector
        def _(e):
            run("dve", e)

        @block.gpsimd
        def _(e):
            run("pool", e)

        @block.sync
        def _(e):
            run("sp", e)


class _Stop(Exception):
    pass


def build_nc(nlayers=L):
    stop_at = int(os.environ.get("MK_STOP", 999))
    stage = [0]

    def ckpt():
        stage[0] += 1
        if stage[0] >= stop_at:
            raise _Stop()

    sub_at = int(os.environ.get("MK_SUB", 999))

    def sub(k):
        if k >= sub_at:
            raise _Stop()

    nc = bass.Bass("TRN2", target_bir_lowering=False)
    S = Sched()
    es = ExitStack()

    def din(name, shape):
        return nc.dram_tensor(name, list(shape), F32, kind="ExternalInput").ap()

    def dout(name, shape):
        return nc.dram_tensor(name, list(shape), F32, kind="ExternalOutput").ap()

    xin = din("xin", [T, D])
    pin = din("pin", [L, T, 256])
    sssm = din("sssm", [L, 16, 1024, 128])
    sconv = din("sconv", [L, 48, CONV])
    spool = din("spool", [L, 240, 1024])
    w_in = din("w_in", [L, D, DIN])
    w_out = din("w_out", [L, 2048, D])
    pool_w = din("pool_w", [L, 4, 256, 256])
    fg = [din("ffn1_gate", [L, D, DFF]), din("ffn2_gate", [L, D, DFF])]
    fu = [din("ffn1_up", [L, D, DFF]), din("ffn2_up", [L, D, DFF])]
    fd = [din("ffn1_down", [L, DFF, D]), din("ffn2_down", [L, DFF, D])]
    ple_gate = din("ple_gate", [L, D, D])
    ple_proj = din("ple_proj", [L, 256, D])
    prm_d = din("prm", [128, NPRM])
    cst_d = din("cst", [128, NCST])
    e16_d = din("e16", [16, 1024])

    y_o = dout("y", [T, D])
    ssm_p_o = dout("ssm_p", [L, 1024, 128])
    conv_p_o = dout("conv_p", [L, 3, CONV])
    pool_p_o = dout("pool_p", [L, 15, 1024])
    ssm_s_o = dout("ssm_s", [L, 16, 1024, 128])
    conv_s_o = dout("conv_s", [L, 48, CONV])
    pool_s_o = dout("pool_s", [L, 240, 1024])

    with es:
        def sb(name, shape, dt=F32):
            return es.enter_context(nc.sbuf_tensor(name, list(shape), dt))

        X = sb("X", [128, 8, T])
        RING = sb("RING", [128, 2, 6144], BF16)
        ARENA_B = 92928
        ARENA = sb("ARENA", [128, ARENA_B // 4])
        HST = sb("HST", [128, 1024])
        HB = sb("HB", [128, 1024], BF16)
        HISTC = sb("HISTC", [128, 12, 3])
        HISTP = sb("HISTP", [128, 8, 15])
        PRM = sb("PRM", [128, NPRM])
        CST = sb("CST", [128, NCST])
        IDB = sb("IDB", [128, 128], BF16)
        ONESB = sb("ONESB", [128, 128], BF16)
        ONESF = sb("ONESF", [128, 128])
        EPS = sb("EPS", [128, 1])
        STRB = sb("STRB", [128, 128], BF16)
        STRSEGB = sb("STRSEGB", [128, 64], BF16)
        A_B = sb("A_B", [128, L, 16])
        ACOL = sb("ACOL", [16, L])
        PS = es.enter_context(nc.psum_tensor("PS", [128, 8, 512], F32))

        IDF = CST[:, 0:128]
        TRI = CST[:, 128:256]
        TRISEG = CST[:, 256:320]
        BLK = CST[:, 320:384]
        MASKCOL = CST[:, 384:400]
        RC = CST[:, 400:464]
        STRICT = CST[:, 464:592]
        STRICTSEG = CST[:, 592:656]

        def av(off_b, shape, dt=F32):
            nel = prod(shape[1:])
            nb = nel * (4 if dt == F32 else 2)
            assert off_b % 4 == 0 and nb % 4 == 0 and off_b + nb <= ARENA_B, (off_b, shape)
            v = ARENA[0:shape[0], off_b // 4: (off_b + nb) // 4]
            if dt == BF16:
                v = v.bitcast(BF16)
            if len(shape) == 3:
                v = v.rearrange("p (a b) -> p a b", a=shape[1])
            elif len(shape) == 4:
                v = v.rearrange("p (a b c) -> p a b c", a=shape[1], b=shape[2])
            return v

        U = av(0, [128, 8, T], BF16)
        HH = av(33792, [128, 2, 2, T], BF16)
        SG = av(50688, [128, 2, 512], BF16)
        SQ = av(62976, [128, 8, 512], BF16)
        RS = av(71168, [128, 512])
        PT = av(73216, [128, 2, T], BF16)
        SGT = av(81664, [128, 512])
        TMP2 = av(83712, [128, 512])
        PSTG = av(85760, [128, 2, 256])
        XSTG = av(0, [128, 2, 1024])
        US = av(0, [128, 8, 1024], BF16)
        MIX = av(16384, [128, 12, 1024], BF16)
        ZB = av(40960, [128, 8, 1024], BF16)
        DTT = av(57344, [16, 1024])
        SC = 61440
        PA = [av(SC, [128, 1040]), av(SC + 4160, [128, 1040])]
        PP1 = av(SC + 8320, [128, 1040])
        PP2 = av(SC + 12480, [128, 1040])
        POUT = av(SC + 16640, [128, 1024])
        PSTGC = av(SC + 20736, [128, 2, 256])
        PASTP = av(SC + 22784, [128, 2, 240])
        SNEW = av(SC + 24704, [128, 2, 64])
        TMP16 = av(SC + 25216, [128, 16])
        RAW = [av(SC, [128, 1028]), av(SC + 4112, [128, 1028])]
        ACC = av(SC + 8224, [128, 1024])
        CSTG = av(SC + 12320, [128, 1536])
        PASTT = av(SC + 18464, [128, 12, 48])
        SELC = av(SC + 20768, [128, 2, 48])
        BIG = av(SC, [128, 1024])
        WW = av(SC + 4096, [128, 1024], BF16)
        XDT = av(SC + 6144, [128, 1024], BF16)
        XDTD = av(SC + 8192, [128, 1024], BF16)
        XSD = av(SC + 10240, [128, 1024], BF16)
        YT = av(SC + 12288, [128, 1024])
        GN = av(SC + 16384, [128, 1024], BF16)
        BTOK = av(SC + 18432, [128, 256], BF16)
        SM = av(SC + 18944, [128, 2, 128])
        SML = av(SC + 19968, [128, 16, 16])
        H0X = av(SC + 20992, [128, 1024])
        CMS = av(SC + 25088, [128, 2, 2, 64], BF16)
        BMS = av(SC + 25600, [128, 2, 256], BF16)
        CDF = av(SC + 26624, [128, 8, 16])
        E16 = av(SC + 27136, [16, 1024])
        SETS = [
            (BIG, WW, XDT, XDTD, XSD, YT, GN, BTOK, SM, SML),
            (av(0, [128, 1024]), av(4096, [128, 1024], BF16), av(6144, [128, 1024], BF16), av(8192, [128, 1024], BF16),
             av(10240, [128, 1024], BF16), av(12288, [128, 1024]), av(SC + 20992, [128, 1024], BF16), av(SC + 23040, [128, 256], BF16),
             av(SC + 23552, [128, 2, 128]), av(SC + 24576, [128, 16, 16])),
        ]

        bank_cur = [0]

        held = set()

        def alloc(n=1, hold=False):
            c = bank_cur[0]
            for _ in range(32):
                if n > 1 and c % n:
                    c += n - c % n
                if c + n > 8:
                    c = 0
                if all((c + i) not in held for i in range(n)):
                    break
                c += 1
            else:
                raise RuntimeError("no free psum banks")
            bank_cur[0] = (c + n) % 8
            if hold:
                for i in range(n):
                    held.add(c + i)
            return c

        def release(b, n=1):
            for i in range(n):
                held.discard(b + i)

        def pk(b, n=1):
            return [("ps", b + i) for i in range(n)]

        ring_cur = [0]
        dma_flip = [0]

        def P(l, off, n=1):
            return PRM[:, l * PL + off: l * PL + off + n]

        def mm(out, pairs):
            def fn(e):
                n = len(pairs)
                for i, (a, b) in enumerate(pairs):
                    m = e.matmul(out, a, b, start=(i == 0), stop=(i == n - 1))
                return m
            return fn

        def mms(groups):
            def fn(e):
                for out, pairs in groups:
                    n = len(pairs)
                    for i, (a, b) in enumerate(pairs):
                        m = e.matmul(out, a, b, start=(i == 0), stop=(i == n - 1))
                return m
            return fn

        def act(out, in_, func, **kw):
            return lambda e: e.activation(out=out, in_=in_, func=func, **kw)

        def tt(out, a, b, op):
            return lambda e: e.tensor_tensor(out=out, in0=a, in1=b, op=op)

        def ts(out, a, s1, s2, op0, op1):
            return lambda e: e.tensor_scalar(out=out, in0=a, scalar1=s1, scalar2=s2, op0=op0, op1=op1)

        def stt(out, a, s, b, op0, op1):
            return lambda e: e.scalar_tensor_tensor(out=out, in0=a, scalar=s, in1=b, op0=op0, op1=op1)

        def cp(out, in_):
            return lambda e: e.tensor_copy(out, in_)

        def ms(ap, v):
            return lambda e: e.memset(ap, v)

        def dma(out, in_):
            return lambda e: e.dma_start(out=out, in_=in_)

        def xk(j, t):
            return ("x", j, t)

        S.add("sp", dma(PRM[:], prm_d[:, :]), writes=[("prm",)], chan="ld_prm")
        S.add("sp", dma(CST[:], cst_d[:, :]), writes=[("cst",)], chan="ld_cst")
        S.add("dve", cp(IDB[:], IDF), reads=[("cst",)], writes=[("idb",)])
        S.add("dve", ms(ONESB[:], 1.0 / 1024.0), writes=[("onesb",)])
        S.add("dve", cp(STRB[:], CST[:, 464:592]), reads=[("cst",)], writes=[("strb",)])
        S.add("dve", cp(STRSEGB[0:64, :], CST[0:64, 592:656]), reads=[("cst",)], writes=[("strsegb",)])
        S.add("dve", ms(ONESF[:], 1.0), writes=[("onesf",)])
        S.add("dve", ms(EPS[:], EPSV), writes=[("eps",)])
        for l in range(L):
            S.add("act", act(A_B[:, l, :], P(l, 108, 16), AF.Exp), reads=[("prm",)], writes=[("ab0", l)])
            S.add("dve", ts(A_B[:, l, :], A_B[:, l, :], -1.0, 0.0, ALU.mult, ALU.add), reads=[("ab0", l)], writes=[("ab",)])
            S.add("act", act(ACOL[:, l:l + 1], PRM[0:16, l * PL + 141: l * PL + 142], AF.Exp), reads=[("prm",)], writes=[("ac0", l)])
            S.add("dve", ts(ACOL[:, l:l + 1], ACOL[:, l:l + 1], -1.0, 0.0, ALU.mult, ALU.add), reads=[("ac0", l)], writes=[("acol",)])
        CK = [("cst",), ("idb",), ("onesb",), ("onesf",), ("eps",), ("ab",), ("acol",), ("prm",), ("strb",), ("strsegb",)]

        for blk in range(17):
            r0 = blk * 128
            n = 128 if blk < 16 else 64
            t = blk // 4
            sl = blk % 2
            S.add("sp", dma(XSTG[0:n, sl, :], xin[r0:r0 + n, :]), writes=[("A", "xstg", sl)], chan="ld_x%d" % sl)
            for hb in range(2):
                b = alloc()
                S.add("pe", mms([(PS[:, b, jj * 128: jj * 128 + n],
                                  [(XSTG[0:n, sl, (hb * 4 + jj) * 128:(hb * 4 + jj + 1) * 128], IDF[0:n, 0:n])]) for jj in range(4)]),
                      reads=[("A", "xstg", sl)] + CK, writes=pk(b))
                src = PS[:, b, :].rearrange("p (j q) -> p j q", j=4)[:, :, 0:n]
                dst = X[:, hb * 4:hb * 4 + 4, r0:r0 + n]
                if (blk + hb) % 2 == 0:
                    S.add("act", act(dst, src, AF.Copy), reads=pk(b), writes=[xk(hb * 4 + jj, t) for jj in range(4)])
                else:
                    S.add("dve", cp(dst, src), reads=pk(b), writes=[xk(hb * 4 + jj, t) for jj in range(4)])
        S.fence()

        def norm(tiles, wl, woff, dst_fn, dkey_fn):
            for t in tiles:
                t0, n = TILES[t]
                S.add("act", act(SQ[:, :, 0:n], X[:, :, t0:t0 + n], AF.Square),
                      reads=[xk(j, t) for j in range(8)], writes=[("A", "sq")])
                b = alloc()
                S.add("pe", mm(PS[:, b, 0:n], [(ONESB[:], SQ[:, k, 0:n]) for k in range(8)]),
                      reads=[("A", "sq")] + CK, writes=pk(b))
                S.add("act", act(RS[:, 0:n], PS[:, b, 0:n], AF.Ln, bias=EPS[:, 0:1]), reads=pk(b) + CK, writes=[("A", "rs")])
                S.add("act", act(RS[:, 0:n], RS[:, 0:n], AF.Exp, scale=-0.5), reads=[("A", "rs")], writes=[("A", "rs")])
                for k in range(8):
                    wcol = PRM[:, wl * PL + woff + k: wl * PL + woff + k + 1] if wl is not None else PRM[:, L * PL + k: L * PL + k + 1]
                    S.add("dve", stt(dst_fn(k, t, n), X[:, k, t0:t0 + n], wcol, RS[:, 0:n], ALU.mult, ALU.mult),
                          reads=[xk(k, t), ("A", "rs")] + CK, writes=[dkey_fn(k, t)])

        def u_full(k, t, n):
            return U[:, k, TILES[t][0]:TILES[t][0] + n]

        def ukey(k, t):
            return ("A", "u", k, t)

        def ring_next():
            s = ring_cur[0]
            ring_cur[0] = 1 - s
            return s

        def wload(slot, part, view, src):
            S.add("pool", dma(view, src), writes=[("ring", slot, part)], chan="r%d%s" % (slot, part))

        def ffn(l, which):
            S.fence()
            norm(range(5), l, 0 if which == 0 else 16, u_full, ukey)
            NG = DFF // 256
            gd, ud, dd = fg[which], fu[which], fd[which]
            slots = {}

            def load(g):
                s = ring_next()
                slots[g] = s
                wload(s, "gu0", RING[:, s, 0:2048].rearrange("p (k n) -> p k n", k=8),
                      gd[l, :, g * 256:(g + 1) * 256].rearrange("(k p) n -> p k n", p=128))
                wload(s, "gu1", RING[:, s, 2048:4096].rearrange("p (k n) -> p k n", k=8),
                      ud[l, :, g * 256:(g + 1) * 256].rearrange("(k p) n -> p k n", p=128))
                wload(s, "d", RING[:, s, 4096:6144].rearrange("p (f n) -> p f n", f=2),
                      dd[l, g * 256:(g + 1) * 256, :].rearrange("(f p) n -> p f n", p=128))

            def gu(g):
                s = slots[g]
                hs = g % 2
                wg = RING[:, s, 0:2048].rearrange("p (k n) -> p k n", k=8)
                wu = RING[:, s, 2048:4096].rearrange("p (k n) -> p k n", k=8)
                for f in range(2):
                    for t in range(5):
                        t0, n = TILES[t]
                        b = alloc(2)
                        S.add("pe", mms([(PS[:, b, 0:n], [(wg[:, k, f * 128:(f + 1) * 128], U[:, k, t0:t0 + n]) for k in range(8)]),
                                         (PS[:, b + 1, 0:n], [(wu[:, k, f * 128:(f + 1) * 128], U[:, k, t0:t0 + n]) for k in range(8)])]),
                              reads=[("ring", s, "gu0"), ("ring", s, "gu1")] + [ukey(k, t) for k in range(8)], writes=pk(b, 2))
                        sgs = (f * 5 + t) % 2
                        S.add("act", act(SG[:, sgs, 0:n], PS[:, b, 0:n], AF.Silu), reads=pk(b), writes=[("A", "sg", sgs)])
                        S.add("dve", tt(HH[:, hs, f, t0:t0 + n], SG[:, sgs, 0:n], PS[:, b + 1, 0:n], ALU.mult),
                              reads=[("A", "sg", sgs)] + pk(b + 1), writes=[("A", "h", hs, f, t)])

            def down(g):
                s = slots[g]
                hs = g % 2
                wd = RING[:, s, 4096:6144].rearrange("p (f n) -> p f n", f=2)
                for j in range(8):
                    for t in range(5):
                        t0, n = TILES[t]
                        b = alloc()
                        S.add("pe", mm(PS[:, b, 0:n], [(wd[:, f, j * 128:(j + 1) * 128], HH[:, hs, f, t0:t0 + n]) for f in range(2)]),
                              reads=[("ring", s, "d")] + [("A", "h", hs, f, t) for f in range(2)], writes=pk(b))
                        S.add("dve", stt(X[:, j, t0:t0 + n], PS[:, b, 0:n], 0.5, X[:, j, t0:t0 + n], ALU.mult, ALU.add),
                              reads=pk(b) + [xk(j, t)], writes=[xk(j, t)])

            load(0)
            load(1)
            gu(0)
            for g in range(NG):
                if g + 1 < NG:
                    gu(g + 1)
                down(g)
                if g + 2 < NG:
                    load(g + 2)

        def colblock_load(s, src2d, col0, ncols, kc, part="gu0", off=0):
            view = RING[:, s, off:off + kc * ncols].rearrange("p (k n) -> p k n", k=kc)
            wload(s, part, view, src2d[:, col0:col0 + ncols].rearrange("(k p) n -> p k n", p=128))
            return s, view

        def mixer(l, seg):
            kind, sidx, s0, ns, tiles = seg
            smp = kind == "S"
            nseq, ntok = (16, 4) if smp else (1, 1024)
            ltile = [(TILES[t][0] - s0, TILES[t][1], t) for t in tiles]

            def us(k, t, n):
                return US[:, k, TILES[t][0] - s0: TILES[t][0] - s0 + n]

            def uskey(k, t):
                return ("A", "us", k, t)

            S.fence()
            norm(tiles, l, 8, us, uskey)
            ckpt()
            S.fence()
            if smp:
                S.add("sp", dma(pool_s_o[l].rearrange("(s r) f -> s r f", r=15)[:, 0:11, :],
                                spool[l].rearrange("(s r) f -> s r f", r=15)[:, 4:15, :]), chan="st_pool_d2d")
            for cb in range(4):
                wwin = 2 << cb
                s = ring_next()
                _, wv = colblock_load(s, w_in[l], 2576 + cb * 256, 256, 8, "gu0", 0)
                wload(s, "gu1", RING[:, s, 2048:2560].rearrange("p (k n) -> p k n", k=2),
                      pool_w[l, cb].rearrange("(k p) n -> p k n", p=128))
                pwv = RING[:, s, 2048:2560].rearrange("p (k n) -> p k n", k=2)
                for m in range(2):
                    c = cb * 2 + m
                    A = PA[m]
                    A3 = A[:, 0:nseq * (16 + ntok)].rearrange("p (s q) -> p s q", s=nseq)
                    akey = ("A", "pa", m)
                    if smp:
                        S.add("sp", dma(PSTGC[0:120, :, 0:128],
                                        spool[l, :, c * 128:(c + 1) * 128].rearrange("(a p) f -> p a f", p=120)),
                              writes=[("A", "pstgc")], chan="ld_sp")
                        b = alloc()
                        S.add("pe", mms([(PS[:, b, a * 120:(a + 1) * 120], [(PSTGC[0:120, a, 0:128], IDF[0:120, 0:120])]) for a in range(2)]),
                              reads=[("A", "pstgc")] + CK, writes=pk(b))
                        S.add("dve", cp(A3[:, :, 1:16], PS[:, b, 0:240].rearrange("p (s r) -> p s r", r=15)), reads=pk(b), writes=[akey])
                    elif sidx == 0:
                        S.add("dve", ms(A[:, 0:16], 0.0), writes=[akey])
                    else:
                        S.add("dve", cp(A[:, 1:16], HISTP[:, c, :]), reads=[("histp", c)], writes=[akey])
                    for (lc, n, t) in ltile:
                        b = alloc()
                        S.add("pe", mm(PS[:, b, 0:n], [(wv[:, k, m * 128:(m + 1) * 128], us(k, t, n)) for k in range(8)]),
                              reads=[("ring", s, "gu0")] + [uskey(k, t) for k in range(8)], writes=pk(b))
                        if smp:
                            S.add("act", act(A3[:, :, 16:20], PS[:, b, 0:64].rearrange("p (s q) -> p s q", q=4), AF.Copy), reads=pk(b), writes=[akey])
                            S.add("act", act(SNEW[:, m, :], PS[:, b, 0:64], AF.Copy), reads=pk(b), writes=[("A", "snew", m)])
                        else:
                            S.add("act", act(A[:, 16 + lc:16 + lc + n], PS[:, b, 0:n], AF.Copy), reads=pk(b), writes=[akey])
                    LL = 16 + ntok
                    if smp:
                        if m == 0 and cb % 2 == 0:
                            pob = alloc(1, hold=True)
                        S.add("pe", mm(PS[0:64, pob, (c % 4) * 128:(c % 4 + 1) * 128], [(SNEW[:, m, :], IDF)]),
                              reads=[("A", "snew", m)] + CK, writes=pk(pob))
                    elif sidx == 0:
                        S.add("dve", cp(HISTP[:, c, :], A[:, LL - 15:LL]), reads=[akey], writes=[("histp", c)])
                    else:
                        if m == 0 and cb % 2 == 0:
                            pob = alloc(1, hold=True)
                        S.add("pe", mm(PS[0:15, pob, (c % 4) * 128:(c % 4 + 1) * 128], [(A[:, LL - 15:LL], IDF)]),
                              reads=[akey] + CK, writes=pk(pob))
                    if (smp or sidx == 1) and c % 4 == 3:
                        nr = 64 if smp else 15
                        hh = c // 4
                        S.add("act", act(POUT[0:nr, hh * 512:(hh + 1) * 512], PS[0:nr, pob, :], AF.Copy), reads=pk(pob), writes=[("A", "pout", hh)])
                        release(pob)
                        if c == 7:
                            if smp:
                                for sq in range(16):
                                    S.add("sp", dma(pool_s_o[l, sq * 15 + 11: sq * 15 + 15, :], POUT[sq * 4: sq * 4 + 4, :]),
                                          reads=[("A", "pout", 0), ("A", "pout", 1)], chan="st_pool%d" % (sq % 4))
                            else:
                                S.add("sp", dma(pool_p_o[l], POUT[0:15, :]), reads=[("A", "pout", 0), ("A", "pout", 1)], chan="st_pool0")
                    cur, curkey = A3, akey
                    bufs = [(PP1, ("A", "pp1")), (PP2, ("A", "pp2"))]
                    sh = 1
                    lvl = 0
                    while sh < wwin:
                        lo = 2 * sh - 1
                        nb_, nk = bufs[lvl % 2]
                        nb3 = nb_[:, 0:nseq * LL].rearrange("p (s q) -> p s q", s=nseq)
                        S.add("dve", tt(nb3[:, :, lo:LL], cur[:, :, lo:LL], cur[:, :, lo - sh:LL - sh], ALU.add), reads=[curkey], writes=[nk])
                        cur, curkey = nb3, nk
                        sh *= 2
                        lvl += 1
                    dst = MIX[:, c, 0:ns].rearrange("p (s q) -> p s q", s=nseq)
                    mkeys = [("A", "mix", c, t) for (_, _, t) in ltile]
                    S.add("dve", stt(dst, cur[:, :, 16:LL], 1.0 / wwin, A3[:, :, 16:LL], ALU.mult, ALU.subtract),
                          reads=[curkey, akey], writes=mkeys)
                    if (not smp) and sidx == 0:
                        S.add("dve", tt(TMP16[:, :], cur[:, 0, 16:32], RC[:, cb * 16:(cb + 1) * 16], ALU.mult), reads=[curkey] + CK, writes=[("A", "tmp16")])
                        S.add("dve", tt(MIX[:, c, 0:16], TMP16[:, :], A[:, 16:32], ALU.subtract), reads=[("A", "tmp16"), akey], writes=mkeys[0:1])
                for (lc, n, t) in ltile:
                    b = alloc(2)
                    S.add("pe", mms([(PS[:, b + m, 0:n], [(pwv[:, k, m * 128:(m + 1) * 128], MIX[:, cb * 2 + k, lc:lc + n]) for k in range(2)]) for m in range(2)]),
                          reads=[("ring", s, "gu1")] + [("A", "mix", cb * 2 + k, t) for k in range(2)], writes=pk(b, 2))
                    for m in range(2):
                        c = cb * 2 + m
                        S.add("act", act(MIX[:, c, lc:lc + n], PS[:, b + m, 0:n], AF.Copy, scale=P(l, 40 + c)),
                              reads=[("ps", b + m)] + CK, writes=[("A", "mix", c, t)])
            for jb in range(4):
                s = ring_next()
                _, wv = colblock_load(s, w_out[l, 1024:2048], jb * 256, 256, 8, "gu0", 0)
                for m in range(2):
                    j = jb * 2 + m
                    for (lc, n, t) in ltile:
                        b = alloc()
                        S.add("pe", mm(PS[:, b, 0:n], [(wv[:, k, m * 128:(m + 1) * 128], MIX[:, k, lc:lc + n]) for k in range(8)]),
                              reads=[("ring", s, "gu0")] + [("A", "mix", k, t) for k in range(8)], writes=pk(b))
                        g0 = TILES[t][0]
                        S.add("dve", tt(X[:, j, g0:g0 + n], X[:, j, g0:g0 + n], PS[:, b, 0:n], ALU.add), reads=pk(b) + [xk(j, t)], writes=[xk(j, t)])
            ckpt()
            S.fence()
            if smp:
                S.add("sp", dma(CSTG[0:48, :], sconv[l]), writes=[("A", "cstg")], chan="ld_sc")
                for q4 in range(3):
                    b = alloc()
                    S.add("pe", mms([(PS[:, b, jj * 48:(jj + 1) * 48], [(CSTG[0:48, (q4 * 4 + jj) * 128:(q4 * 4 + jj + 1) * 128], IDF[0:48, 0:48])]) for jj in range(4)]),
                          reads=[("A", "cstg")] + CK, writes=pk(b))
                    S.add("dve", cp(PASTT[:, q4 * 4:q4 * 4 + 4, :], PS[:, b, 0:192].rearrange("p (j q) -> p j q", j=4)), reads=pk(b), writes=[("A", "pastt", q4)])
            LR = 3 + ntok
            for cb in range(6):
                s = ring_next()
                _, wv = colblock_load(s, w_in[l], 1024 + cb * 256, 256, 8, "gu0", 0)
                for m in range(2):
                    c = cb * 2 + m
                    R = RAW[m]
                    R3 = R[:, 0:nseq * LR].rearrange("p (s q) -> p s q", s=nseq)
                    rkey = ("A", "raw", m)
                    if smp:
                        S.add("dve", cp(R3[:, :, 0:3], PASTT[:, c, :].rearrange("p (s r) -> p s r", r=3)), reads=[("A", "pastt", c // 4)], writes=[rkey])
                    elif sidx == 0:
                        S.add("dve", ms(R[:, 0:3], 0.0), writes=[rkey])
                    else:
                        S.add("dve", cp(R[:, 0:3], HISTC[:, c, :]), reads=[("histc", c)], writes=[rkey])
                    for (lc, n, t) in ltile:
                        b = alloc()
                        S.add("pe", mm(PS[:, b, 0:n], [(wv[:, k, m * 128:(m + 1) * 128], us(k, t, n)) for k in range(8)]),
                              reads=[("ring", s, "gu0")] + [uskey(k, t) for k in range(8)], writes=pk(b))
                        if smp:
                            S.add("act", act(R3[:, :, 3:7], PS[:, b, 0:64].rearrange("p (s q) -> p s q", q=4), AF.Copy), reads=pk(b), writes=[rkey])
                        else:
                            S.add("act", act(R[:, 3 + lc:3 + lc + n], PS[:, b, 0:n], AF.Copy), reads=pk(b), writes=[rkey])
                    if smp:
                        S.add("dve", cp(SELC[:, m, :].rearrange("p (s r) -> p s r", r=3), R3[:, :, 4:7]), reads=[rkey], writes=[("A", "selc", m)])
                        if c % 4 == 0:
                            cob = alloc(1, hold=True)
                        S.add("pe", mm(PS[0:48, cob, (c % 4) * 128:(c % 4 + 1) * 128], [(SELC[:, m, :], IDF)]), reads=[("A", "selc", m)] + CK, writes=pk(cob))
                    elif sidx == 0:
                        S.add("dve", cp(HISTC[:, c, :], R[:, LR - 3:LR]), reads=[rkey], writes=[("histc", c)])
                    else:
                        if c % 4 == 0:
                            cob = alloc(1, hold=True)
                        S.add("pe", mm(PS[0:3, cob, (c % 4) * 128:(c % 4 + 1) * 128], [(R[:, LR - 3:LR], IDF)]), reads=[rkey] + CK, writes=pk(cob))
                    if (smp or sidx == 1) and c % 4 == 3:
                        nr = 48 if smp else 3
                        q4 = c // 4
                        S.add("act", act(CSTG[0:nr, q4 * 512:(q4 + 1) * 512], PS[0:nr, cob, :], AF.Copy), reads=pk(cob), writes=[("A", "cstg")])
                        release(cob)
                        if c == 11:
                            S.add("sp", dma((conv_s_o if smp else conv_p_o)[l], CSTG[0:nr, :]), reads=[("A", "cstg")], chan="st_conv")
                    A3 = ACC[:, 0:ns].rearrange("p (s q) -> p s q", s=nseq)
                    S.add("dve", ts(A3, R3[:, :, 0:ntok], P(l, 48 + c * 4), P(l, 96 + c), ALU.mult, ALU.add), reads=[rkey] + CK, writes=[("A", "acc")])
                    for jt in range(1, 4):
                        S.add("dve", stt(A3, R3[:, :, jt:jt + ntok], P(l, 48 + c * 4 + jt), A3, ALU.mult, ALU.add), reads=[rkey, ("A", "acc")] + CK, writes=[("A", "acc")])
                    S.add("act", act(MIX[:, c, 0:ns], ACC[:, 0:ns], AF.Silu), reads=[("A", "acc")], writes=[("A", "mix", c, t) for (_, _, t) in ltile])
            s = ring_next()
            _, wv = colblock_load(s, w_in[l], 2560, 16, 8, "gu0", 0)
            for (lc, n, t) in ltile:
                b = alloc()
                S.add("pe", mm(PS[0:16, b, 0:n], [(wv[:, k, :], us(k, t, n)) for k in range(8)]),
                      reads=[("ring", s, "gu0")] + [uskey(k, t) for k in range(8)], writes=pk(b))
                S.add("act", act(DTT[:, lc:lc + n], PS[0:16, b, 0:n], AF.Exp, bias=PRM[0:16, l * PL + 140:l * PL + 141]), reads=pk(b) + CK, writes=[("A", "dtt", t)])
                S.add("act", act(DTT[:, lc:lc + n], DTT[:, lc:lc + n], AF.Ln, bias=1.0), reads=[("A", "dtt", t)], writes=[("A", "dtt", t)])
            for cb in range(4):
                s = ring_next()
                _, wv = colblock_load(s, w_in[l], cb * 256, 256, 8, "gu0", 0)
                for m in range(2):
                    c = cb * 2 + m
                    for (lc, n, t) in ltile:
                        b = alloc()
                        S.add("pe", mm(PS[:, b, 0:n], [(wv[:, k, m * 128:(m + 1) * 128], us(k, t, n)) for k in range(8)]),
                              reads=[("ring", s, "gu0")] + [uskey(k, t) for k in range(8)], writes=pk(b))
                        S.add("act", act(ZB[:, c, lc:lc + n], PS[:, b, 0:n], AF.Silu), reads=pk(b), writes=[("A", "zb", c, t)])
            ckpt()
            S.fence()
            AB = A_B[:, l, :]
            DB = P(l, 124, 16)
            if (not smp) and sidx == 0:
                S.add("dve", ms(HST[:], 0.0), writes=[("hst",)])
                S.add("dve", ms(HB[:], 0.0), writes=[("hb",)])
            if smp:
                S.add("sp", dma(E16[:, :], e16_d[:, :]), writes=[("A", "e16")], chan="ld_e16")
            nq = 64 if smp else 128
            nch = 1 if smp else min(8, int(os.environ.get("MK_NCH", 8)))
            TR = TRISEG[0:64, 0:64] if smp else TRI
            def chunk_gen(ci):
                q0 = ci * 128
                par = 0 if smp else ci % 2
                sfx = str(par)
                BIG, WW, XDT, XDTD, XSD, YT, GN, BTOK, SM, SML = SETS[par]
                DTTOK, DA, CUMCOL, EXPA, CDB, DOUT, SSQ, RSTD, TOTT = (SML[:, i, :] for i in range(9))
                tg = tiles[q0 // 512] if not smp else tiles[0]
                mk = lambda c: ("A", "mix", c, tg)
                bx = alloc(2)
                S.add("pe", mms([(PS[0:nq, bx + jj // 4, (jj % 4) * 128:(jj % 4 + 1) * 128], [(MIX[:, jj, q0:q0 + nq], IDB[:])]) for jj in range(8)]),
                      reads=[mk(c) for c in range(8)] + CK, writes=pk(bx, 2))
                bb = alloc()
                S.add("pe", mms([(PS[0:nq, bb, g * 128:(g + 1) * 128], [(MIX[:, 8 + g, q0:q0 + nq], IDB[:])]) for g in range(2)]
                                + [(PS[0:nq, bb, 256:272], [(DTT[0:16, q0:q0 + nq], IDF[0:16, 0:16])])]),
                      reads=[mk(8), mk(9), ("A", "dtt", tg)] + CK, writes=pk(bb))
                S.add("dve", cp(DTTOK[0:nq, :], PS[0:nq, bb, 256:272]), reads=pk(bb), writes=[("A", "dttok" + sfx)])
                S.add("dve", tt(DA[0:nq, :], PS[0:nq, bb, 256:272], AB[0:nq, :], ALU.mult), reads=pk(bb) + CK, writes=[("A", "da" + sfx)])
                S.add("act", act(BTOK[0:nq, :], PS[0:nq, bb, 0:256], AF.Copy), reads=pk(bb), writes=[("A", "btok" + sfx)])
                sub(1)
                bc = alloc()
                grp = [(PS[0:nq, bc, 0:16], [(TR[0:nq, 0:nq], DA[0:nq, :])])]
                if smp:
                    grp.append((PS[0:nq, bc, 16:32], [(BLK[0:64, 0:64], DA[0:nq, :])]))
                else:
                    grp.append((PS[0:nq, bc, 32:48], [(ONESF[0:nq, 0:nq], DA[0:nq, :])]))
                S.add("pe", mms(grp), reads=[("A", "da" + sfx)] + CK, writes=pk(bc))
                S.add("dve", cp(CUMCOL[0:nq, :], PS[0:nq, bc, 0:16]), reads=pk(bc), writes=[("A", "cumcol" + sfx)])
                S.add("act", act(EXPA[0:nq, :], PS[0:nq, bc, 0:16], AF.Exp), reads=pk(bc), writes=[("A", "expa" + sfx)])
                if not smp:
                    S.add("act", act(CDB[:, :], PS[:, bc, 32:48], AF.Exp), reads=pk(bc), writes=[("A", "cdb" + sfx, 0), ("A", "cdb" + sfx, 1)])
                if smp:
                    S.add("dve", tt(DOUT[0:nq, :], PS[0:nq, bc, 16:32], CUMCOL[0:nq, :], ALU.subtract), reads=pk(bc) + [("A", "cumcol" + sfx)], writes=[("A", "dout" + sfx)])
                    S.add("act", act(DOUT[0:nq, :], DOUT[0:nq, :], AF.Exp), reads=[("A", "dout" + sfx)], writes=[("A", "dout" + sfx)])
                sub(2)
                xs3 = PS[0:nq, bx:bx + 2, :].rearrange("p a (r q) -> p (a r) q", q=64)
                S.add("dve", tt(XDT[0:nq, :].rearrange("p (r q) -> p r q", q=64), xs3, DTTOK[0:nq, :].unsqueeze(2).broadcast_to([nq, 16, 64]), ALU.mult),
                      reads=pk(bx, 2) + [("A", "dttok" + sfx)], writes=[("A", "xdt" + sfx)])
                S.add("dve", tt(XSD[0:nq, :].rearrange("p (r q) -> p r q", q=64), xs3, DB[0:nq, :].unsqueeze(2).broadcast_to([nq, 16, 64]), ALU.mult),
                      reads=pk(bx, 2) + CK, writes=[("A", "xsd" + sfx)])
                sub(3)
                bs = alloc()
                S.add("pe", mms([(PS[0:nq, bs, g * 128:g * 128 + nq], [(MIX[:, 8 + g, q0:q0 + nq], MIX[:, 10 + g, q0:q0 + nq])]) for g in range(2)]),
                      reads=[mk(8), mk(9), mk(10), mk(11)], writes=pk(bs))
                S.add("dve", tt(SM[0:nq, :, 0:nq], PS[0:nq, bs, 0:256].rearrange("p (g q) -> p g q", g=2)[:, :, 0:nq],
                                TR[0:nq, 0:nq].unsqueeze(1).broadcast_to([nq, 2, nq]), ALU.mult), reads=pk(bs) + CK, writes=[("A", "sm" + sfx)])
                sub(4)
                DAHL = SML[0:nq, 14, :].bitcast(BF16)
                S.add("dve", cp(DAHL[:, 0:16], DA[0:nq, :]), reads=[("A", "da" + sfx)], writes=[("A", "dahl" + sfx)])
                S.add("dve", tt(DAHL[:, 16:32], DA[0:nq, :], DAHL[:, 0:16], ALU.subtract), reads=[("A", "da" + sfx), ("A", "dahl" + sfx)], writes=[("A", "dahl" + sfx)])
                nbk = (8 * nq) // 512
                SLB = STRSEGB[0:64, 0:64] if smp else STRB[:, :]
                grp_state = []
                for g in range(2):
                    BIGg, WWg = SETS[g][0], SETS[g][1]
                    gs = "g%d" % g
                    B3 = BIGg[0:nq, 0:8 * nq].rearrange("p (r q) -> p r q", q=nq)
                    W3 = WWg[0:nq, 0:8 * nq].rearrange("p (r q) -> p r q", q=nq)
                    BH = BIGg[0:nq, 0:512].bitcast(BF16)
                    BL = BIGg[0:nq, 512:1024].bitcast(BF16)
                    for hl, BX in ((0, BH), (1, BL)):
                        S.add("pool", tt(BX[:, 0:8 * nq].rearrange("p (r q) -> p r q", q=nq), TR[0:nq, 0:nq].unsqueeze(1).broadcast_to([nq, 8, nq]),
                                         DAHL[:, hl * 16 + 8 * g: hl * 16 + 8 * g + 8].unsqueeze(2).broadcast_to([nq, 8, nq]), ALU.mult),
                              reads=[("A", "dahl" + sfx)] + CK, writes=[("A", "big" + gs)])
                    br = alloc(nbk)
                    S.add("pe", mms([(PS[0:nq, br + i, :], [(SLB[0:nq, 0:nq], BH[:, i * 512:(i + 1) * 512]), (SLB[0:nq, 0:nq], BL[:, i * 512:(i + 1) * 512])])
                                     for i in range(nbk)]),
                          reads=[("A", "big" + gs)] + CK, writes=pk(br, nbk))
                    if nbk == 2:
                        cr3 = PS[0:nq, br:br + 2, :].rearrange("p a (r q) -> p (a r) q", q=nq)
                    else:
                        cr3 = PS[0:nq, br, :].rearrange("p (r q) -> p r q", q=nq)
                    grp_state.append((g, gs, B3, W3, br, cr3))
                for (g, gs, B3, W3, br, cr3) in grp_state:
                    S.add("act", act(B3, cr3, AF.Exp), reads=pk(br, nbk), writes=[("A", "big" + gs)])
                    if not smp:
                        S.add("dve", cp(DOUT[0:nq, 8 * g:8 * g + 8], B3[:, :, nq - 1]), reads=[("A", "big" + gs)], writes=[("A", "dout" + sfx)])
                    S.add("dve", tt(W3, B3, SM[0:nq, g, 0:nq].unsqueeze(1).broadcast_to([nq, 8, nq]), ALU.mult),
                          reads=[("A", "big" + gs), ("A", "sm" + sfx)], writes=[("A", "ww" + gs)])
                by = alloc(2, hold=True)
                for (g, gs, B3, W3, br, cr3) in grp_state:
                    def ydiag(g=g, W3=W3, by=by, nq=nq, XSD=XSD, XDT=XDT):
                        def fn(e):
                            e.matmul(PS[0:nq, by + g, :], IDB[0:nq, 0:nq], XSD[0:nq, g * 512:(g + 1) * 512], start=True, stop=False)
                            for r in range(8):
                                m_ = e.matmul(PS[0:nq, by + g, r * 64:(r + 1) * 64], W3[:, r, :],
                                              XDT[0:nq, (8 * g + r) * 64:(8 * g + r + 1) * 64], start=False, stop=(r == 7))
                            return m_
                        return fn
                    S.add("pe", ydiag(), reads=[("A", "ww" + gs), ("A", "xdt" + sfx), ("A", "xsd" + sfx)] + CK, writes=pk(by + g))
                sub(5)
                S.add("pool", tt(XDTD[0:nq, :].rearrange("p (r q) -> p r q", q=64), XDT[0:nq, :].rearrange("p (r q) -> p r q", q=64),
                                DOUT[0:nq, :].unsqueeze(2).broadcast_to([nq, 16, 64]), ALU.mult), reads=[("A", "xdt" + sfx), ("A", "dout" + sfx)], writes=[("A", "xdtd" + sfx)])
                bo = alloc(2, hold=True)
                if not smp:
                    S.add("pe", mms([(PS[0:nq, bo + g, :], [(MIX[:, 10 + g, q0:q0 + nq], HB[:, g * 512:(g + 1) * 512])]) for g in range(2)]),
                          reads=[mk(10), mk(11), ("hb",)], writes=pk(bo, 2))
                else:
                    S.add("dve", ts(SML[0:16, 10:14, :].rearrange("p a b -> p (a b)"), DTT[0:16, 0:64], ACOL[:, l:l + 1], 0.0, ALU.mult, ALU.add),
                          reads=[("A", "dtt", tg)] + CK, writes=[("A", "dat")])
                    S.add("dve", lambda e: e.tensor_reduce(out=TOTT[0:16, :], in_=SML[0:16, 10:14, :].rearrange("p a b -> p (a b)").rearrange("p (s q) -> p s q", q=4),
                                                            axis=AX.X, op=ALU.add), reads=[("A", "dat")], writes=[("A", "tott")])
                    bcd = alloc()
                    S.add("pe", mms([(PS[:, bcd, j * 16:(j + 1) * 16], [(E16[0:16, j * 128:(j + 1) * 128], TOTT[0:16, :])]) for j in range(8)]),
                          reads=[("A", "e16"), ("A", "tott")], writes=pk(bcd))
                    S.add("act", act(CDF[:, :, :], PS[:, bcd, 0:128].rearrange("p (j s) -> p j s", j=8), AF.Exp), reads=pk(bcd), writes=[("A", "cdf")])
                    hbufs = [(HST, ("hst",)), (H0X, ("A", "h0x")), (av(8192, [128, 1024]), ("A", "h0y")), (av(12288, [128, 1024]), ("A", "h0z"))]

                    def hload(sq_):
                        H0_, hkey_ = hbufs[sq_ % 4]
                        S.add("sp", dma(H0_[:, :].rearrange("p (j n) -> p j n", j=8), sssm[l, sq_].rearrange("(j p) n -> p j n", p=128)),
                              writes=[hkey_], chan="ld_h%d" % (sq_ % 4))
                    hload(0)
                    hload(1)
                    for sq in range(16):
                        H0, hkey = hbufs[sq % 4]
                        H03 = H0[:, :].rearrange("p (j n) -> p j n", j=8)
                        if sq + 2 < 16:
                            hload(sq + 2)
                        bt = alloc(2)
                        S.add("pe", mms([(PS[:, bt + j // 4, (j % 4) * 128:(j % 4 + 1) * 128], [(H03[:, j, :], IDF)]) for j in range(8)]),
                              reads=[hkey] + CK, writes=pk(bt, 2))
                        S.add("act", act(HB[:, :].rearrange("p (a b) -> p a b", a=2), PS[:, bt:bt + 2, :], AF.Copy), reads=pk(bt, 2), writes=[("hb",)])
                        cs_ = sq % 2
                        S.add("dve", ms(CMS[:, cs_, :, :], 0.0), writes=[("A", "cms", cs_)])
                        S.add("dve", cp(CMS[:, cs_, :, 4 * sq:4 * sq + 4], MIX[:, 10:12, 4 * sq:4 * sq + 4]), reads=[mk(10), mk(11)], writes=[("A", "cms", cs_)])
                        S.add("pe", (lambda sq=sq, cs_=cs_, bo=bo: (lambda e: [e.matmul(PS[0:64, bo + g, :], CMS[:, cs_, g, :], HB[:, g * 512:(g + 1) * 512],
                                                                                    start=(sq == 0), stop=(sq == 15)) for g in range(2)][-1]))(),
                              reads=[("A", "cms", cs_), ("hb",)], writes=pk(bo, 2))
                        S.add("dve", ts(BMS[0:64, cs_, :], BTOK[0:64, :], MASKCOL[0:64, sq:sq + 1], 0.0, ALU.mult, ALU.add),
                              reads=[("A", "btok" + sfx)] + CK, writes=[("A", "bms", cs_)])
                        bn = alloc(2)
                        S.add("pe", mms([(PS[:, bn + j // 4, (j % 4) * 128:(j % 4 + 1) * 128],
                                          [(XDTD[0:64, j * 128:(j + 1) * 128], BMS[0:64, cs_, (j // 4) * 128:(j // 4 + 1) * 128])]) for j in range(8)]),
                              reads=[("A", "xdtd" + sfx), ("A", "bms", cs_)], writes=pk(bn, 2))
                        S.add("dve", tt(H03, H03, CDF[:, :, sq].unsqueeze(2).broadcast_to([128, 8, 128]), ALU.mult), reads=[hkey, ("A", "cdf")], writes=[hkey])
                        S.add("dve", tt(H03, H03, PS[:, bn:bn + 2, :].rearrange("p a (j n) -> p (a j) n", n=128), ALU.add), reads=[hkey] + pk(bn, 2), writes=[hkey])
                        S.add("sp", dma(ssm_s_o[l, sq].rearrange("(j p) n -> p j n", p=128), H03), reads=[hkey], chan="st_h%d" % (sq % 4))
                if not smp:
                    bsp = alloc(2)
                    S.add("pe", mms([(PS[:, bsp + g, :], [(BTOK[:, g * 128:(g + 1) * 128], XDTD[:, g * 512:(g + 1) * 512])]) for g in range(2)]),
                          reads=[("A", "btok" + sfx), ("A", "xdtd" + sfx)], writes=pk(bsp, 2))
                    S.add("pool", tt(HST[:, :].rearrange("p (r q) -> p r q", q=64), HST[:, :].rearrange("p (r q) -> p r q", q=64),
                                    CDB[:, :].unsqueeze(2).broadcast_to([128, 16, 64]), ALU.mult), reads=[("hst",), ("A", "cdb" + sfx, 0), ("A", "cdb" + sfx, 1)], writes=[("hst",)])
                    S.add("dve", tt(HST[:, :].rearrange("p (a b) -> p a b", a=2), HST[:, :].rearrange("p (a b) -> p a b", a=2), PS[:, bsp:bsp + 2, :], ALU.add),
                          reads=[("hst",)] + pk(bsp, 2), writes=[("hst",)])
                    S.add("act", act(HB[:], HST[:], AF.Copy), reads=[("hst",)], writes=[("hb",)])
                sub(6)
                S.add("dve", tt(YT[0:nq, :].rearrange("p (r q) -> p r q", q=64), PS[0:nq, bo:bo + 2, :].rearrange("p a (r q) -> p (a r) q", q=64),
                                EXPA[0:nq, :].unsqueeze(2).broadcast_to([nq, 16, 64]), ALU.mult), reads=pk(bo, 2) + [("A", "expa" + sfx)], writes=[("A", "yt" + sfx)])
                release(bo, 2)
                S.add("dve", tt(YT[0:nq, :].rearrange("p (a b) -> p a b", a=2), YT[0:nq, :].rearrange("p (a b) -> p a b", a=2), PS[0:nq, by:by + 2, :], ALU.add),
                      reads=pk(by, 2) + [("A", "yt" + sfx)], writes=[("A", "yt" + sfx)])
                release(by, 2)
                yield
                bz = alloc(2)
                S.add("pe", mms([(PS[0:nq, bz + jj // 4, (jj % 4) * 128:(jj % 4 + 1) * 128], [(ZB[:, jj, q0:q0 + nq], IDB[:])]) for jj in range(8)]),
                      reads=[("A", "zb", c, tg) for c in range(8)] + CK, writes=pk(bz, 2))
                S.add("dve", tt(YT[0:nq, :].rearrange("p (a b) -> p a b", a=2), YT[0:nq, :].rearrange("p (a b) -> p a b", a=2), PS[0:nq, bz:bz + 2, :], ALU.mult),
                      reads=pk(bz, 2) + [("A", "yt" + sfx)], writes=[("A", "yt" + sfx)])
                S.add("dve", ms(SSQ[0:nq, 0:1], 0.0), writes=[("A", "ssq" + sfx)])
                S.add("act", act(GN[0:nq, :], YT[0:nq, :], AF.Square, accum_out=SSQ[0:nq, 0:1]), reads=[("A", "yt" + sfx), ("A", "ssq" + sfx)], writes=[("A", "gn" + sfx), ("A", "ssq" + sfx)])
                S.add("act", act(RSTD[0:nq, 0:1], SSQ[0:nq, 0:1], AF.Ln, bias=EPS[0:nq, 0:1], scale=1.0 / 1024.0), reads=[("A", "ssq" + sfx)] + CK, writes=[("A", "rstd" + sfx)])
                S.add("act", act(RSTD[0:nq, 0:1], RSTD[0:nq, 0:1], AF.Exp, scale=-0.5), reads=[("A", "rstd" + sfx)], writes=[("A", "rstd" + sfx)])
                S.add("act", act(GN[0:nq, :], YT[0:nq, :], AF.Copy, scale=RSTD[0:nq, 0:1]), reads=[("A", "yt" + sfx), ("A", "rstd" + sfx)], writes=[("A", "gn" + sfx)])
                sub(7)
                bk = alloc(2)
                S.add("pe", mms([(PS[:, bk + jj // 4, (jj % 4) * 128:(jj % 4) * 128 + nq], [(GN[0:nq, jj * 128:(jj + 1) * 128], IDB[0:nq, 0:nq])]) for jj in range(8)]),
                      reads=[("A", "gn" + sfx)] + CK, writes=pk(bk, 2))
                for a in range(2):
                    S.add("dve", tt(MIX[:, 4 * a:4 * a + 4, q0:q0 + nq], PS[:, bk + a, :].rearrange("p (j q) -> p j q", j=4)[:, :, 0:nq],
                                    P(l, 32 + 4 * a, 4).unsqueeze(2).broadcast_to([128, 4, nq]), ALU.mult),
                          reads=pk(bk + a) + CK, writes=[mk(c) for c in range(4 * a, 4 * a + 4)])
            prev = None
            for ci in range(nch):
                gcur = chunk_gen(ci)
                next(gcur)
                if prev is not None:
                    for _ in prev:
                        pass
                prev = gcur
            for _ in prev:
                pass
            if (not smp) and sidx == 1:
                YT = SETS[0][5]
                sfx = "0"
                bf = alloc(2)
                S.add("pe", mms([(PS[:, bf + j // 4, (j % 4) * 128:(j % 4 + 1) * 128], [(HST[:, j * 128:(j + 1) * 128], IDF)]) for j in range(8)]),
                      reads=[("hst",)] + CK, writes=pk(bf, 2))
                S.add("act", act(YT[:, :].rearrange("p (a b) -> p a b", a=2), PS[:, bf:bf + 2, :], AF.Copy), reads=pk(bf, 2), writes=[("A", "yt" + sfx)])
                S.add("sp", dma(ssm_p_o[l].rearrange("(j p) n -> p j n", p=128), YT[:, :].rearrange("p (j n) -> p j n", j=8)), reads=[("A", "yt" + sfx)], chan="st_ssmp")
            ckpt()
            for jb in range(4):
                s = ring_next()
                _, wv = colblock_load(s, w_out[l, 0:1024], jb * 256, 256, 8, "gu0", 0)
                for m in range(2):
                    j = jb * 2 + m
                    for (lc, n, t) in ltile:
                        b = alloc()
                        S.add("pe", mm(PS[:, b, 0:n], [(wv[:, k, m * 128:(m + 1) * 128], MIX[:, k, lc:lc + n]) for k in range(8)]),
                              reads=[("ring", s, "gu0")] + [("A", "mix", k, t) for k in range(8)], writes=pk(b))
                        g0 = TILES[t][0]
                        S.add("dve", tt(X[:, j, g0:g0 + n], X[:, j, g0:g0 + n], PS[:, b, 0:n], ALU.add), reads=pk(b) + [xk(j, t)], writes=[xk(j, t)])

        def ple(l):
            S.fence()
            for blk in range(17):
                r0 = blk * 128
                n = 128 if blk < 16 else 64
                t = blk // 4
                sl = blk % 2
                S.add("sp", dma(PSTG[0:n, sl, :], pin[l, r0:r0 + n, :]), writes=[("A", "pstg", sl)], chan="ld_p%d" % sl)
                b = alloc()
                S.add("pe", mms([(PS[:, b, k * 128:k * 128 + n], [(PSTG[0:n, sl, k * 128:(k + 1) * 128], IDF[0:n, 0:n])]) for k in range(2)]),
                      reads=[("A", "pstg", sl)] + CK, writes=pk(b))
                S.add("act", act(PT[:, :, r0:r0 + n], PS[:, b, 0:256].rearrange("p (k q) -> p k q", k=2)[:, :, 0:n], AF.Copy), reads=pk(b), writes=[("A", "pt", blk)])
            norm(range(5), l, 24, u_full, ukey)
            for jb in range(4):
                s = ring_next()
                _, wv = colblock_load(s, ple_gate[l], jb * 256, 256, 8, "gu0", 0)
                wload(s, "gu1", RING[:, s, 2048:2560].rearrange("p (k n) -> p k n", k=2),
                      ple_proj[l][:, jb * 256:(jb + 1) * 256].rearrange("(k p) n -> p k n", p=128))
                pv = RING[:, s, 2048:2560].rearrange("p (k n) -> p k n", k=2)
                for m in range(2):
                    j = jb * 2 + m
                    for t in range(5):
                        t0, n = TILES[t]
                        b = alloc(2)
                        S.add("pe", mms([(PS[:, b, 0:n], [(wv[:, k, m * 128:(m + 1) * 128], U[:, k, t0:t0 + n]) for k in range(8)]),
                                         (PS[:, b + 1, 0:n], [(pv[:, k, m * 128:(m + 1) * 128], PT[:, k, t0:t0 + n]) for k in range(2)])]),
                              reads=[("ring", s, "gu0"), ("ring", s, "gu1")] + [ukey(k, t) for k in range(8)] + [("A", "pt", bb_) for bb_ in range(17)], writes=pk(b, 2))
                        S.add("act", act(SGT[:, 0:n], PS[:, b, 0:n], AF.Sigmoid), reads=pk(b), writes=[("A", "sgt")])
                        S.add("dve", tt(TMP2[:, 0:n], SGT[:, 0:n], PS[:, b + 1, 0:n], ALU.mult), reads=[("A", "sgt")] + pk(b + 1), writes=[("A", "tmp2")])
                        S.add("dve", tt(X[:, j, t0:t0 + n], X[:, j, t0:t0 + n], TMP2[:, 0:n], ALU.add), reads=[("A", "tmp2"), xk(j, t)], writes=[xk(j, t)])

        SEGS = [("P", 0, 0, 1024, [0, 1]), ("P", 1, 1024, 1024, [2, 3]), ("S", 2, 2048, 64, [4])]
        try:
            ckpt()
            for l in range(nlayers):
                ffn(l, 0)
                ckpt()
                for seg in SEGS:
                    mixer(l, seg)
                    ckpt()
                ffn(l, 1)
                ckpt()
                ple(l)
                ckpt()
        except _Stop:
            pass
        S.fence()
        norm(range(5), None, 0, lambda k, t, n: X[:, k, TILES[t][0]:TILES[t][0] + n], lambda k, t: xk(k, t))
        S.fence()
        for blk in range(17):
            r0 = blk * 128
            n = 128 if blk < 16 else 64
            t = blk // 4
            sl = blk % 2
            b = alloc(2)
            S.add("pe", mms([(PS[0:n, b + j // 4, (j % 4) * 128:(j % 4 + 1) * 128], [(X[:, j, r0:r0 + n], IDF)]) for j in range(8)]),
                  reads=[xk(j, t) for j in range(8)] + CK, writes=pk(b, 2))
            if blk % 2 == 0:
                S.add("act", act(XSTG[0:n, sl, :].rearrange("p (a b) -> p a b", a=2), PS[0:n, b:b + 2, :], AF.Copy), reads=pk(b, 2), writes=[("A", "xstg", sl)])
            else:
                S.add("dve", cp(XSTG[0:n, sl, :].rearrange("p (a b) -> p a b", a=2), PS[0:n, b:b + 2, :]), reads=pk(b, 2), writes=[("A", "xstg", sl)])
            S.add("sp", dma(y_o[r0:r0 + n, :], XSTG[0:n, sl, :]), reads=[("A", "xstg", sl)], chan="st_y%d" % sl)
        S.emit(nc, es)
    return nc


def _host_consts():
    cst = np.zeros((128, NCST), np.float32)
    cst[:, 0:128] = np.eye(128, dtype=np.float32)
    s = np.arange(128)[:, None]
    q = np.arange(128)[None, :]
    cst[:, 128:256] = (s <= q).astype(np.float32)
    s6 = np.arange(64)[:, None]
    q6 = np.arange(64)[None, :]
    cst[0:64, 256:320] = ((s6 <= q6) & (s6 // 4 == q6 // 4)).astype(np.float32)
    cst[0:64, 320:384] = (s6 // 4 == q6 // 4).astype(np.float32)
    cst[0:64, 384:400] = (s6 // 4 == np.arange(16)[None, :]).astype(np.float32)
    for gi, w in enumerate((2, 4, 8, 16)):
        cst[:, 400 + gi * 16: 400 + (gi + 1) * 16] = (1.0 / np.minimum(np.arange(16) + 1, w)).astype(np.float32)[None, :]
    cst[:, 464:592] = (s > q).astype(np.float32)
    cst[0:64, 592:656] = ((s6 > q6) & (s6 // 4 == q6 // 4)).astype(np.float32)
    e16 = np.zeros((16, 1024), np.float32)
    for r in range(16):
        e16[r, r * 64:(r + 1) * 64] = 1.0
    return cst, e16


def _host_params(inp):
    prm = np.zeros((128, NPRM), np.float32)

    def cols(v):
        return np.asarray(v, np.float32).reshape(8, 128).T

    for l in range(L):
        b = l * PL
        prm[:, b + 0:b + 8] = cols(inp["norm_ffn1"][l])
        prm[:, b + 8:b + 16] = cols(inp["norm_mix"][l])
        prm[:, b + 16:b + 24] = cols(inp["norm_ffn2"][l])
        prm[:, b + 24:b + 32] = cols(inp["norm_ple"][l])
        prm[:, b + 32:b + 40] = cols(inp["ssd_norm_w"][l])
        prm[:, b + 40:b + 48] = cols(inp["pool_scale"][l])
        cw = np.asarray(inp["conv_w"][l], np.float32)
        prm[:, b + 48:b + 96] = cw.reshape(4, 12, 128).transpose(2, 1, 0).reshape(128, 48)
        prm[:, b + 96:b + 108] = np.asarray(inp["conv_b"][l], np.float32).reshape(12, 128).T
        prm[:, b + 108:b + 124] = np.asarray(inp["a_log"][l], np.float32)[None, :]
        prm[:, b + 124:b + 140] = np.asarray(inp["d_skip"][l], np.float32)[None, :]
        prm[0:16, b + 140] = np.asarray(inp["dt_bias"][l], np.float32)
        prm[0:16, b + 141] = np.asarray(inp["a_log"][l], np.float32)
    prm[:, L * PL:L * PL + 8] = cols(inp["final_norm"])
    return prm


_NC_CACHE = {}


def kernel(**inp):
    inp = {k: np.asarray(v) for k, v in inp.items()}
    nlayers = int(os.environ.get("MK_LAYERS", L))
    if nlayers not in _NC_CACHE:
        _NC_CACHE[nlayers] = build_nc(nlayers)
    nc = _NC_CACHE[nlayers]
    cst, e16 = _host_consts()
    prm = _host_params(inp)
    shared = {k: np.ascontiguousarray(inp[k], dtype=np.float32) for k in
              ("w_in", "w_out", "pool_w", "ffn1_gate", "ffn2_gate", "ffn1_up", "ffn2_up", "ffn1_down", "ffn2_down", "ple_gate", "ple_proj")}
    in_maps = []
    for c in range(NCORE):
        sl = slice(16 * c, 16 * c + 16)
        m = dict(shared)
        m["xin"] = np.ascontiguousarray(np.concatenate([inp["x_prompt"][c], inp["x_sample"][sl].reshape(NSM, D)], axis=0), dtype=np.float32)
        m["pin"] = np.ascontiguousarray(np.concatenate([inp["p_prompt"][:, c], inp["p_sample"][:, sl].reshape(L, NSM, 256)], axis=1), dtype=np.float32)
        m["sssm"] = np.ascontiguousarray(inp["state_ssm"][:, sl].reshape(L, 16, 1024, 128), dtype=np.float32)
        m["sconv"] = np.ascontiguousarray(inp["state_conv"][:, sl].reshape(L, 48, CONV), dtype=np.float32)
        m["spool"] = np.ascontiguousarray(inp["state_pool"][:, sl].reshape(L, 240, 1024), dtype=np.float32)
        m["prm"] = prm
        m["cst"] = cst
        m["e16"] = e16
        in_maps.append(m)
    ncr = int(os.environ.get("MK_CORES", NCORE))
    res = run_bass_kernel_spmd(nc, in_maps[:ncr], core_ids=list(range(ncr)))
    R = list(res.results) + [res.results[0]] * (NCORE - ncr)
    y_prompt = np.stack([R[c]["y"][0:NPR] for c in range(NCORE)], axis=0)
    y_sample = np.concatenate([R[c]["y"][NPR:].reshape(16, 4, D) for c in range(NCORE)], axis=0)
    ssm_prompt = np.stack([R[c]["ssm_p"].reshape(L, 16, 64, 128) for c in range(NCORE)], axis=1)
    conv_prompt = np.stack([R[c]["conv_p"] for c in range(NCORE)], axis=1)
    pool_prompt = np.stack([R[c]["pool_p"] for c in range(NCORE)], axis=1)
    ssm_sample = np.concatenate([R[c]["ssm_s"].reshape(L, 16, 16, 64, 128) for c in range(NCORE)], axis=1)
    conv_sample = np.concatenate([R[c]["conv_s"].reshape(L, 16, 3, CONV) for c in range(NCORE)], axis=1)
    pool_sample = np.concatenate([R[c]["pool_s"].reshape(L, 16, 15, 1024) for c in range(NCORE)], axis=1)
    f = lambda a: np.ascontiguousarray(a, dtype=np.float32)
    return (f(y_prompt), f(y_sample), f(ssm_prompt), f(conv_prompt), f(pool_prompt), f(ssm_sample), f(conv_sample), f(pool_sample))
```

```python
import os
from contextlib import ExitStack
from math import prod

import numpy as np
import concourse.bass as bass
import concourse.mybir as mybir
from concourse.bass_utils import run_bass_kernel_spmd

F32 = mybir.dt.float32
BF16 = mybir.dt.bfloat16
AF = mybir.ActivationFunctionType
ALU = mybir.AluOpType
AX = mybir.AxisListType

NCORE = 8
L = 4
D = 1024
NPR = 2048
NSM = 64
T = NPR + NSM
DFF = 2816
DIN = 3600
CONV = 1536
EPSV = 1e-6
TILES = [(0, 512), (512, 512), (1024, 512), (1536, 512), (2048, 64)]
PL = 142
NPRM = L * PL + 8
NCST = 656
ENGS = ("pe", "act", "dve", "pool", "sp")


class Sched:
    def __init__(self):
        self.ops = []
        self.last_writer = {}
        self.readers = {}
        self.chan_last = {}
        self.fence_deps = set()
        self.fenced = set()
        self.arena_touch = {}

    def add(self, eng, fn, reads=(), writes=(), chan=None):
        idx = len(self.ops)
        if idx >= int(os.environ.get("MK_MAXOPS", 10 ** 9)) and not getattr(self, "nolimit", False):
            self.nolimit = True
            print("MAXOPS stop at", idx, eng, reads, writes)
            raise _Stop()
        deps = set()
        touches = False
        if eng != "pe":
            writes = list(writes) + [k for k in reads if k[0] == "ps" and k not in writes]
        for k in list(reads) + list(writes):
            if k[0] == "A":
                touches = True
                if k not in self.fenced:
                    deps.update(self.fence_deps)
                    self.fenced.add(k)
        for k in reads:
            if k in self.last_writer:
                deps.add(self.last_writer[k])
        for k in writes:
            if k in self.last_writer:
                deps.add(self.last_writer[k])
            deps.update(self.readers.get(k, ()))
        if chan is not None and chan in self.chan_last:
            deps.add(self.chan_last[chan])
        deps.discard(idx)
        for k in reads:
            self.readers.setdefault(k, []).append(idx)
        for k in writes:
            self.last_writer[k] = idx
            self.readers[k] = []
        if chan is not None:
            self.chan_last[chan] = idx
        if touches:
            self.arena_touch[chan if chan is not None else eng] = idx
        self.ops.append(dict(eng=eng, fn=fn, deps=deps, chan=chan, signal=False))
        return idx

    def fence(self):
        allp = set(self.fence_deps) | set(self.arena_touch.values())
        best = {}
        for i in allp:
            o = self.ops[i]
            k = o["chan"] if o["chan"] is not None else o["eng"]
            if k not in best or best[k] < i:
                best[k] = i
        self.fence_deps = set(best.values())
        self.fenced = set()
        self.arena_touch = {}
        for k in [k for k in self.last_writer if k[0] == "A"]:
            del self.last_writer[k]
        for k in [k for k in self.readers if k[0] == "A"]:
            del self.readers[k]

    def emit(self, nc, es):
        ops = self.ops
        for o in ops:
            nd = set()
            for d in o["deps"]:
                p = ops[d]
                if o["eng"] == "pe" and p["eng"] == "pe" and p["chan"] is None and o["chan"] is None:
                    continue
                nd.add(d)
                p["signal"] = True
            o["deps"] = nd
        chans = sorted({o["chan"] for o in ops if o["chan"] is not None})
        sems = {}
        for e in ENGS:
            sems[e] = es.enter_context(nc.semaphore("s_" + e))
        for c in chans:
            sems[c] = es.enter_context(nc.semaphore("c_" + c))
        cnt = {k: 0 for k in sems}
        for o in ops:
            if o["chan"] is not None:
                cnt[o["chan"]] += 16
                o["tick"] = (o["chan"], cnt[o["chan"]])
            elif o["signal"]:
                cnt[o["eng"]] += 1
                o["tick"] = (o["eng"], cnt[o["eng"]])
        block = es.enter_context(nc.Block())

        def run(engname, e):
            waited = {}
            for o in ops:
                if o["eng"] != engname:
                    continue
                need = {}
                for d in o["deps"]:
                    s, v = ops[d]["tick"]
                    if need.get(s, 0) < v:
                        need[s] = v
                for s, v in need.items():
                    if waited.get(s, 0) < v:
                        e.wait_ge(sems[s], v)
                        waited[s] = v
                ins = o["fn"](e)
                if o["chan"] is not None:
                    ins.then_inc(sems[o["chan"]], 16)
                elif o["signal"]:
                    ins.then_inc(sems[o["eng"]], 1)
            if engname == "sp":
                for s, v in cnt.items():
                    if v > 0 and waited.get(s, 0) < v:
                        e.wait_ge(sems[s], v)

        @block.tensor
        def _(e):
            run("pe", e)

        @block.scalar
        def _(e):
            run("act", e)

        @block.vector
        def _(e):
            run("dve", e)

        @block.gpsimd
        def _(e):
            run("pool", e)

        @block.sync
        def _(e):
            run("sp", e)


class _Stop(Exception):
    pass


def build_nc(nlayers=L):
    stop_at = int(os.environ.get("MK_STOP", 999))
    stage = [0]

    def ckpt():
        stage[0] += 1
        if stage[0] >= stop_at:
            raise _Stop()

    sub_at = int(os.environ.get("MK_SUB", 999))

    def sub(k):
        if k >= sub_at:
            raise _Stop()

    nc = bass.Bass("TRN2", target_bir_lowering=False)
    S = Sched()
    es = ExitStack()

    def din(name, shape):
        return nc.dram_tensor(name, list(shape), F32, kind="ExternalInput").ap()

    def dout(name, shape):
        return nc.dram_tensor(name, list(shape), F32, kind="ExternalOutput").ap()

    xin = din("xin", [T, D])
    pin = din("pin", [L, T, 256])
    sssm = din("sssm", [L, 16, 1024, 128])
    sconv = din("sconv", [L, 48, CONV])
    spool = din("spool", [L, 240, 1024])
    w_in = din("w_in", [L, D, DIN])
    w_out = din("w_out", [L, 2048, D])
    pool_w = din("pool_w", [L, 4, 256, 256])
    fg = [din("ffn1_gate", [L, D, DFF]), din("ffn2_gate", [L, D, DFF])]
    fu = [din("ffn1_up", [L, D, DFF]), din("ffn2_up", [L, D, DFF])]
    fd = [din("ffn1_down", [L, DFF, D]), din("ffn2_down", [L, DFF, D])]
    ple_gate = din("ple_gate", [L, D, D])
    ple_proj = din("ple_proj", [L, 256, D])
    prm_d = din("prm", [128, NPRM])
    cst_d = din("cst", [128, NCST])
    e16_d = din("e16", [16, 1024])

    y_o = dout("y", [T, D])
    ssm_p_o = dout("ssm_p", [L, 1024, 128])
    conv_p_o = dout("conv_p", [L, 3, CONV])
    pool_p_o = dout("pool_p", [L, 15, 1024])
    ssm_s_o = dout("ssm_s", [L, 16, 1024, 128])
    conv_s_o = dout("conv_s", [L, 48, CONV])
    pool_s_o = dout("pool_s", [L, 240, 1024])

    with es:
        def sb(name, shape, dt=F32):
            return es.enter_context(nc.sbuf_tensor(name, list(shape), dt))

        X = sb("X", [128, 8, T])
        RING = sb("RING", [128, 2, 6144], BF16)
        ARENA_B = 92928
        ARENA = sb("ARENA", [128, ARENA_B // 4])
        HST = sb("HST", [128, 1024])
        HB = sb("HB", [128, 1024], BF16)
        HISTC = sb("HISTC", [128, 12, 3])
        HISTP = sb("HISTP", [128, 8, 15])
        PRM = sb("PRM", [128, NPRM])
        CST = sb("CST", [128, NCST])
        IDB = sb("IDB", [128, 128], BF16)
        ONESB = sb("ONESB", [128, 128], BF16)
        ONESF = sb("ONESF", [128, 128])
        EPS = sb("EPS", [128, 1])
        STRB = sb("STRB", [128, 128], BF16)
        STRSEGB = sb("STRSEGB", [128, 64], BF16)
        A_B = sb("A_B", [128, L, 16])
        ACOL = sb("ACOL", [16, L])
        PS = es.enter_context(nc.psum_tensor("PS", [128, 8, 512], F32))

        IDF = CST[:, 0:128]
        TRI = CST[:, 128:256]
        TRISEG = CST[:, 256:320]
        BLK = CST[:, 320:384]
        MASKCOL = CST[:, 384:400]
        RC = CST[:, 400:464]
        STRICT = CST[:, 464:592]
        STRICTSEG = CST[:, 592:656]

        def av(off_b, shape, dt=F32):
            nel = prod(shape[1:])
            nb = nel * (4 if dt == F32 else 2)
            assert off_b % 4 == 0 and nb % 4 == 0 and off_b + nb <= ARENA_B, (off_b, shape)
            v = ARENA[0:shape[0], off_b // 4: (off_b + nb) // 4]
            if dt == BF16:
                v = v.bitcast(BF16)
            if len(shape) == 3:
                v = v.rearrange("p (a b) -> p a b", a=shape[1])
            elif len(shape) == 4:
                v = v.rearrange("p (a b c) -> p a b c", a=shape[1], b=shape[2])
            return v

        U = av(0, [128, 8, T], BF16)
        HH = av(33792, [128, 2, 2, T], BF16)
        SG = av(50688, [128, 2, 512], BF16)
        SQ = av(62976, [128, 8, 512], BF16)
        RS = av(71168, [128, 512])
        PT = av(73216, [128, 2, T], BF16)
        SGT = av(81664, [128, 512])
        TMP2 = av(83712, [128, 512])
        PSTG = av(85760, [128, 2, 256])
        XSTG = av(0, [128, 2, 1024])
        US = av(0, [128, 8, 1024], BF16)
        MIX = av(16384, [128, 12, 1024], BF16)
        ZB = av(40960, [128, 8, 1024], BF16)
        DTT = av(57344, [16, 1024])
        SC = 61440
        PA = [av(SC, [128, 1040]), av(SC + 4160, [128, 1040])]
        PP1 = av(SC + 8320, [128, 1040])
        PP2 = av(SC + 12480, [128, 1040])
        POUT = av(SC + 16640, [128, 1024])
        PSTGC = av(SC + 20736, [128, 2, 256])
        PASTP = av(SC + 22784, [128, 2, 240])
        SNEW = av(SC + 24704, [128, 2, 64])
        TMP16 = av(SC + 25216, [128, 16])
        RAW = [av(SC, [128, 1028]), av(SC + 4112, [128, 1028])]
        ACC = av(SC + 8224, [128, 1024])
        CSTG = av(SC + 12320, [128, 1536])
        PASTT = av(SC + 18464, [128, 12, 48])
        SELC = av(SC + 20768, [128, 2, 48])
        BIG = av(SC, [128, 1024])
        WW = av(SC + 4096, [128, 1024], BF16)
        XDT = av(SC + 6144, [128, 1024], BF16)
        XDTD = av(SC + 8192, [128, 1024], BF16)
        XSD = av(SC + 10240, [128, 1024], BF16)
        YT = av(SC + 12288, [128, 1024])
        GN = av(SC + 16384, [128, 1024], BF16)
        BTOK = av(SC + 18432, [128, 256], BF16)
        SM = av(SC + 18944, [128, 2, 128])
        SML = av(SC + 19968, [128, 16, 16])
        H0X = av(SC + 20992, [128, 1024])
        CMS = av(SC + 25088, [128, 2, 2, 64], BF16)
        BMS = av(SC + 25600, [128, 2, 256], BF16)
        CDF = av(SC + 26624, [128, 8, 16])
        E16 = av(SC + 27136, [16, 1024])
        SETS = [
            (BIG, WW, XDT, XDTD, XSD, YT, GN, BTOK, SM, SML),
            (av(0, [128, 1024]), av(4096, [128, 1024], BF16), av(6144, [128, 1024], BF16), av(8192, [128, 1024], BF16),
             av(10240, [128, 1024], BF16), av(12288, [128, 1024]), av(SC + 20992, [128, 1024], BF16), av(SC + 23040, [128, 256], BF16),
             av(SC + 23552, [128, 2, 128]), av(SC + 24576, [128, 16, 16])),
        ]

        bank_cur = [0]

        held = set()

        def alloc(n=1, hold=False):
            c = bank_cur[0]
            for _ in range(32):
                if n > 1 and c % n:
                    c += n - c % n
                if c + n > 8:
                    c = 0
                if all((c + i) not in held for i in range(n)):
                    break
                c += 1
            else:
                raise RuntimeError("no free psum banks")
            bank_cur[0] = (c + n) % 8
            if hold:
                for i in range(n):
                    held.add(c + i)
            return c

        def release(b, n=1):
            for i in range(n):
                held.discard(b + i)

        def pk(b, n=1):
            return [("ps", b + i) for i in range(n)]

        ring_cur = [0]
        dma_flip = [0]

        def P(l, off, n=1):
            return PRM[:, l * PL + off: l * PL + off + n]

        def mm(out, pairs):
            def fn(e):
                n = len(pairs)
                for i, (a, b) in enumerate(pairs):
                    m = e.matmul(out, a, b, start=(i == 0), stop=(i == n - 1))
                return m
            return fn

        def mms(groups):
            def fn(e):
                for out, pairs in groups:
                    n = len(pairs)
                    for i, (a, b) in enumerate(pairs):
                        m = e.matmul(out, a, b, start=(i == 0), stop=(i == n - 1))
                return m
            return fn

        def act(out, in_, func, **kw):
            return lambda e: e.activation(out=out, in_=in_, func=func, **kw)

        def tt(out, a, b, op):
            return lambda e: e.tensor_tensor(out=out, in0=a, in1=b, op=op)

        def ts(out, a, s1, s2, op0, op1):
            return lambda e: e.tensor_scalar(out=out, in0=a, scalar1=s1, scalar2=s2, op0=op0, op1=op1)

        def stt(out, a, s, b, op0, op1):
            return lambda e: e.scalar_tensor_tensor(out=out, in0=a, scalar=s, in1=b, op0=op0, op1=op1)

        def cp(out, in_):
            return lambda e: e.tensor_copy(out, in_)

        def ms(ap, v):
            return lambda e: e.memset(ap, v)

        def dma(out, in_):
            return lambda e: e.dma_start(out=out, in_=in_)

        def xk(j, t):
            return ("x", j, t)

        S.add("sp", dma(PRM[:], prm_d[:, :]), writes=[("prm",)], chan="ld_prm")
        S.add("sp", dma(CST[:], cst_d[:, :]), writes=[("cst",)], chan="ld_cst")
        S.add("dve", cp(IDB[:], IDF), reads=[("cst",)], writes=[("idb",)])
        S.add("dve", ms(ONESB[:], 1.0 / 1024.0), writes=[("onesb",)])
        S.add("dve", cp(STRB[:], CST[:, 464:592]), reads=[("cst",)], writes=[("strb",)])
        S.add("dve", cp(STRSEGB[0:64, :], CST[0:64, 592:656]), reads=[("cst",)], writes=[("strsegb",)])
        S.add("dve", ms(ONESF[:], 1.0), writes=[("onesf",)])
        S.add("dve", ms(EPS[:], EPSV), writes=[("eps",)])
        for l in range(L):
            S.add("act", act(A_B[:, l, :], P(l, 108, 16), AF.Exp), reads=[("prm",)], writes=[("ab0", l)])
            S.add("dve", ts(A_B[:, l, :], A_B[:, l, :], -1.0, 0.0, ALU.mult, ALU.add), reads=[("ab0", l)], writes=[("ab",)])
            S.add("act", act(ACOL[:, l:l + 1], PRM[0:16, l * PL + 141: l * PL + 142], AF.Exp), reads=[("prm",)], writes=[("ac0", l)])
            S.add("dve", ts(ACOL[:, l:l + 1], ACOL[:, l:l + 1], -1.0, 0.0, ALU.mult, ALU.add), reads=[("ac0", l)], writes=[("acol",)])
        CK = [("cst",), ("idb",), ("onesb",), ("onesf",), ("eps",), ("ab",), ("acol",), ("prm",), ("strb",), ("strsegb",)]

        for blk in range(17):
            r0 = blk * 128
            n = 128 if blk < 16 else 64
            t = blk // 4
            sl = blk % 2
            S.add("sp", dma(XSTG[0:n, sl, :], xin[r0:r0 + n, :]), writes=[("A", "xstg", sl)], chan="ld_x%d" % sl)
            for hb in range(2):
                b = alloc()
                S.add("pe", mms([(PS[:, b, jj * 128: jj * 128 + n],
                                  [(XSTG[0:n, sl, (hb * 4 + jj) * 128:(hb * 4 + jj + 1) * 128], IDF[0:n, 0:n])]) for jj in range(4)]),
                      reads=[("A", "xstg", sl)] + CK, writes=pk(b))
                src = PS[:, b, :].rearrange("p (j q) -> p j q", j=4)[:, :, 0:n]
                dst = X[:, hb * 4:hb * 4 + 4, r0:r0 + n]
                if (blk + hb) % 2 == 0:
                    S.add("act", act(dst, src, AF.Copy), reads=pk(b), writes=[xk(hb * 4 + jj, t) for jj in range(4)])
                else:
                    S.add("dve", cp(dst, src), reads=pk(b), writes=[xk(hb * 4 + jj, t) for jj in range(4)])
        S.fence()

        def norm(tiles, wl, woff, dst_fn, dkey_fn):
            for t in tiles:
                t0, n = TILES[t]
                S.add("act", act(SQ[:, :, 0:n], X[:, :, t0:t0 + n], AF.Square),
                      reads=[xk(j, t) for j in range(8)], writes=[("A", "sq")])
                b = alloc()
                S.add("pe", mm(PS[:, b, 0:n], [(ONESB[:], SQ[:, k, 0:n]) for k in range(8)]),
                      reads=[("A", "sq")] + CK, writes=pk(b))
                S.add("act", act(RS[:, 0:n], PS[:, b, 0:n], AF.Ln, bias=EPS[:, 0:1]), reads=pk(b) + CK, writes=[("A", "rs")])
                S.add("act", act(RS[:, 0:n], RS[:, 0:n], AF.Exp, scale=-0.5), reads=[("A", "rs")], writes=[("A", "rs")])
                for k in range(8):
                    wcol = PRM[:, wl * PL + woff + k: wl * PL + woff + k + 1] if wl is not None else PRM[:, L * PL + k: L * PL + k + 1]
                    S.add("dve", stt(dst_fn(k, t, n), X[:, k, t0:t0 + n], wcol, RS[:, 0:n], ALU.mult, ALU.mult),
                          reads=[xk(k, t), ("A", "rs")] + CK, writes=[dkey_fn(k, t)])

        def u_full(k, t, n):
            return U[:, k, TILES[t][0]:TILES[t][0] + n]

        def ukey(k, t):
            return ("A", "u", k, t)

        def ring_next():
            s = ring_cur[0]
            ring_cur[0] = 1 - s
            return s

        def wload(slot, part, view, src):
            S.add("pool", dma(view, src), writes=[("ring", slot, part)], chan="r%d%s" % (slot, part))

        def ffn(l, which):
            S.fence()
            norm(range(5), l, 0 if which == 0 else 16, u_full, ukey)
            NG = DFF // 256
            gd, ud, dd = fg[which], fu[which], fd[which]
            slots = {}

            def load(g):
                s = ring_next()
                slots[g] = s
                wload(s, "gu0", RING[:, s, 0:2048].rearrange("p (k n) -> p k n", k=8),
                      gd[l, :, g * 256:(g + 1) * 256].rearrange("(k p) n -> p k n", p=128))
                wload(s, "gu1", RING[:, s, 2048:4096].rearrange("p (k n) -> p k n", k=8),
                      ud[l, :, g * 256:(g + 1) * 256].rearrange("(k p) n -> p k n", p=128))
                wload(s, "d", RING[:, s, 4096:6144].rearrange("p (f n) -> p f n", f=2),
                      dd[l, g * 256:(g + 1) * 256, :].rearrange("(f p) n -> p f n", p=128))

            def gu(g):
                s = slots[g]
                hs = g % 2
                wg = RING[:, s, 0:2048].rearrange("p (k n) -> p k n", k=8)
                wu = RING[:, s, 2048:4096].rearrange("p (k n) -> p k n", k=8)
                for f in range(2):
                    for t in range(5):
                        t0, n = TILES[t]
                        b = alloc(2)
                        S.add("pe", mms([(PS[:, b, 0:n], [(wg[:, k, f * 128:(f + 1) * 128], U[:, k, t0:t0 + n]) for k in range(8)]),
                                         (PS[:, b + 1, 0:n], [(wu[:, k, f * 128:(f + 1) * 128], U[:, k, t0:t0 + n]) for k in range(8)])]),
                              reads=[("ring", s, "gu0"), ("ring", s, "gu1")] + [ukey(k, t) for k in range(8)], writes=pk(b, 2))
                        sgs = (f * 5 + t) % 2
                        S.add("act", act(SG[:, sgs, 0:n], PS[:, b, 0:n], AF.Silu), reads=pk(b), writes=[("A", "sg", sgs)])
                        S.add("dve", tt(HH[:, hs, f, t0:t0 + n], SG[:, sgs, 0:n], PS[:, b + 1, 0:n], ALU.mult),
                              reads=[("A", "sg", sgs)] + pk(b + 1), writes=[("A", "h", hs, f, t)])

            def down(g):
                s = slots[g]
                hs = g % 2
                wd = RING[:, s, 4096:6144].rearrange("p (f n) -> p f n", f=2)
                for j in range(8):
                    for t in range(5):
                        t0, n = TILES[t]
                        b = alloc()
                        S.add("pe", mm(PS[:, b, 0:n], [(wd[:, f, j * 128:(j + 1) * 128], HH[:, hs, f, t0:t0 + n]) for f in range(2)]),
                              reads=[("ring", s, "d")] + [("A", "h", hs, f, t) for f in range(2)], writes=pk(b))
                        S.add("dve", stt(X[:, j, t0:t0 + n], PS[:, b, 0:n], 0.5, X[:, j, t0:t0 + n], ALU.mult, ALU.add),
                              reads=pk(b) + [xk(j, t)], writes=[xk(j, t)])

            load(0)
            load(1)
            gu(0)
            for g in range(NG):
                if g + 1 < NG:
                    gu(g + 1)
                down(g)
                if g + 2 < NG:
                    load(g + 2)

        def colblock_load(s, src2d, col0, ncols, kc, part="gu0", off=0):
            view = RING[:, s, off:off + kc * ncols].rearrange("p (k n) -> p k n", k=kc)
            wload(s, part, view, src2d[:, col0:col0 + ncols].rearrange("(k p) n -> p k n", p=128))
            return s, view

        def mixer(l, seg):
            kind, sidx, s0, ns, tiles = seg
            smp = kind == "S"
            nseq, ntok = (16, 4) if smp else (1, 1024)
            ltile = [(TILES[t][0] - s0, TILES[t][1], t) for t in tiles]

            def us(k, t, n):
                return US[:, k, TILES[t][0] - s0: TILES[t][0] - s0 + n]

            def uskey(k, t):
                return ("A", "us", k, t)

            S.fence()
            norm(tiles, l, 8, us, uskey)
            ckpt()
            S.fence()
            if smp:
                S.add("sp", dma(pool_s_o[l].rearrange("(s r) f -> s r f", r=15)[:, 0:11, :],
                                spool[l].rearrange("(s r) f -> s r f", r=15)[:, 4:15, :]), chan="st_pool_d2d")
            for cb in range(4):
                wwin = 2 << cb
                s = ring_next()
                _, wv = colblock_load(s, w_in[l], 2576 + cb * 256, 256, 8, "gu0", 0)
                wload(s, "gu1", RING[:, s, 2048:2560].rearrange("p (k n) -> p k n", k=2),
                      pool_w[l, cb].rearrange("(k p) n -> p k n", p=128))
                pwv = RING[:, s, 2048:2560].rearrange("p (k n) -> p k n", k=2)
                for m in range(2):
                    c = cb * 2 + m
                    A = PA[m]
                    A3 = A[:, 0:nseq * (16 + ntok)].rearrange("p (s q) -> p s q", s=nseq)
                    akey = ("A", "pa", m)
                    if smp:
                        S.add("sp", dma(PSTGC[0:120, :, 0:128],
                                        spool[l, :, c * 128:(c + 1) * 128].rearrange("(a p) f -> p a f", p=120)),
                              writes=[("A", "pstgc")], chan="ld_sp")
                        b = alloc()
                        S.add("pe", mms([(PS[:, b, a * 120:(a + 1) * 120], [(PSTGC[0:120, a, 0:128], IDF[0:120, 0:120])]) for a in range(2)]),
                              reads=[("A", "pstgc")] + CK, writes=pk(b))
                        S.add("dve", cp(A3[:, :, 1:16], PS[:, b, 0:240].rearrange("p (s r) -> p s r", r=15)), reads=pk(b), writes=[akey])
                    elif sidx == 0:
                        S.add("dve", ms(A[:, 0:16], 0.0), writes=[akey])
                    else:
                        S.add("dve", cp(A[:, 1:16], HISTP[:, c, :]), reads=[("histp", c)], writes=[akey])
                    for (lc, n, t) in ltile:
                        b = alloc()
                        S.add("pe", mm(PS[:, b, 0:n], [(wv[:, k, m * 128:(m + 1) * 128], us(k, t, n)) for k in range(8)]),
                              reads=[("ring", s, "gu0")] + [uskey(k, t) for k in range(8)], writes=pk(b))
                        if smp:
                            S.add("act", act(A3[:, :, 16:20], PS[:, b, 0:64].rearrange("p (s q) -> p s q", q=4), AF.Copy), reads=pk(b), writes=[akey])
                            S.add("act", act(SNEW[:, m, :], PS[:, b, 0:64], AF.Copy), reads=pk(b), writes=[("A", "snew", m)])
                        else:
                            S.add("act", act(A[:, 16 + lc:16 + lc + n], PS[:, b, 0:n], AF.Copy), reads=pk(b), writes=[akey])
                    LL = 16 + ntok
                    if smp:
                        if m == 0 and cb % 2 == 0:
                            pob = alloc(1, hold=True)
                        S.add("pe", mm(PS[0:64, pob, (c % 4) * 128:(c % 4 + 1) * 128], [(SNEW[:, m, :], IDF)]),
                              reads=[("A", "snew", m)] + CK, writes=pk(pob))
                    elif sidx == 0:
                        S.add("dve", cp(HISTP[:, c, :], A[:, LL - 15:LL]), reads=[akey], writes=[("histp", c)])
                    else:
                        if m == 0 and cb % 2 == 0:
                            pob = alloc(1, hold=True)
                        S.add("pe", mm(PS[0:15, pob, (c % 4) * 128:(c % 4 + 1) * 128], [(A[:, LL - 15:LL], IDF)]),
                              reads=[akey] + CK, writes=pk(pob))
                    if (smp or sidx == 1) and c % 4 == 3:
                        nr = 64 if smp else 15
                        hh = c // 4
                        S.add("act", act(POUT[0:nr, hh * 512:(hh + 1) * 512], PS[0:nr, pob, :], AF.Copy), reads=pk(pob), writes=[("A", "pout", hh)])
                        release(pob)
                        if c == 7:
                            if smp:
                                for sq in range(16):
                                    S.add("sp", dma(pool_s_o[l, sq * 15 + 11: sq * 15 + 15, :], POUT[sq * 4: sq * 4 + 4, :]),
                                          reads=[("A", "pout", 0), ("A", "pout", 1)], chan="st_pool%d" % (sq % 4))
                            else:
                                S.add("sp", dma(pool_p_o[l], POUT[0:15, :]), reads=[("A", "pout", 0), ("A", "pout", 1)], chan="st_pool0")
                    cur, curkey = A3, akey
                    bufs = [(PP1, ("A", "pp1")), (PP2, ("A", "pp2"))]
                    sh = 1
                    lvl = 0
                    while sh < wwin:
                        lo = 2 * sh - 1
                        nb_, nk = bufs[lvl % 2]
                        nb3 = nb_[:, 0:nseq * LL].rearrange("p (s q) -> p s q", s=nseq)
                        S.add("dve", tt(nb3[:, :, lo:LL], cur[:, :, lo:LL], cur[:, :, lo - sh:LL - sh], ALU.add), reads=[curkey], writes=[nk])
                        cur, curkey = nb3, nk
                        sh *= 2
                        lvl += 1
                    dst = MIX[:, c, 0:ns].rearrange("p (s q) -> p s q", s=nseq)
                    mkeys = [("A", "mix", c, t) for (_, _, t) in ltile]
                    S.add("dve", stt(dst, cur[:, :, 16:LL], 1.0 / wwin, A3[:, :, 16:LL], ALU.mult, ALU.subtract),
                          reads=[curkey, akey], writes=mkeys)
                    if (not smp) and sidx == 0:
                        S.add("dve", tt(TMP16[:, :], cur[:, 0, 16:32], RC[:, cb * 16:(cb + 1) * 16], ALU.mult), reads=[curkey] + CK, writes=[("A", "tmp16")])
                        S.add("dve", tt(MIX[:, c, 0:16], TMP16[:, :], A[:, 16:32], ALU.subtract), reads=[("A", "tmp16"), akey], writes=mkeys[0:1])
                for (lc, n, t) in ltile:
                    b = alloc(2)
                    S.add("pe", mms([(PS[:, b + m, 0:n], [(pwv[:, k, m * 128:(m + 1) * 128], MIX[:, cb * 2 + k, lc:lc + n]) for k in range(2)]) for m in range(2)]),
                          reads=[("ring", s, "gu1")] + [("A", "mix", cb * 2 + k, t) for k in range(2)], writes=pk(b, 2))
                    for m in range(2):
                        c = cb * 2 + m
                        S.add("act", act(MIX[:, c, lc:lc + n], PS[:, b + m, 0:n], AF.Copy, scale=P(l, 40 + c)),
                              reads=[("ps", b + m)] + CK, writes=[("A", "mix", c, t)])
            for jb in range(4):
                s = ring_next()
                _, wv = colblock_load(s, w_out[l, 1024:2048], jb * 256, 256, 8, "gu0", 0)
                for m in range(2):
                    j = jb * 2 + m
                    for (lc, n, t) in ltile:
                        b = alloc()
                        S.add("pe", mm(PS[:, b, 0:n], [(wv[:, k, m * 128:(m + 1) * 128], MIX[:, k, lc:lc + n]) for k in range(8)]),
                              reads=[("ring", s, "gu0")] + [("A", "mix", k, t) for k in range(8)], writes=pk(b))
                        g0 = TILES[t][0]
                        S.add("dve", tt(X[:, j, g0:g0 + n], X[:, j, g0:g0 + n], PS[:, b, 0:n], ALU.add), reads=pk(b) + [xk(j, t)], writes=[xk(j, t)])
            ckpt()
            S.fence()
            if smp:
                S.add("sp", dma(CSTG[0:48, :], sconv[l]), writes=[("A", "cstg")], chan="ld_sc")
                for q4 in range(3):
                    b = alloc()
                    S.add("pe", mms([(PS[:, b, jj * 48:(jj + 1) * 48], [(CSTG[0:48, (q4 * 4 + jj) * 128:(q4 * 4 + jj + 1) * 128], IDF[0:48, 0:48])]) for jj in range(4)]),
                          reads=[("A", "cstg")] + CK, writes=pk(b))
                    S.add("dve", cp(PASTT[:, q4 * 4:q4 * 4 + 4, :], PS[:, b, 0:192].rearrange("p (j q) -> p j q", j=4)), reads=pk(b), writes=[("A", "pastt", q4)])
            LR = 3 + ntok
            for cb in range(6):
                s = ring_next()
                _, wv = colblock_load(s, w_in[l], 1024 + cb * 256, 256, 8, "gu0", 0)
                for m in range(2):
                    c = cb * 2 + m
                    R = RAW[m]
                    R3 = R[:, 0:nseq * LR].rearrange("p (s q) -> p s q", s=nseq)
                    rkey = ("A", "raw", m)
                    if smp:
                        S.add("dve", cp(R3[:, :, 0:3], PASTT[:, c, :].rearrange("p (s r) -> p s r", r=3)), reads=[("A", "pastt", c // 4)], writes=[rkey])
                    elif sidx == 0:
                        S.add("dve", ms(R[:, 0:3], 0.0), writes=[rkey])
                    else:
                        S.add("dve", cp(R[:, 0:3], HISTC[:, c, :]), reads=[("histc", c)], writes=[rkey])
                    for (lc, n, t) in ltile:
                        b = alloc()
                        S.add("pe", mm(PS[:, b, 0:n], [(wv[:, k, m * 128:(m + 1) * 128], us(k, t, n)) for k in range(8)]),
                              reads=[("ring", s, "gu0")] + [uskey(k, t) for k in range(8)], writes=pk(b))
                        if smp:
                            S.add("act", act(R3[:, :, 3:7], PS[:, b, 0:64].rearrange("p (s q) -> p s q", q=4), AF.Copy), reads=pk(b), writes=[rkey])
                        else:
                            S.add("act", act(R[:, 3 + lc:3 + lc + n], PS[:, b, 0:n], AF.Copy), reads=pk(b), writes=[rkey])
                    if smp:
                        S.add("dve", cp(SELC[:, m, :].rearrange("p (s r) -> p s r", r=3), R3[:, :, 4:7]), reads=[rkey], writes=[("A", "selc", m)])
                        if c % 4 == 0:
                            cob = alloc(1, hold=True)
                        S.add("pe", mm(PS[0:48, cob, (c % 4) * 128:(c % 4 + 1) * 128], [(SELC[:, m, :], IDF)]), reads=[("A", "selc", m)] + CK, writes=pk(cob))
                    elif sidx == 0:
                        S.add("dve", cp(HISTC[:, c, :], R[:, LR - 3:LR]), reads=[rkey], writes=[("histc", c)])
                    else:
                        if c % 4 == 0:
                            cob = alloc(1, hold=True)
                        S.add("pe", mm(PS[0:3, cob, (c % 4) * 128:(c % 4 + 1) * 128], [(R[:, LR - 3:LR], IDF)]), reads=[rkey] + CK, writes=pk(cob))
                    if (smp or sidx == 1) and c % 4 == 3:
                        nr = 48 if smp else 3
                        q4 = c // 4
                        S.add("act", act(CSTG[0:nr, q4 * 512:(q4 + 1) * 512], PS[0:nr, cob, :], AF.Copy), reads=pk(cob), writes=[("A", "cstg")])
                        release(cob)
                        if c == 11:
                            S.add("sp", dma((conv_s_o if smp else conv_p_o)[l], CSTG[0:nr, :]), reads=[("A", "cstg")], chan="st_conv")
                    A3 = ACC[:, 0:ns].rearrange("p (s q) -> p s q", s=nseq)
                    S.add("dve", ts(A3, R3[:, :, 0:ntok], P(l, 48 + c * 4), P(l, 96 + c), ALU.mult, ALU.add), reads=[rkey] + CK, writes=[("A", "acc")])
                    for jt in range(1, 4):
                        S.add("dve", stt(A3, R3[:, :, jt:jt + ntok], P(l, 48 + c * 4 + jt), A3, ALU.mult, ALU.add), reads=[rkey, ("A", "acc")] + CK, writes=[("A", "acc")])
                    S.add("act", act(MIX[:, c, 0:ns], ACC[:, 0:ns], AF.Silu), reads=[("A", "acc")], writes=[("A", "mix", c, t) for (_, _, t) in ltile])
            s = ring_next()
            _, wv = colblock_load(s, w_in[l], 2560, 16, 8, "gu0", 0)
            for (lc, n, t) in ltile:
                b = alloc()
                S.add("pe", mm(PS[0:16, b, 0:n], [(wv[:, k, :], us(k, t, n)) for k in range(8)]),
                      reads=[("ring", s, "gu0")] + [uskey(k, t) for k in range(8)], writes=pk(b))
                S.add("act", act(DTT[:, lc:lc + n], PS[0:16, b, 0:n], AF.Exp, bias=PRM[0:16, l * PL + 140:l * PL + 141]), reads=pk(b) + CK, writes=[("A", "dtt", t)])
                S.add("act", act(DTT[:, lc:lc + n], DTT[:, lc:lc + n], AF.Ln, bias=1.0), reads=[("A", "dtt", t)], writes=[("A", "dtt", t)])
            for cb in range(4):
                s = ring_next()
                _, wv = colblock_load(s, w_in[l], cb * 256, 256, 8, "gu0", 0)
                for m in range(2):
                    c = cb * 2 + m
                    for (lc, n, t) in ltile:
                        b = alloc()
                        S.add("pe", mm(PS[:, b, 0:n], [(wv[:, k, m * 128:(m + 1) * 128], us(k, t, n)) for k in range(8)]),
                              reads=[("ring", s, "gu0")] + [uskey(k, t) for k in range(8)], writes=pk(b))
                        S.add("act", act(ZB[:, c, lc:lc + n], PS[:, b, 0:n], AF.Silu), reads=pk(b), writes=[("A", "zb", c, t)])
            ckpt()
            S.fence()
            AB = A_B[:, l, :]
            DB = P(l, 124, 16)
            if (not smp) and sidx == 0:
                S.add("dve", ms(HST[:], 0.0), writes=[("hst",)])
                S.add("dve", ms(HB[:], 0.0), writes=[("hb",)])
            if smp:
                S.add("sp", dma(E16[:, :], e16_d[:, :]), writes=[("A", "e16")], chan="ld_e16")
            nq = 64 if smp else 128
            nch = 1 if smp else min(8, int(os.environ.get("MK_NCH", 8)))
            TR = TRISEG[0:64, 0:64] if smp else TRI
            def chunk_gen(ci):
                q0 = ci * 128
                par = 0 if smp else ci % 2
                sfx = str(par)
                BIG, WW, XDT, XDTD, XSD, YT, GN, BTOK, SM, SML = SETS[par]
                DTTOK, DA, CUMCOL, EXPA, CDB, DOUT, SSQ, RSTD, TOTT = (SML[:, i, :] for i in range(9))
                tg = tiles[q0 // 512] if not smp else tiles[0]
                mk = lambda c: ("A", "mix", c, tg)
                bx = alloc(2)
                S.add("pe", mms([(PS[0:nq, bx + jj // 4, (jj % 4) * 128:(jj % 4 + 1) * 128], [(MIX[:, jj, q0:q0 + nq], IDB[:])]) for jj in range(8)]),
                      reads=[mk(c) for c in range(8)] + CK, writes=pk(bx, 2))
                bb = alloc()
                S.add("pe", mms([(PS[0:nq, bb, g * 128:(g + 1) * 128], [(MIX[:, 8 + g, q0:q0 + nq], IDB[:])]) for g in range(2)]
                                + [(PS[0:nq, bb, 256:272], [(DTT[0:16, q0:q0 + nq], IDF[0:16, 0:16])])]),
                      reads=[mk(8), mk(9), ("A", "dtt", tg)] + CK, writes=pk(bb))
                S.add("dve", cp(DTTOK[0:nq, :], PS[0:nq, bb, 256:272]), reads=pk(bb), writes=[("A", "dttok" + sfx)])
                S.add("dve", tt(DA[0:nq, :], PS[0:nq, bb, 256:272], AB[0:nq, :], ALU.mult), reads=pk(bb) + CK, writes=[("A", "da" + sfx)])
                DAHL = SML[0:nq, 14, :].bitcast(BF16)
                S.add("dve", cp(DAHL[:, 0:16], DA[0:nq, :]), reads=[("A", "da" + sfx)], writes=[("A", "dahl" + sfx)])
                S.add("dve", tt(DAHL[:, 16:32], DA[0:nq, :], DAHL[:, 0:16], ALU.subtract), reads=[("A", "da" + sfx), ("A", "dahl" + sfx)], writes=[("A", "dahl" + sfx)])
                S.add("act", act(BTOK[0:nq, :], PS[0:nq, bb, 0:256], AF.Copy), reads=pk(bb), writes=[("A", "btok" + sfx)])
                sub(1)
                bc = alloc()
                grp = [(PS[0:nq, bc, 0:16], [(TR[0:nq, 0:nq], DA[0:nq, :])])]
                if smp:
                    grp.append((PS[0:nq, bc, 16:32], [(BLK[0:64, 0:64], DA[0:nq, :])]))
                else:
                    grp.append((PS[0:nq, bc, 32:48], [(ONESF[0:nq, 0:nq], DA[0:nq, :])]))
                S.add("pe", mms(grp), reads=[("A", "da" + sfx)] + CK, writes=pk(bc))
                S.add("dve", cp(CUMCOL[0:nq, :], PS[0:nq, bc, 0:16]), reads=pk(bc), writes=[("A", "cumcol" + sfx)])
                S.add("act", act(EXPA[0:nq, :], PS[0:nq, bc, 0:16], AF.Exp), reads=pk(bc), writes=[("A", "expa" + sfx)])
                if not smp:
                    S.add("act", act(CDB[:, :], PS[:, bc, 32:48], AF.Exp), reads=pk(bc), writes=[("A", "cdb" + sfx, 0), ("A", "cdb" + sfx, 1)])
                if smp:
                    S.add("dve", tt(DOUT[0:nq, :], PS[0:nq, bc, 16:32], CUMCOL[0:nq, :], ALU.subtract), reads=pk(bc) + [("A", "cumcol" + sfx)], writes=[("A", "dout" + sfx)])
                    S.add("act", act(DOUT[0:nq, :], DOUT[0:nq, :], AF.Exp), reads=[("A", "dout" + sfx)], writes=[("A", "dout" + sfx)])
                sub(2)
                xs3 = PS[0:nq, bx:bx + 2, :].rearrange("p a (r q) -> p (a r) q", q=64)
                S.add("dve", tt(XDT[0:nq, :].rearrange("p (r q) -> p r q", q=64), xs3, DTTOK[0:nq, :].unsqueeze(2).broadcast_to([nq, 16, 64]), ALU.mult),
                      reads=pk(bx, 2) + [("A", "dttok" + sfx)], writes=[("A", "xdt" + sfx)])
                S.add("dve", tt(XSD[0:nq, :].rearrange("p (r q) -> p r q", q=64), xs3, DB[0:nq, :].unsqueeze(2).broadcast_to([nq, 16, 64]), ALU.mult),
                      reads=pk(bx, 2) + CK, writes=[("A", "xsd" + sfx)])
                sub(3)
                bs = alloc()
                S.add("pe", mms([(PS[0:nq, bs, g * 128:g * 128 + nq], [(MIX[:, 8 + g, q0:q0 + nq], MIX[:, 10 + g, q0:q0 + nq])]) for g in range(2)]),
                      reads=[mk(8), mk(9), mk(10), mk(11)], writes=pk(bs))
                S.add("dve", tt(SM[0:nq, :, 0:nq], PS[0:nq, bs, 0:256].rearrange("p (g q) -> p g q", g=2)[:, :, 0:nq],
                                TR[0:nq, 0:nq].unsqueeze(1).broadcast_to([nq, 2, nq]), ALU.mult), reads=pk(bs) + CK, writes=[("A", "sm" + sfx)])
                sub(4)
                nbk = (8 * nq) // 512
                SLB = STRSEGB[0:64, 0:64] if smp else STRB[:, :]
                grp_state = []
                for g in range(2):
                    BIGg, WWg = SETS[g][0], SETS[g][1]
                    gs = "g%d" % g
                    B3 = BIGg[0:nq, 0:8 * nq].rearrange("p (r q) -> p r q", q=nq)
                    W3 = WWg[0:nq, 0:8 * nq].rearrange("p (r q) -> p r q", q=nq)
                    BH = BIGg[0:nq, 0:512].bitcast(BF16)
                    BL = BIGg[0:nq, 512:1024].bitcast(BF16)
                    for hl, BX in ((0, BH), (1, BL)):
                        S.add("dve" if (g == 1 and hl == 1) else "pool", tt(BX[:, 0:8 * nq].rearrange("p (r q) -> p r q", q=nq), TR[0:nq, 0:nq].unsqueeze(1).broadcast_to([nq, 8, nq]),
                                         DAHL[:, hl * 16 + 8 * g: hl * 16 + 8 * g + 8].unsqueeze(2).broadcast_to([nq, 8, nq]), ALU.mult),
                              reads=[("A", "dahl" + sfx)] + CK, writes=[("A", "big" + gs)])
                    br = alloc(nbk)
                    S.add("pe", mms([(PS[0:nq, br + i, :], [(SLB[0:nq, 0:nq], BH[:, i * 512:(i + 1) * 512]), (SLB[0:nq, 0:nq], BL[:, i * 512:(i + 1) * 512])])
                                     for i in range(nbk)]),
                          reads=[("A", "big" + gs)] + CK, writes=pk(br, nbk))
                    if nbk == 2:
                        cr3 = PS[0:nq, br:br + 2, :].rearrange("p a (r q) -> p (a r) q", q=nq)
                    else:
                        cr3 = PS[0:nq, br, :].rearrange("p (r q) -> p r q", q=nq)
                    grp_state.append((g, gs, B3, W3, br, cr3))
                for (g, gs, B3, W3, br, cr3) in grp_state:
                    S.add("act", act(B3, cr3, AF.Exp), reads=pk(br, nbk), writes=[("A", "big" + gs)])
                    if not smp:
                        S.add("dve", cp(DOUT[0:nq, 8 * g:8 * g + 8], B3[:, :, nq - 1]), reads=[("A", "big" + gs)], writes=[("A", "dout" + sfx)])
                    S.add("dve", tt(W3, B3, SM[0:nq, g, 0:nq].unsqueeze(1).broadcast_to([nq, 8, nq]), ALU.mult),
                          reads=[("A", "big" + gs), ("A", "sm" + sfx)], writes=[("A", "ww" + gs)])
                by = alloc(2, hold=True)
                for (g, gs, B3, W3, br, cr3) in grp_state:
                    def ydiag(g=g, W3=W3, by=by, nq=nq, XSD=XSD, XDT=XDT):
                        def fn(e):
                            e.matmul(PS[0:nq, by + g, :], IDB[0:nq, 0:nq], XSD[0:nq, g * 512:(g + 1) * 512], start=True, stop=False)
                            for r in range(8):
                                m_ = e.matmul(PS[0:nq, by + g, r * 64:(r + 1) * 64], W3[:, r, :],
                                              XDT[0:nq, (8 * g + r) * 64:(8 * g + r + 1) * 64], start=False, stop=(r == 7))
                            return m_
                        return fn
                    S.add("pe", ydiag(), reads=[("A", "ww" + gs), ("A", "xdt" + sfx), ("A", "xsd" + sfx)] + CK, writes=pk(by + g))
                sub(5)
                S.add("dve", tt(XDTD[0:nq, :].rearrange("p (r q) -> p r q", q=64), XDT[0:nq, :].rearrange("p (r q) -> p r q", q=64),
                                DOUT[0:nq, :].unsqueeze(2).broadcast_to([nq, 16, 64]), ALU.mult), reads=[("A", "xdt" + sfx), ("A", "dout" + sfx)], writes=[("A", "xdtd" + sfx)])
                bo = alloc(2, hold=True)
                if not smp:
                    S.add("pe", mms([(PS[0:nq, bo + g, :], [(MIX[:, 10 + g, q0:q0 + nq], HB[:, g * 512:(g + 1) * 512])]) for g in range(2)]),
                          reads=[mk(10), mk(11), ("hb",)], writes=pk(bo, 2))
                else:
                    S.add("dve", ts(SML[0:16, 10:14, :].rearrange("p a b -> p (a b)"), DTT[0:16, 0:64], ACOL[:, l:l + 1], 0.0, ALU.mult, ALU.add),
                          reads=[("A", "dtt", tg)] + CK, writes=[("A", "dat")])
                    S.add("dve", lambda e: e.tensor_reduce(out=TOTT[0:16, :], in_=SML[0:16, 10:14, :].rearrange("p a b -> p (a b)").rearrange("p (s q) -> p s q", q=4),
                                                            axis=AX.X, op=ALU.add), reads=[("A", "dat")], writes=[("A", "tott")])
                    bcd = alloc()
                    S.add("pe", mms([(PS[:, bcd, j * 16:(j + 1) * 16], [(E16[0:16, j * 128:(j + 1) * 128], TOTT[0:16, :])]) for j in range(8)]),
                          reads=[("A", "e16"), ("A", "tott")], writes=pk(bcd))
                    S.add("act", act(CDF[:, :, :], PS[:, bcd, 0:128].rearrange("p (j s) -> p j s", j=8), AF.Exp), reads=pk(bcd), writes=[("A", "cdf")])
                    hbufs = [(HST, ("hst",)), (H0X, ("A", "h0x")), (av(8192, [128, 1024]), ("A", "h0y")), (av(12288, [128, 1024]), ("A", "h0z"))]

                    def hload(sq_):
                        H0_, hkey_ = hbufs[sq_ % 4]
                        S.add("sp", dma(H0_[:, :].rearrange("p (j n) -> p j n", j=8), sssm[l, sq_].rearrange("(j p) n -> p j n", p=128)),
                              writes=[hkey_], chan="ld_h%d" % (sq_ % 4))
                    hload(0)
                    hload(1)
                    for sq in range(16):
                        H0, hkey = hbufs[sq % 4]
                        H03 = H0[:, :].rearrange("p (j n) -> p j n", j=8)
                        if sq + 2 < 16:
                            hload(sq + 2)
                        bt = alloc(2)
                        S.add("pe", mms([(PS[:, bt + j // 4, (j % 4) * 128:(j % 4 + 1) * 128], [(H03[:, j, :], IDF)]) for j in range(8)]),
                              reads=[hkey] + CK, writes=pk(bt, 2))
                        S.add("act", act(HB[:, :].rearrange("p (a b) -> p a b", a=2), PS[:, bt:bt + 2, :], AF.Copy), reads=pk(bt, 2), writes=[("hb",)])
                        cs_ = sq % 2
                        S.add("dve", ms(CMS[:, cs_, :, :], 0.0), writes=[("A", "cms", cs_)])
                        S.add("dve", cp(CMS[:, cs_, :, 4 * sq:4 * sq + 4], MIX[:, 10:12, 4 * sq:4 * sq + 4]), reads=[mk(10), mk(11)], writes=[("A", "cms", cs_)])
                        S.add("pe", (lambda sq=sq, cs_=cs_, bo=bo: (lambda e: [e.matmul(PS[0:64, bo + g, :], CMS[:, cs_, g, :], HB[:, g * 512:(g + 1) * 512],
                                                                                    start=(sq == 0), stop=(sq == 15)) for g in range(2)][-1]))(),
                              reads=[("A", "cms", cs_), ("hb",)], writes=pk(bo, 2))
                        S.add("dve", ts(BMS[0:64, cs_, :], BTOK[0:64, :], MASKCOL[0:64, sq:sq + 1], 0.0, ALU.mult, ALU.add),
                              reads=[("A", "btok" + sfx)] + CK, writes=[("A", "bms", cs_)])
                        bn = alloc(2)
                        S.add("pe", mms([(PS[:, bn + j // 4, (j % 4) * 128:(j % 4 + 1) * 128],
                                          [(XDTD[0:64, j * 128:(j + 1) * 128], BMS[0:64, cs_, (j // 4) * 128:(j // 4 + 1) * 128])]) for j in range(8)]),
                              reads=[("A", "xdtd" + sfx), ("A", "bms", cs_)], writes=pk(bn, 2))
                        S.add("dve", tt(H03, H03, CDF[:, :, sq].unsqueeze(2).broadcast_to([128, 8, 128]), ALU.mult), reads=[hkey, ("A", "cdf")], writes=[hkey])
                        S.add("dve", tt(H03, H03, PS[:, bn:bn + 2, :].rearrange("p a (j n) -> p (a j) n", n=128), ALU.add), reads=[hkey] + pk(bn, 2), writes=[hkey])
                        S.add("sp", dma(ssm_s_o[l, sq].rearrange("(j p) n -> p j n", p=128), H03), reads=[hkey], chan="st_h%d" % (sq % 4))
                if not smp:
                    bsp = alloc(2)
                    S.add("pe", mms([(PS[:, bsp + g, :], [(BTOK[:, g * 128:(g + 1) * 128], XDTD[:, g * 512:(g + 1) * 512])]) for g in range(2)]),
                          reads=[("A", "btok" + sfx), ("A", "xdtd" + sfx)], writes=pk(bsp, 2))
                    S.add("pool", tt(HST[:, :].rearrange("p (r q) -> p r q", q=64), HST[:, :].rearrange("p (r q) -> p r q", q=64),
                                    CDB[:, :].unsqueeze(2).broadcast_to([128, 16, 64]), ALU.mult), reads=[("hst",), ("A", "cdb" + sfx, 0), ("A", "cdb" + sfx, 1)], writes=[("hst",)])
                    S.add("dve", tt(HST[:, :].rearrange("p (a b) -> p a b", a=2), HST[:, :].rearrange("p (a b) -> p a b", a=2), PS[:, bsp:bsp + 2, :], ALU.add),
                          reads=[("hst",)] + pk(bsp, 2), writes=[("hst",)])
                    S.add("act", act(HB[:], HST[:], AF.Copy), reads=[("hst",)], writes=[("hb",)])
                sub(6)
                S.add("dve", tt(YT[0:nq, :].rearrange("p (r q) -> p r q", q=64), PS[0:nq, bo:bo + 2, :].rearrange("p a (r q) -> p (a r) q", q=64),
                                EXPA[0:nq, :].unsqueeze(2).broadcast_to([nq, 16, 64]), ALU.mult), reads=pk(bo, 2) + [("A", "expa" + sfx)], writes=[("A", "yt" + sfx)])
                release(bo, 2)
                S.add("dve", tt(YT[0:nq, :].rearrange("p (a b) -> p a b", a=2), YT[0:nq, :].rearrange("p (a b) -> p a b", a=2), PS[0:nq, by:by + 2, :], ALU.add),
                      reads=pk(by, 2) + [("A", "yt" + sfx)], writes=[("A", "yt" + sfx)])
                release(by, 2)
                yield
                bz = alloc(2)
                S.add("pe", mms([(PS[0:nq, bz + jj // 4, (jj % 4) * 128:(jj % 4 + 1) * 128], [(ZB[:, jj, q0:q0 + nq], IDB[:])]) for jj in range(8)]),
                      reads=[("A", "zb", c, tg) for c in range(8)] + CK, writes=pk(bz, 2))
                S.add("dve", tt(YT[0:nq, :].rearrange("p (a b) -> p a b", a=2), YT[0:nq, :].rearrange("p (a b) -> p a b", a=2), PS[0:nq, bz:bz + 2, :], ALU.mult),
                      reads=pk(bz, 2) + [("A", "yt" + sfx)], writes=[("A", "yt" + sfx)])
                S.add("dve", ms(SSQ[0:nq, 0:1], 0.0), writes=[("A", "ssq" + sfx)])
                S.add("act", act(GN[0:nq, :], YT[0:nq, :], AF.Square, accum_out=SSQ[0:nq, 0:1]), reads=[("A", "yt" + sfx), ("A", "ssq" + sfx)], writes=[("A", "gn" + sfx), ("A", "ssq" + sfx)])
                S.add("act", act(RSTD[0:nq, 0:1], SSQ[0:nq, 0:1], AF.Ln, bias=EPS[0:nq, 0:1], scale=1.0 / 1024.0), reads=[("A", "ssq" + sfx)] + CK, writes=[("A", "rstd" + sfx)])
                S.add("act", act(RSTD[0:nq, 0:1], RSTD[0:nq, 0:1], AF.Exp, scale=-0.5), reads=[("A", "rstd" + sfx)], writes=[("A", "rstd" + sfx)])
                S.add("act", act(GN[0:nq, :], YT[0:nq, :], AF.Copy, scale=RSTD[0:nq, 0:1]), reads=[("A", "yt" + sfx), ("A", "rstd" + sfx)], writes=[("A", "gn" + sfx)])
                sub(7)
                bk = alloc(2)
                S.add("pe", mms([(PS[:, bk + jj // 4, (jj % 4) * 128:(jj % 4) * 128 + nq], [(GN[0:nq, jj * 128:(jj + 1) * 128], IDB[0:nq, 0:nq])]) for jj in range(8)]),
                      reads=[("A", "gn" + sfx)] + CK, writes=pk(bk, 2))
                for a in range(2):
                    S.add("dve", tt(MIX[:, 4 * a:4 * a + 4, q0:q0 + nq], PS[:, bk + a, :].rearrange("p (j q) -> p j q", j=4)[:, :, 0:nq],
                                    P(l, 32 + 4 * a, 4).unsqueeze(2).broadcast_to([128, 4, nq]), ALU.mult),
                          reads=pk(bk + a) + CK, writes=[mk(c) for c in range(4 * a, 4 * a + 4)])
            prev = None
            for ci in range(nch):
                gcur = chunk_gen(ci)
                next(gcur)
                if prev is not None:
                    for _ in prev:
                        pass
                prev = gcur
            for _ in prev:
                pass
            if (not smp) and sidx == 1:
                YT = SETS[0][5]
                sfx = "0"
                bf = alloc(2)
                S.add("pe", mms([(PS[:, bf + j // 4, (j % 4) * 128:(j % 4 + 1) * 128], [(HST[:, j * 128:(j + 1) * 128], IDF)]) for j in range(8)]),
                      reads=[("hst",)] + CK, writes=pk(bf, 2))
                S.add("act", act(YT[:, :].rearrange("p (a b) -> p a b", a=2), PS[:, bf:bf + 2, :], AF.Copy), reads=pk(bf, 2), writes=[("A", "yt" + sfx)])
                S.add("sp", dma(ssm_p_o[l].rearrange("(j p) n -> p j n", p=128), YT[:, :].rearrange("p (j n) -> p j n", j=8)), reads=[("A", "yt" + sfx)], chan="st_ssmp")
            ckpt()
            for jb in range(4):
                s = ring_next()
                _, wv = colblock_load(s, w_out[l, 0:1024], jb * 256, 256, 8, "gu0", 0)
                for m in range(2):
                    j = jb * 2 + m
                    for (lc, n, t) in ltile:
                        b = alloc()
                        S.add("pe", mm(PS[:, b, 0:n], [(wv[:, k, m * 128:(m + 1) * 128], MIX[:, k, lc:lc + n]) for k in range(8)]),
                              reads=[("ring", s, "gu0")] + [("A", "mix", k, t) for k in range(8)], writes=pk(b))
                        g0 = TILES[t][0]
                        S.add("dve", tt(X[:, j, g0:g0 + n], X[:, j, g0:g0 + n], PS[:, b, 0:n], ALU.add), reads=pk(b) + [xk(j, t)], writes=[xk(j, t)])

        def ple(l):
            S.fence()
            for blk in range(17):
                r0 = blk * 128
                n = 128 if blk < 16 else 64
                t = blk // 4
                sl = blk % 2
                S.add("sp", dma(PSTG[0:n, sl, :], pin[l, r0:r0 + n, :]), writes=[("A", "pstg", sl)], chan="ld_p%d" % sl)
                b = alloc()
                S.add("pe", mms([(PS[:, b, k * 128:k * 128 + n], [(PSTG[0:n, sl, k * 128:(k + 1) * 128], IDF[0:n, 0:n])]) for k in range(2)]),
                      reads=[("A", "pstg", sl)] + CK, writes=pk(b))
                S.add("act", act(PT[:, :, r0:r0 + n], PS[:, b, 0:256].rearrange("p (k q) -> p k q", k=2)[:, :, 0:n], AF.Copy), reads=pk(b), writes=[("A", "pt", blk)])
            norm(range(5), l, 24, u_full, ukey)
            for jb in range(4):
                s = ring_next()
                _, wv = colblock_load(s, ple_gate[l], jb * 256, 256, 8, "gu0", 0)
                wload(s, "gu1", RING[:, s, 2048:2560].rearrange("p (k n) -> p k n", k=2),
                      ple_proj[l][:, jb * 256:(jb + 1) * 256].rearrange("(k p) n -> p k n", p=128))
                pv = RING[:, s, 2048:2560].rearrange("p (k n) -> p k n", k=2)
                for m in range(2):
                    j = jb * 2 + m
                    for t in range(5):
                        t0, n = TILES[t]
                        b = alloc(2)
                        S.add("pe", mms([(PS[:, b, 0:n], [(wv[:, k, m * 128:(m + 1) * 128], U[:, k, t0:t0 + n]) for k in range(8)]),
                                         (PS[:, b + 1, 0:n], [(pv[:, k, m * 128:(m + 1) * 128], PT[:, k, t0:t0 + n]) for k in range(2)])]),
                              reads=[("ring", s, "gu0"), ("ring", s, "gu1")] + [ukey(k, t) for k in range(8)] + [("A", "pt", bb_) for bb_ in range(17)], writes=pk(b, 2))
                        S.add("act", act(SGT[:, 0:n], PS[:, b, 0:n], AF.Sigmoid), reads=pk(b), writes=[("A", "sgt")])
                        S.add("dve", tt(TMP2[:, 0:n], SGT[:, 0:n], PS[:, b + 1, 0:n], ALU.mult), reads=[("A", "sgt")] + pk(b + 1), writes=[("A", "tmp2")])
                        S.add("dve", tt(X[:, j, t0:t0 + n], X[:, j, t0:t0 + n], TMP2[:, 0:n], ALU.add), reads=[("A", "tmp2"), xk(j, t)], writes=[xk(j, t)])

        SEGS = [("P", 0, 0, 1024, [0, 1]), ("P", 1, 1024, 1024, [2, 3]), ("S", 2, 2048, 64, [4])]
        try:
            ckpt()
            for l in range(nlayers):
                ffn(l, 0)
                ckpt()
                for seg in SEGS:
                    mixer(l, seg)
                    ckpt()
                ffn(l, 1)
                ckpt()
                ple(l)
                ckpt()
        except _Stop:
            pass
        S.fence()
        norm(range(5), None, 0, lambda k, t, n: X[:, k, TILES[t][0]:TILES[t][0] + n], lambda k, t: xk(k, t))
        S.fence()
        for blk in range(17):
            r0 = blk * 128
            n = 128 if blk < 16 else 64
            t = blk // 4
            sl = blk % 2
            b = alloc(2)
            S.add("pe", mms([(PS[0:n, b + j // 4, (j % 4) * 128:(j % 4 + 1) * 128], [(X[:, j, r0:r0 + n], IDF)]) for j in range(8)]),
                  reads=[xk(j, t) for j in range(8)] + CK, writes=pk(b, 2))
            if blk % 2 == 0:
                S.add("act", act(XSTG[0:n, sl, :].rearrange("p (a b) -> p a b", a=2), PS[0:n, b:b + 2, :], AF.Copy), reads=pk(b, 2), writes=[("A", "xstg", sl)])
            else:
                S.add("dve", cp(XSTG[0:n, sl, :].rearrange("p (a b) -> p a b", a=2), PS[0:n, b:b + 2, :]), reads=pk(b, 2), writes=[("A", "xstg", sl)])
            S.add("sp", dma(y_o[r0:r0 + n, :], XSTG[0:n, sl, :]), reads=[("A", "xstg", sl)], chan="st_y%d" % sl)
        S.emit(nc, es)
    return nc


def _host_consts():
    cst = np.zeros((128, NCST), np.float32)
    cst[:, 0:128] = np.eye(128, dtype=np.float32)
    s = np.arange(128)[:, None]
    q = np.arange(128)[None, :]
    cst[:, 128:256] = (s <= q).astype(np.float32)
    s6 = np.arange(64)[:, None]
    q6 = np.arange(64)[None, :]
    cst[0:64, 256:320] = ((s6 <= q6) & (s6 // 4 == q6 // 4)).astype(np.float32)
    cst[0:64, 320:384] = (s6 // 4 == q6 // 4).astype(np.float32)
    cst[0:64, 384:400] = (s6 // 4 == np.arange(16)[None, :]).astype(np.float32)
    for gi, w in enumerate((2, 4, 8, 16)):
        cst[:, 400 + gi * 16: 400 + (gi + 1) * 16] = (1.0 / np.minimum(np.arange(16) + 1, w)).astype(np.float32)[None, :]
    cst[:, 464:592] = (s > q).astype(np.float32)
    cst[0:64, 592:656] = ((s6 > q6) & (s6 // 4 == q6 // 4)).astype(np.float32)
    e16 = np.zeros((16, 1024), np.float32)
    for r in range(16):
        e16[r, r * 64:(r + 1) * 64] = 1.0
    return cst, e16


def _host_params(inp):
    prm = np.zeros((128, NPRM), np.float32)

    def cols(v):
        return np.asarray(v, np.float32).reshape(8, 128).T

    for l in range(L):
        b = l * PL
        prm[:, b + 0:b + 8] = cols(inp["norm_ffn1"][l])
        prm[:, b + 8:b + 16] = cols(inp["norm_mix"][l])
        prm[:, b + 16:b + 24] = cols(inp["norm_ffn2"][l])
        prm[:, b + 24:b + 32] = cols(inp["norm_ple"][l])
        prm[:, b + 32:b + 40] = cols(inp["ssd_norm_w"][l])
        prm[:, b + 40:b + 48] = cols(inp["pool_scale"][l])
        cw = np.asarray(inp["conv_w"][l], np.float32)
        prm[:, b + 48:b + 96] = cw.reshape(4, 12, 128).transpose(2, 1, 0).reshape(128, 48)
        prm[:, b + 96:b + 108] = np.asarray(inp["conv_b"][l], np.float32).reshape(12, 128).T
        prm[:, b + 108:b + 124] = np.asarray(inp["a_log"][l], np.float32)[None, :]
        prm[:, b + 124:b + 140] = np.asarray(inp["d_skip"][l], np.float32)[None, :]
        prm[0:16, b + 140] = np.asarray(inp["dt_bias"][l], np.float32)
        prm[0:16, b + 141] = np.asarray(inp["a_log"][l], np.float32)
    prm[:, L * PL:L * PL + 8] = cols(inp["final_norm"])
    return prm


_NC_CACHE = {}


def kernel(**inp):
    inp = {k: np.asarray(v) for k, v in inp.items()}
    nlayers = int(os.environ.get("MK_LAYERS", L))
    if nlayers not in _NC_CACHE:
        _NC_CACHE[nlayers] = build_nc(nlayers)
    nc = _NC_CACHE[nlayers]
    cst, e16 = _host_consts()
    prm = _host_params(inp)
    shared = {k: np.ascontiguousarray(inp[k], dtype=np.float32) for k in
              ("w_in", "w_out", "pool_w", "ffn1_gate", "ffn2_gate", "ffn1_up", "ffn2_up", "ffn1_down", "ffn2_down", "ple_gate", "ple_proj")}
    in_maps = []
    for c in range(NCORE):
        sl = slice(16 * c, 16 * c + 16)
        m = dict(shared)
        m["xin"] = np.ascontiguousarray(np.concatenate([inp["x_prompt"][c], inp["x_sample"][sl].reshape(NSM, D)], axis=0), dtype=np.float32)
        m["pin"] = np.ascontiguousarray(np.concatenate([inp["p_prompt"][:, c], inp["p_sample"][:, sl].reshape(L, NSM, 256)], axis=1), dtype=np.float32)
        m["sssm"] = np.ascontiguousarray(inp["state_ssm"][:, sl].reshape(L, 16, 1024, 128), dtype=np.float32)
        m["sconv"] = np.ascontiguousarray(inp["state_conv"][:, sl].reshape(L, 48, CONV), dtype=np.float32)
        m["spool"] = np.ascontiguousarray(inp["state_pool"][:, sl].reshape(L, 240, 1024), dtype=np.float32)
        m["prm"] = prm
        m["cst"] = cst
        m["e16"] = e16
        in_maps.append(m)
    ncr = int(os.environ.get("MK_CORES", NCORE))
    res = run_bass_kernel_spmd(nc, in_maps[:ncr], core_ids=list(range(ncr)))
    R = list(res.results) + [res.results[0]] * (NCORE - ncr)
    y_prompt = np.stack([R[c]["y"][0:NPR] for c in range(NCORE)], axis=0)
    y_sample = np.concatenate([R[c]["y"][NPR:].reshape(16, 4, D) for c in range(NCORE)], axis=0)
    ssm_prompt = np.stack([R[c]["ssm_p"].reshape(L, 16, 64, 128) for c in range(NCORE)], axis=1)
    conv_prompt = np.stack([R[c]["conv_p"] for c in range(NCORE)], axis=1)
    pool_prompt = np.stack([R[c]["pool_p"] for c in range(NCORE)], axis=1)
    ssm_sample = np.concatenate([R[c]["ssm_s"].reshape(L, 16, 16, 64, 128) for c in range(NCORE)], axis=1)
    conv_sample = np.concatenate([R[c]["conv_s"].reshape(L, 16, 3, CONV) for c in range(NCORE)], axis=1)
    pool_sample = np.concatenate([R[c]["pool_s"].reshape(L, 16, 15, 1024) for c in range(NCORE)], axis=1)
    f = lambda a: np.ascontiguousarray(a, dtype=np.float32)
    return (f(y_prompt), f(y_sample), f(ssm_prompt), f(conv_prompt), f(pool_prompt), f(ssm_sample), f(conv_sample), f(pool_sample))
```

```python
import os
from contextlib import ExitStack
from math import prod

import numpy as np
import concourse.bass as bass
import concourse.mybir as mybir
from concourse.bass_utils import run_bass_kernel_spmd

F32 = mybir.dt.float32
BF16 = mybir.dt.bfloat16
AF = mybir.ActivationFunctionType
ALU = mybir.AluOpType
AX = mybir.AxisListType

NCORE = 8
L = 4
D = 1024
NPR = 2048
NSM = 64
T = NPR + NSM
DFF = 2816
DIN = 3600
CONV = 1536
EPSV = 1e-6
TILES = [(0, 512), (512, 512), (1024, 512), (1536, 512), (2048, 64)]
PL = 142
NPRM = L * PL + 8
NCST = 656
ENGS = ("pe", "act", "dve", "pool", "sp")


class Sched:
    def __init__(self):
        self.ops = []
        self.last_writer = {}
        self.readers = {}
        self.chan_last = {}
        self.fence_deps = set()
        self.fenced = set()
        self.arena_touch = {}

    def add(self, eng, fn, reads=(), writes=(), chan=None):
        idx = len(self.ops)
        if idx >= int(os.environ.get("MK_MAXOPS", 10 ** 9)) and not getattr(self, "nolimit", False):
            self.nolimit = True
            print("MAXOPS stop at", idx, eng, reads, writes)
            raise _Stop()
        deps = set()
        touches = False
        if eng != "pe":
            writes = list(writes) + [k for k in reads if k[0] == "ps" and k not in writes]
        for k in list(reads) + list(writes):
            if k[0] == "A":
                touches = True
                if k not in self.fenced:
                    deps.update(self.fence_deps)
                    self.fenced.add(k)
        for k in reads:
            if k in self.last_writer:
                deps.add(self.last_writer[k])
        for k in writes:
            if k in self.last_writer:
                deps.add(self.last_writer[k])
            deps.update(self.readers.get(k, ()))
        if chan is not None and chan in self.chan_last:
            deps.add(self.chan_last[chan])
        deps.discard(idx)
        for k in reads:
            self.readers.setdefault(k, []).append(idx)
        for k in writes:
            self.last_writer[k] = idx
            self.readers[k] = []
        if chan is not None:
            self.chan_last[chan] = idx
        if touches:
            self.arena_touch[chan if chan is not None else eng] = idx
        self.ops.append(dict(eng=eng, fn=fn, deps=deps, chan=chan, signal=False))
        return idx

    def fence(self):
        allp = set(self.fence_deps) | set(self.arena_touch.values())
        best = {}
        for i in allp:
            o = self.ops[i]
            k = o["chan"] if o["chan"] is not None else o["eng"]
            if k not in best or best[k] < i:
                best[k] = i
        self.fence_deps = set(best.values())
        self.fenced = set()
        self.arena_touch = {}
        for k in [k for k in self.last_writer if k[0] == "A"]:
            del self.last_writer[k]
        for k in [k for k in self.readers if k[0] == "A"]:
            del self.readers[k]

    def emit(self, nc, es):
        ops = self.ops
        for o in ops:
            nd = set()
            for d in o["deps"]:
                p = ops[d]
                if o["eng"] == "pe" and p["eng"] == "pe" and p["chan"] is None and o["chan"] is None:
                    continue
                nd.add(d)
                p["signal"] = True
            o["deps"] = nd
        chans = sorted({o["chan"] for o in ops if o["chan"] is not None})
        sems = {}
        for e in ENGS:
            sems[e] = es.enter_context(nc.semaphore("s_" + e))
        for c in chans:
            sems[c] = es.enter_context(nc.semaphore("c_" + c))
        cnt = {k: 0 for k in sems}
        for o in ops:
            if o["chan"] is not None:
                cnt[o["chan"]] += 16
                o["tick"] = (o["chan"], cnt[o["chan"]])
            elif o["signal"]:
                cnt[o["eng"]] += 1
                o["tick"] = (o["eng"], cnt[o["eng"]])
        block = es.enter_context(nc.Block())

        def run(engname, e):
            waited = {}
            for o in ops:
                if o["eng"] != engname:
                    continue
                need = {}
                for d in o["deps"]:
                    s, v = ops[d]["tick"]
                    if need.get(s, 0) < v:
                        need[s] = v
                for s, v in need.items():
                    if waited.get(s, 0) < v:
                        e.wait_ge(sems[s], v)
                        waited[s] = v
                ins = o["fn"](e)
                if o["chan"] is not None:
                    ins.then_inc(sems[o["chan"]], 16)
                elif o["signal"]:
                    ins.then_inc(sems[o["eng"]], 1)
            if engname == "sp":
                for s, v in cnt.items():
                    if v > 0 and waited.get(s, 0) < v:
                        e.wait_ge(sems[s], v)

        @block.tensor
        def _(e):
            run("pe", e)

        @block.scalar
        def _(e):
            run("act", e)

        @block.vector
        def _(e):
            run("dve", e)

        @block.gpsimd
        def _(e):
            run("pool", e)

        @block.sync
        def _(e):
            run("sp", e)


class _Stop(Exception):
    pass


def build_nc(nlayers=L):
    stop_at = int(os.environ.get("MK_STOP", 999))
    stage = [0]

    def ckpt():
        stage[0] += 1
        if stage[0] >= stop_at:
            raise _Stop()

    sub_at = int(os.environ.get("MK_SUB", 999))

    def sub(k):
        if k >= sub_at:
            raise _Stop()

    nc = bass.Bass("TRN2", target_bir_lowering=False)
    S = Sched()
    es = ExitStack()

    def din(name, shape):
        return nc.dram_tensor(name, list(shape), F32, kind="ExternalInput").ap()

    def dout(name, shape):
        return nc.dram_tensor(name, list(shape), F32, kind="ExternalOutput").ap()

    xin = din("xin", [T, D])
    pin = din("pin", [L, T, 256])
    sssm = din("sssm", [L, 16, 1024, 128])
    sconv = din("sconv", [L, 48, CONV])
    spool = din("spool", [L, 240, 1024])
    w_in = din("w_in", [L, D, DIN])
    w_out = din("w_out", [L, 2048, D])
    pool_w = din("pool_w", [L, 4, 256, 256])
    fg = [din("ffn1_gate", [L, D, DFF]), din("ffn2_gate", [L, D, DFF])]
    fu = [din("ffn1_up", [L, D, DFF]), din("ffn2_up", [L, D, DFF])]
    fd = [din("ffn1_down", [L, DFF, D]), din("ffn2_down", [L, DFF, D])]
    ple_gate = din("ple_gate", [L, D, D])
    ple_proj = din("ple_proj", [L, 256, D])
    prm_d = din("prm", [128, NPRM])
    cst_d = din("cst", [128, NCST])
    e16_d = din("e16", [16, 1024])

    y_o = dout("y", [T, D])
    ssm_p_o = dout("ssm_p", [L, 1024, 128])
    conv_p_o = dout("conv_p", [L, 3, CONV])
    pool_p_o = dout("pool_p", [L, 15, 1024])
    ssm_s_o = dout("ssm_s", [L, 16, 1024, 128])
    conv_s_o = dout("conv_s", [L, 48, CONV])
    pool_s_o = dout("pool_s", [L, 240, 1024])

    with es:
        def sb(name, shape, dt=F32):
            return es.enter_context(nc.sbuf_tensor(name, list(shape), dt))

        X = sb("X", [128, 8, T])
        RING = sb("RING", [128, 2, 6144], BF16)
        ARENA_B = 92928
        ARENA = sb("ARENA", [128, ARENA_B // 4])
        HST = sb("HST", [128, 1024])
        HB = sb("HB", [128, 1024], BF16)
        HISTC = sb("HISTC", [128, 12, 3])
        HISTP = sb("HISTP", [128, 8, 15])
        PRM = sb("PRM", [128, NPRM])
        CST = sb("CST", [128, NCST])
        IDB = sb("IDB", [128, 128], BF16)
        ONESB = sb("ONESB", [128, 128], BF16)
        ONESF = sb("ONESF", [128, 128])
        EPS = sb("EPS", [128, 1])
        STRB = sb("STRB", [128, 128], BF16)
        STRSEGB = sb("STRSEGB", [128, 64], BF16)
        A_B = sb("A_B", [128, L, 16])
        ACOL = sb("ACOL", [16, L])
        PS = es.enter_context(nc.psum_tensor("PS", [128, 8, 512], F32))

        IDF = CST[:, 0:128]
        TRI = CST[:, 128:256]
        TRISEG = CST[:, 256:320]
        BLK = CST[:, 320:384]
        MASKCOL = CST[:, 384:400]
        RC = CST[:, 400:464]
        STRICT = CST[:, 464:592]
        STRICTSEG = CST[:, 592:656]

        def av(off_b, shape, dt=F32):
            nel = prod(shape[1:])
            nb = nel * (4 if dt == F32 else 2)
            assert off_b % 4 == 0 and nb % 4 == 0 and off_b + nb <= ARENA_B, (off_b, shape)
            v = ARENA[0:shape[0], off_b // 4: (off_b + nb) // 4]
            if dt == BF16:
                v = v.bitcast(BF16)
            if len(shape) == 3:
                v = v.rearrange("p (a b) -> p a b", a=shape[1])
            elif len(shape) == 4:
                v = v.rearrange("p (a b c) -> p a b c", a=shape[1], b=shape[2])
            return v

        U = av(0, [128, 8, T], BF16)
        HH = av(33792, [128, 2, 2, T], BF16)
        SG = av(50688, [128, 2, 512], BF16)
        SQ = av(62976, [128, 8, 512], BF16)
        RS = av(71168, [128, 512])
        PT = av(73216, [128, 2, T], BF16)
        SGT = av(81664, [128, 512])
        TMP2 = av(83712, [128, 512])
        PSTG = av(85760, [128, 2, 256])
        XSTG = av(0, [128, 2, 1024])
        US = av(0, [128, 8, 1024], BF16)
        MIX = av(16384, [128, 12, 1024], BF16)
        ZB = av(40960, [128, 8, 1024], BF16)
        DTT = av(57344, [16, 1024])
        SC = 61440
        PA = [av(SC, [128, 1040]), av(SC + 4160, [128, 1040])]
        PP1 = av(SC + 8320, [128, 1040])
        PP2 = av(SC + 12480, [128, 1040])
        POUT = av(SC + 16640, [128, 1024])
        PSTGC = av(SC + 20736, [128, 2, 256])
        PASTP = av(SC + 22784, [128, 2, 240])
        SNEW = av(SC + 24704, [128, 2, 64])
        TMP16 = av(SC + 25216, [128, 16])
        RAW = [av(SC, [128, 1028]), av(SC + 4112, [128, 1028])]
        ACC = av(SC + 8224, [128, 1024])
        CSTG = av(SC + 12320, [128, 1536])
        PASTT = av(SC + 18464, [128, 12, 48])
        SELC = av(SC + 20768, [128, 2, 48])
        BIG = av(SC, [128, 1024])
        WW = av(SC + 4096, [128, 1024], BF16)
        XDT = av(SC + 6144, [128, 1024], BF16)
        XDTD = av(SC + 8192, [128, 1024], BF16)
        XSD = av(SC + 10240, [128, 1024], BF16)
        YT = av(SC + 12288, [128, 1024])
        GN = av(SC + 16384, [128, 1024], BF16)
        BTOK = av(SC + 18432, [128, 256], BF16)
        SM = av(SC + 18944, [128, 2, 128])
        SML = av(SC + 19968, [128, 16, 16])
        H0X = av(SC + 20992, [128, 1024])
        CMS = av(SC + 25088, [128, 2, 2, 64], BF16)
        BMS = av(SC + 25600, [128, 2, 256], BF16)
        CDF = av(SC + 26624, [128, 8, 16])
        E16 = av(SC + 27136, [16, 1024])
        SETS = [
            (BIG, WW, XDT, XDTD, XSD, YT, GN, BTOK, SM, SML),
            (av(0, [128, 1024]), av(4096, [128, 1024], BF16), av(6144, [128, 1024], BF16), av(8192, [128, 1024], BF16),
             av(10240, [128, 1024], BF16), av(12288, [128, 1024]), av(SC + 20992, [128, 1024], BF16), av(SC + 23040, [128, 256], BF16),
             av(SC + 23552, [128, 2, 128]), av(SC + 24576, [128, 16, 16])),
        ]

        bank_cur = [0]

        held = set()

        def alloc(n=1, hold=False):
            c = bank_cur[0]
            for _ in range(32):
                if n > 1 and c % n:
                    c += n - c % n
                if c + n > 8:
                    c = 0
                if all((c + i) not in held for i in range(n)):
                    break
                c += 1
            else:
                raise RuntimeError("no free psum banks")
            bank_cur[0] = (c + n) % 8
            if hold:
                for i in range(n):
                    held.add(c + i)
            return c

        def release(b, n=1):
            for i in range(n):
                held.discard(b + i)

        def pk(b, n=1):
            return [("ps", b + i) for i in range(n)]

        ring_cur = [0]
        dma_flip = [0]

        def P(l, off, n=1):
            return PRM[:, l * PL + off: l * PL + off + n]

        def mm(out, pairs):
            def fn(e):
                n = len(pairs)
                for i, (a, b) in enumerate(pairs):
                    m = e.matmul(out, a, b, start=(i == 0), stop=(i == n - 1))
                return m
            return fn

        def mms(groups):
            def fn(e):
                for out, pairs in groups:
                    n = len(pairs)
                    for i, (a, b) in enumerate(pairs):
                        m = e.matmul(out, a, b, start=(i == 0), stop=(i == n - 1))
                return m
            return fn

        def act(out, in_, func, **kw):
            return lambda e: e.activation(out=out, in_=in_, func=func, **kw)

        def tt(out, a, b, op):
            return lambda e: e.tensor_tensor(out=out, in0=a, in1=b, op=op)

        def ts(out, a, s1, s2, op0, op1):
            return lambda e: e.tensor_scalar(out=out, in0=a, scalar1=s1, scalar2=s2, op0=op0, op1=op1)

        def stt(out, a, s, b, op0, op1):
            return lambda e: e.scalar_tensor_tensor(out=out, in0=a, scalar=s, in1=b, op0=op0, op1=op1)

        def cp(out, in_):
            return lambda e: e.tensor_copy(out, in_)

        def ms(ap, v):
            return lambda e: e.memset(ap, v)

        def dma(out, in_):
            return lambda e: e.dma_start(out=out, in_=in_)

        def xk(j, t):
            return ("x", j, t)

        S.add("sp", dma(PRM[:], prm_d[:, :]), writes=[("prm",)], chan="ld_prm")
        S.add("sp", dma(CST[:], cst_d[:, :]), writes=[("cst",)], chan="ld_cst")
        S.add("dve", cp(IDB[:], IDF), reads=[("cst",)], writes=[("idb",)])
        S.add("dve", ms(ONESB[:], 1.0 / 1024.0), writes=[("onesb",)])
        S.add("dve", cp(STRB[:], CST[:, 464:592]), reads=[("cst",)], writes=[("strb",)])
        S.add("dve", cp(STRSEGB[0:64, :], CST[0:64, 592:656]), reads=[("cst",)], writes=[("strsegb",)])
        S.add("dve", ms(ONESF[:], 1.0), writes=[("onesf",)])
        S.add("dve", ms(EPS[:], EPSV), writes=[("eps",)])
        for l in range(L):
            S.add("act", act(A_B[:, l, :], P(l, 108, 16), AF.Exp), reads=[("prm",)], writes=[("ab0", l)])
            S.add("dve", ts(A_B[:, l, :], A_B[:, l, :], -1.0, 0.0, ALU.mult, ALU.add), reads=[("ab0", l)], writes=[("ab",)])
            S.add("act", act(ACOL[:, l:l + 1], PRM[0:16, l * PL + 141: l * PL + 142], AF.Exp), reads=[("prm",)], writes=[("ac0", l)])
            S.add("dve", ts(ACOL[:, l:l + 1], ACOL[:, l:l + 1], -1.0, 0.0, ALU.mult, ALU.add), reads=[("ac0", l)], writes=[("acol",)])
        CK = [("cst",), ("idb",), ("onesb",), ("onesf",), ("eps",), ("ab",), ("acol",), ("prm",), ("strb",), ("strsegb",)]

        for blk in range(17):
            r0 = blk * 128
            n = 128 if blk < 16 else 64
            t = blk // 4
            sl = blk % 2
            S.add("sp", dma(XSTG[0:n, sl, :], xin[r0:r0 + n, :]), writes=[("A", "xstg", sl)], chan="ld_x%d" % sl)
            for hb in range(2):
                b = alloc()
                S.add("pe", mms([(PS[:, b, jj * 128: jj * 128 + n],
                                  [(XSTG[0:n, sl, (hb * 4 + jj) * 128:(hb * 4 + jj + 1) * 128], IDF[0:n, 0:n])]) for jj in range(4)]),
                      reads=[("A", "xstg", sl)] + CK, writes=pk(b))
                src = PS[:, b, :].rearrange("p (j q) -> p j q", j=4)[:, :, 0:n]
                dst = X[:, hb * 4:hb * 4 + 4, r0:r0 + n]
                if (blk + hb) % 2 == 0:
                    S.add("act", act(dst, src, AF.Copy), reads=pk(b), writes=[xk(hb * 4 + jj, t) for jj in range(4)])
                else:
                    S.add("dve", cp(dst, src), reads=pk(b), writes=[xk(hb * 4 + jj, t) for jj in range(4)])
        S.fence()

        def norm(tiles, wl, woff, dst_fn, dkey_fn):
            for t in tiles:
                t0, n = TILES[t]
                S.add("act", act(SQ[:, :, 0:n], X[:, :, t0:t0 + n], AF.Square),
                      reads=[xk(j, t) for j in range(8)], writes=[("A", "sq")])
                b = alloc()
                S.add("pe", mm(PS[:, b, 0:n], [(ONESB[:], SQ[:, k, 0:n]) for k in range(8)]),
                      reads=[("A", "sq")] + CK, writes=pk(b))
                S.add("act", act(RS[:, 0:n], PS[:, b, 0:n], AF.Ln, bias=EPS[:, 0:1]), reads=pk(b) + CK, writes=[("A", "rs")])
                S.add("act", act(RS[:, 0:n], RS[:, 0:n], AF.Exp, scale=-0.5), reads=[("A", "rs")], writes=[("A", "rs")])
                for k in range(8):
                    wcol = PRM[:, wl * PL + woff + k: wl * PL + woff + k + 1] if wl is not None else PRM[:, L * PL + k: L * PL + k + 1]
                    S.add("dve", stt(dst_fn(k, t, n), X[:, k, t0:t0 + n], wcol, RS[:, 0:n], ALU.mult, ALU.mult),
                          reads=[xk(k, t), ("A", "rs")] + CK, writes=[dkey_fn(k, t)])

        def u_full(k, t, n):
            return U[:, k, TILES[t][0]:TILES[t][0] + n]

        def ukey(k, t):
            return ("A", "u", k, t)

        def ring_next():
            s = ring_cur[0]
            ring_cur[0] = 1 - s
            return s

        def wload(slot, part, view, src):
            S.add("pool", dma(view, src), writes=[("ring", slot, part)], chan="r%d%s" % (slot, part))

        def ffn(l, which):
            S.fence()
            norm(range(5), l, 0 if which == 0 else 16, u_full, ukey)
            NG = DFF // 256
            gd, ud, dd = fg[which], fu[which], fd[which]
            slots = {}

            def load(g):
                s = ring_next()
                slots[g] = s
                wload(s, "gu0", RING[:, s, 0:2048].rearrange("p (k n) -> p k n", k=8),
                      gd[l, :, g * 256:(g + 1) * 256].rearrange("(k p) n -> p k n", p=128))
                wload(s, "gu1", RING[:, s, 2048:4096].rearrange("p (k n) -> p k n", k=8),
                      ud[l, :, g * 256:(g + 1) * 256].rearrange("(k p) n -> p k n", p=128))
                wload(s, "d", RING[:, s, 4096:6144].rearrange("p (f n) -> p f n", f=2),
                      dd[l, g * 256:(g + 1) * 256, :].rearrange("(f p) n -> p f n", p=128))

            def gu(g):
                s = slots[g]
                hs = g % 2
                wg = RING[:, s, 0:2048].rearrange("p (k n) -> p k n", k=8)
                wu = RING[:, s, 2048:4096].rearrange("p (k n) -> p k n", k=8)
                for f in range(2):
                    for t in range(5):
                        t0, n = TILES[t]
                        b = alloc(2)
                        S.add("pe", mms([(PS[:, b, 0:n], [(wg[:, k, f * 128:(f + 1) * 128], U[:, k, t0:t0 + n]) for k in range(8)]),
                                         (PS[:, b + 1, 0:n], [(wu[:, k, f * 128:(f + 1) * 128], U[:, k, t0:t0 + n]) for k in range(8)])]),
                              reads=[("ring", s, "gu0"), ("ring", s, "gu1")] + [ukey(k, t) for k in range(8)], writes=pk(b, 2))
                        sgs = (f * 5 + t) % 2
                        S.add("act", act(SG[:, sgs, 0:n], PS[:, b, 0:n], AF.Silu), reads=pk(b), writes=[("A", "sg", sgs)])
                        S.add("dve", tt(HH[:, hs, f, t0:t0 + n], SG[:, sgs, 0:n], PS[:, b + 1, 0:n], ALU.mult),
                              reads=[("A", "sg", sgs)] + pk(b + 1), writes=[("A", "h", hs, f, t)])

            def down(g):
                s = slots[g]
                hs = g % 2
                wd = RING[:, s, 4096:6144].rearrange("p (f n) -> p f n", f=2)
                for j in range(8):
                    for t in range(5):
                        t0, n = TILES[t]
                        b = alloc()
                        S.add("pe", mm(PS[:, b, 0:n], [(wd[:, f, j * 128:(j + 1) * 128], HH[:, hs, f, t0:t0 + n]) for f in range(2)]),
                              reads=[("ring", s, "d")] + [("A", "h", hs, f, t) for f in range(2)], writes=pk(b))
                        S.add("dve", stt(X[:, j, t0:t0 + n], PS[:, b, 0:n], 0.5, X[:, j, t0:t0 + n], ALU.mult, ALU.add),
                              reads=pk(b) + [xk(j, t)], writes=[xk(j, t)])

            load(0)
            load(1)
            gu(0)
            for g in range(NG):
                if g + 1 < NG:
                    gu(g + 1)
                down(g)
                if g + 2 < NG:
                    load(g + 2)

        def colblock_load(s, src2d, col0, ncols, kc, part="gu0", off=0):
            view = RING[:, s, off:off + kc * ncols].rearrange("p (k n) -> p k n", k=kc)
            wload(s, part, view, src2d[:, col0:col0 + ncols].rearrange("(k p) n -> p k n", p=128))
            return s, view

        def mixer(l, seg):
            kind, sidx, s0, ns, tiles = seg
            smp = kind == "S"
            nseq, ntok = (16, 4) if smp else (1, 1024)
            ltile = [(TILES[t][0] - s0, TILES[t][1], t) for t in tiles]

            def us(k, t, n):
                return US[:, k, TILES[t][0] - s0: TILES[t][0] - s0 + n]

            def uskey(k, t):
                return ("A", "us", k, t)

            S.fence()
            norm(tiles, l, 8, us, uskey)
            ckpt()
            S.fence()
            if smp:
                S.add("sp", dma(pool_s_o[l].rearrange("(s r) f -> s r f", r=15)[:, 0:11, :],
                                spool[l].rearrange("(s r) f -> s r f", r=15)[:, 4:15, :]), chan="st_pool_d2d")
            for cb in range(4):
                wwin = 2 << cb
                s = ring_next()
                _, wv = colblock_load(s, w_in[l], 2576 + cb * 256, 256, 8, "gu0", 0)
                wload(s, "gu1", RING[:, s, 2048:2560].rearrange("p (k n) -> p k n", k=2),
                      pool_w[l, cb].rearrange("(k p) n -> p k n", p=128))
                pwv = RING[:, s, 2048:2560].rearrange("p (k n) -> p k n", k=2)
                for m in range(2):
                    c = cb * 2 + m
                    A = PA[m]
                    A3 = A[:, 0:nseq * (16 + ntok)].rearrange("p (s q) -> p s q", s=nseq)
                    akey = ("A", "pa", m)
                    if smp:
                        S.add("sp", dma(PSTGC[0:120, :, 0:128],
                                        spool[l, :, c * 128:(c + 1) * 128].rearrange("(a p) f -> p a f", p=120)),
                              writes=[("A", "pstgc")], chan="ld_sp")
                        b = alloc()
                        S.add("pe", mms([(PS[:, b, a * 120:(a + 1) * 120], [(PSTGC[0:120, a, 0:128], IDF[0:120, 0:120])]) for a in range(2)]),
                              reads=[("A", "pstgc")] + CK, writes=pk(b))
                        S.add("dve", cp(A3[:, :, 1:16], PS[:, b, 0:240].rearrange("p (s r) -> p s r", r=15)), reads=pk(b), writes=[akey])
                    elif sidx == 0:
                        S.add("dve", ms(A[:, 0:16], 0.0), writes=[akey])
                    else:
                        S.add("dve", cp(A[:, 1:16], HISTP[:, c, :]), reads=[("histp", c)], writes=[akey])
                    for (lc, n, t) in ltile:
                        b = alloc()
                        S.add("pe", mm(PS[:, b, 0:n], [(wv[:, k, m * 128:(m + 1) * 128], us(k, t, n)) for k in range(8)]),
                              reads=[("ring", s, "gu0")] + [uskey(k, t) for k in range(8)], writes=pk(b))
                        if smp:
                            S.add("act", act(A3[:, :, 16:20], PS[:, b, 0:64].rearrange("p (s q) -> p s q", q=4), AF.Copy), reads=pk(b), writes=[akey])
                            S.add("act", act(SNEW[:, m, :], PS[:, b, 0:64], AF.Copy), reads=pk(b), writes=[("A", "snew", m)])
                        else:
                            S.add("act", act(A[:, 16 + lc:16 + lc + n], PS[:, b, 0:n], AF.Copy), reads=pk(b), writes=[akey])
                    LL = 16 + ntok
                    if smp:
                        if m == 0 and cb % 2 == 0:
                            pob = alloc(1, hold=True)
                        S.add("pe", mm(PS[0:64, pob, (c % 4) * 128:(c % 4 + 1) * 128], [(SNEW[:, m, :], IDF)]),
                              reads=[("A", "snew", m)] + CK, writes=pk(pob))
                    elif sidx == 0:
                        S.add("dve", cp(HISTP[:, c, :], A[:, LL - 15:LL]), reads=[akey], writes=[("histp", c)])
                    else:
                        if m == 0 and cb % 2 == 0:
                            pob = alloc(1, hold=True)
                        S.add("pe", mm(PS[0:15, pob, (c % 4) * 128:(c % 4 + 1) * 128], [(A[:, LL - 15:LL], IDF)]),
                              reads=[akey] + CK, writes=pk(pob))
                    if (smp or sidx == 1) and c % 4 == 3:
                        nr = 64 if smp else 15
                        hh = c // 4
                        S.add("act", act(POUT[0:nr, hh * 512:(hh + 1) * 512], PS[0:nr, pob, :], AF.Copy), reads=pk(pob), writes=[("A", "pout", hh)])
                        release(pob)
                        if c == 7:
                            if smp:
                                for sq in range(16):
                                    S.add("sp", dma(pool_s_o[l, sq * 15 + 11: sq * 15 + 15, :], POUT[sq * 4: sq * 4 + 4, :]),
                                          reads=[("A", "pout", 0), ("A", "pout", 1)], chan="st_pool%d" % (sq % 4))
                            else:
                                S.add("sp", dma(pool_p_o[l], POUT[0:15, :]), reads=[("A", "pout", 0), ("A", "pout", 1)], chan="st_pool0")
                    cur, curkey = A3, akey
                    bufs = [(PP1, ("A", "pp1")), (PP2, ("A", "pp2"))]
                    sh = 1
                    lvl = 0
                    while sh < wwin:
                        lo = 2 * sh - 1
                        nb_, nk = bufs[lvl % 2]
                        nb3 = nb_[:, 0:nseq * LL].rearrange("p (s q) -> p s q", s=nseq)
                        S.add("dve", tt(nb3[:, :, lo:LL], cur[:, :, lo:LL], cur[:, :, lo - sh:LL - sh], ALU.add), reads=[curkey], writes=[nk])
                        cur, curkey = nb3, nk
                        sh *= 2
                        lvl += 1
                    dst = MIX[:, c, 0:ns].rearrange("p (s q) -> p s q", s=nseq)
                    mkeys = [("A", "mix", c, t) for (_, _, t) in ltile]
                    S.add("dve", stt(dst, cur[:, :, 16:LL], 1.0 / wwin, A3[:, :, 16:LL], ALU.mult, ALU.subtract),
                          reads=[curkey, akey], writes=mkeys)
                    if (not smp) and sidx == 0:
                        S.add("dve", tt(TMP16[:, :], cur[:, 0, 16:32], RC[:, cb * 16:(cb + 1) * 16], ALU.mult), reads=[curkey] + CK, writes=[("A", "tmp16")])
                        S.add("dve", tt(MIX[:, c, 0:16], TMP16[:, :], A[:, 16:32], ALU.subtract), reads=[("A", "tmp16"), akey], writes=mkeys[0:1])
                for (lc, n, t) in ltile:
                    b = alloc(2)
                    S.add("pe", mms([(PS[:, b + m, 0:n], [(pwv[:, k, m * 128:(m + 1) * 128], MIX[:, cb * 2 + k, lc:lc + n]) for k in range(2)]) for m in range(2)]),
                          reads=[("ring", s, "gu1")] + [("A", "mix", cb * 2 + k, t) for k in range(2)], writes=pk(b, 2))
                    for m in range(2):
                        c = cb * 2 + m
                        S.add("act", act(MIX[:, c, lc:lc + n], PS[:, b + m, 0:n], AF.Copy, scale=P(l, 40 + c)),
                              reads=[("ps", b + m)] + CK, writes=[("A", "mix", c, t)])
            for jb in range(4):
                s = ring_next()
                _, wv = colblock_load(s, w_out[l, 1024:2048], jb * 256, 256, 8, "gu0", 0)
                for m in range(2):
                    j = jb * 2 + m
                    for (lc, n, t) in ltile:
                        b = alloc()
                        S.add("pe", mm(PS[:, b, 0:n], [(wv[:, k, m * 128:(m + 1) * 128], MIX[:, k, lc:lc + n]) for k in range(8)]),
                              reads=[("ring", s, "gu0")] + [("A", "mix", k, t) for k in range(8)], writes=pk(b))
                        g0 = TILES[t][0]
                        S.add("dve", tt(X[:, j, g0:g0 + n], X[:, j, g0:g0 + n], PS[:, b, 0:n], ALU.add), reads=pk(b) + [xk(j, t)], writes=[xk(j, t)])
            ckpt()
            S.fence()
            if smp:
                S.add("sp", dma(CSTG[0:48, :], sconv[l]), writes=[("A", "cstg")], chan="ld_sc")
                for q4 in range(3):
                    b = alloc()
                    S.add("pe", mms([(PS[:, b, jj * 48:(jj + 1) * 48], [(CSTG[0:48, (q4 * 4 + jj) * 128:(q4 * 4 + jj + 1) * 128], IDF[0:48, 0:48])]) for jj in range(4)]),
                          reads=[("A", "cstg")] + CK, writes=pk(b))
                    S.add("dve", cp(PASTT[:, q4 * 4:q4 * 4 + 4, :], PS[:, b, 0:192].rearrange("p (j q) -> p j q", j=4)), reads=pk(b), writes=[("A", "pastt", q4)])
            LR = 3 + ntok
            for cb in range(6):
                s = ring_next()
                _, wv = colblock_load(s, w_in[l], 1024 + cb * 256, 256, 8, "gu0", 0)
                for m in range(2):
                    c = cb * 2 + m
                    R = RAW[m]
                    R3 = R[:, 0:nseq * LR].rearrange("p (s q) -> p s q", s=nseq)
                    rkey = ("A", "raw", m)
                    if smp:
                        S.add("dve", cp(R3[:, :, 0:3], PASTT[:, c, :].rearrange("p (s r) -> p s r", r=3)), reads=[("A", "pastt", c // 4)], writes=[rkey])
                    elif sidx == 0:
                        S.add("dve", ms(R[:, 0:3], 0.0), writes=[rkey])
                    else:
                        S.add("dve", cp(R[:, 0:3], HISTC[:, c, :]), reads=[("histc", c)], writes=[rkey])
                    for (lc, n, t) in ltile:
                        b = alloc()
                        S.add("pe", mm(PS[:, b, 0:n], [(wv[:, k, m * 128:(m + 1) * 128], us(k, t, n)) for k in range(8)]),
                              reads=[("ring", s, "gu0")] + [uskey(k, t) for k in range(8)], writes=pk(b))
                        if smp:
                            S.add("act", act(R3[:, :, 3:7], PS[:, b, 0:64].rearrange("p (s q) -> p s q", q=4), AF.Copy), reads=pk(b), writes=[rkey])
                        else:
                            S.add("act", act(R[:, 3 + lc:3 + lc + n], PS[:, b, 0:n], AF.Copy), reads=pk(b), writes=[rkey])
                    if smp:
                        S.add("dve", cp(SELC[:, m, :].rearrange("p (s r) -> p s r", r=3), R3[:, :, 4:7]), reads=[rkey], writes=[("A", "selc", m)])
                        if c % 4 == 0:
                            cob = alloc(1, hold=True)
                        S.add("pe", mm(PS[0:48, cob, (c % 4) * 128:(c % 4 + 1) * 128], [(SELC[:, m, :], IDF)]), reads=[("A", "selc", m)] + CK, writes=pk(cob))
                    elif sidx == 0:
                        S.add("dve", cp(HISTC[:, c, :], R[:, LR - 3:LR]), reads=[rkey], writes=[("histc", c)])
                    else:
                        if c % 4 == 0:
                            cob = alloc(1, hold=True)
                        S.add("pe", mm(PS[0:3, cob, (c % 4) * 128:(c % 4 + 1) * 128], [(R[:, LR - 3:LR], IDF)]), reads=[rkey] + CK, writes=pk(cob))
                    if (smp or sidx == 1) and c % 4 == 3:
                        nr = 48 if smp else 3
                        q4 = c // 4
                        S.add("act", act(CSTG[0:nr, q4 * 512:(q4 + 1) * 512], PS[0:nr, cob, :], AF.Copy), reads=pk(cob), writes=[("A", "cstg")])
                        release(cob)
                        if c == 11:
                            S.add("sp", dma((conv_s_o if smp else conv_p_o)[l], CSTG[0:nr, :]), reads=[("A", "cstg")], chan="st_conv")
                    A3 = ACC[:, 0:ns].rearrange("p (s q) -> p s q", s=nseq)
                    S.add("dve", ts(A3, R3[:, :, 0:ntok], P(l, 48 + c * 4), P(l, 96 + c), ALU.mult, ALU.add), reads=[rkey] + CK, writes=[("A", "acc")])
                    for jt in range(1, 4):
                        S.add("dve", stt(A3, R3[:, :, jt:jt + ntok], P(l, 48 + c * 4 + jt), A3, ALU.mult, ALU.add), reads=[rkey, ("A", "acc")] + CK, writes=[("A", "acc")])
                    S.add("act", act(MIX[:, c, 0:ns], ACC[:, 0:ns], AF.Silu), reads=[("A", "acc")], writes=[("A", "mix", c, t) for (_, _, t) in ltile])
            s = ring_next()
            _, wv = colblock_load(s, w_in[l], 2560, 16, 8, "gu0", 0)
            for (lc, n, t) in ltile:
                b = alloc()
                S.add("pe", mm(PS[0:16, b, 0:n], [(wv[:, k, :], us(k, t, n)) for k in range(8)]),
                      reads=[("ring", s, "gu0")] + [uskey(k, t) for k in range(8)], writes=pk(b))
                S.add("act", act(DTT[:, lc:lc + n], PS[0:16, b, 0:n], AF.Exp, bias=PRM[0:16, l * PL + 140:l * PL + 141]), reads=pk(b) + CK, writes=[("A", "dtt", t)])
                S.add("act", act(DTT[:, lc:lc + n], DTT[:, lc:lc + n], AF.Ln, bias=1.0), reads=[("A", "dtt", t)], writes=[("A", "dtt", t)])
            for cb in range(4):
                s = ring_next()
                _, wv = colblock_load(s, w_in[l], cb * 256, 256, 8, "gu0", 0)
                for m in range(2):
                    c = cb * 2 + m
                    for (lc, n, t) in ltile:
                        b = alloc()
                        S.add("pe", mm(PS[:, b, 0:n], [(wv[:, k, m * 128:(m + 1) * 128], us(k, t, n)) for k in range(8)]),
                              reads=[("ring", s, "gu0")] + [uskey(k, t) for k in range(8)], writes=pk(b))
                        S.add("act", act(ZB[:, c, lc:lc + n], PS[:, b, 0:n], AF.Silu), reads=pk(b), writes=[("A", "zb", c, t)])
            ckpt()
            S.fence()
            AB = A_B[:, l, :]
            DB = P(l, 124, 16)
            if (not smp) and sidx == 0:
                S.add("dve", ms(HST[:], 0.0), writes=[("hst",)])
                S.add("dve", ms(HB[:], 0.0), writes=[("hb",)])
            if smp:
                S.add("sp", dma(E16[:, :], e16_d[:, :]), writes=[("A", "e16")], chan="ld_e16")
            nq = 64 if smp else 128
            nch = 1 if smp else min(8, int(os.environ.get("MK_NCH", 8)))
            TR = TRISEG[0:64, 0:64] if smp else TRI
            def chunk_gen(ci):
                q0 = ci * 128
                par = 0 if smp else ci % 2
                sfx = str(par)
                BIG, WW, XDT, XDTD, XSD, YT, GN, BTOK, SM, SML = SETS[par]
                DTTOK, DA, CUMCOL, EXPA, CDB, DOUT, SSQ, RSTD, TOTT = (SML[:, i, :] for i in range(9))
                tg = tiles[q0 // 512] if not smp else tiles[0]
                mk = lambda c: ("A", "mix", c, tg)
                bx = alloc(2)
                S.add("pe", mms([(PS[0:nq, bx + jj // 4, (jj % 4) * 128:(jj % 4 + 1) * 128], [(MIX[:, jj, q0:q0 + nq], IDB[:])]) for jj in range(8)]),
                      reads=[mk(c) for c in range(8)] + CK, writes=pk(bx, 2))
                bb = alloc()
                S.add("pe", mms([(PS[0:nq, bb, g * 128:(g + 1) * 128], [(MIX[:, 8 + g, q0:q0 + nq], IDB[:])]) for g in range(2)]
                                + [(PS[0:nq, bb, 256:272], [(DTT[0:16, q0:q0 + nq], IDF[0:16, 0:16])])]),
                      reads=[mk(8), mk(9), ("A", "dtt", tg)] + CK, writes=pk(bb))
                S.add("dve", cp(DTTOK[0:nq, :], PS[0:nq, bb, 256:272]), reads=pk(bb), writes=[("A", "dttok" + sfx)])
                S.add("dve", tt(DA[0:nq, :], PS[0:nq, bb, 256:272], AB[0:nq, :], ALU.mult), reads=pk(bb) + CK, writes=[("A", "da" + sfx)])
                DAHL = SML[0:nq, 14, :].bitcast(BF16)
                S.add("dve", cp(DAHL[:, 0:16], DA[0:nq, :]), reads=[("A", "da" + sfx)], writes=[("A", "dahl" + sfx)])
                S.add("dve", tt(DAHL[:, 16:32], DA[0:nq, :], DAHL[:, 0:16], ALU.subtract), reads=[("A", "da" + sfx), ("A", "dahl" + sfx)], writes=[("A", "dahl" + sfx)])
                S.add("act", act(BTOK[0:nq, :], PS[0:nq, bb, 0:256], AF.Copy), reads=pk(bb), writes=[("A", "btok" + sfx)])
                sub(1)
                bc = alloc()
                grp = [(PS[0:nq, bc, 0:16], [(TR[0:nq, 0:nq], DA[0:nq, :])])]
                if smp:
                    grp.append((PS[0:nq, bc, 16:32], [(BLK[0:64, 0:64], DA[0:nq, :])]))
                else:
                    grp.append((PS[0:nq, bc, 32:48], [(ONESF[0:nq, 0:nq], DA[0:nq, :])]))
                S.add("pe", mms(grp), reads=[("A", "da" + sfx)] + CK, writes=pk(bc))
                S.add("dve", cp(CUMCOL[0:nq, :], PS[0:nq, bc, 0:16]), reads=pk(bc), writes=[("A", "cumcol" + sfx)])
                S.add("act", act(EXPA[0:nq, :], PS[0:nq, bc, 0:16], AF.Exp), reads=pk(bc), writes=[("A", "expa" + sfx)])
                if not smp:
                    S.add("act", act(CDB[:, :], PS[:, bc, 32:48], AF.Exp), reads=pk(bc), writes=[("A", "cdb" + sfx, 0), ("A", "cdb" + sfx, 1)])
                if smp:
                    S.add("dve", tt(DOUT[0:nq, :], PS[0:nq, bc, 16:32], CUMCOL[0:nq, :], ALU.subtract), reads=pk(bc) + [("A", "cumcol" + sfx)], writes=[("A", "dout" + sfx)])
                    S.add("act", act(DOUT[0:nq, :], DOUT[0:nq, :], AF.Exp), reads=[("A", "dout" + sfx)], writes=[("A", "dout" + sfx)])
                sub(2)
                xs3 = PS[0:nq, bx:bx + 2, :].rearrange("p a (r q) -> p (a r) q", q=64)
                S.add("dve", tt(XDT[0:nq, :].rearrange("p (r q) -> p r q", q=64), xs3, DTTOK[0:nq, :].unsqueeze(2).broadcast_to([nq, 16, 64]), ALU.mult),
                      reads=pk(bx, 2) + [("A", "dttok" + sfx)], writes=[("A", "xdt" + sfx)])
                S.add("dve", tt(XSD[0:nq, :].rearrange("p (r q) -> p r q", q=64), xs3, DB[0:nq, :].unsqueeze(2).broadcast_to([nq, 16, 64]), ALU.mult),
                      reads=pk(bx, 2) + CK, writes=[("A", "xsd" + sfx)])
                sub(3)
                bs = alloc()
                S.add("pe", mms([(PS[0:nq, bs, g * 128:g * 128 + nq], [(MIX[:, 8 + g, q0:q0 + nq], MIX[:, 10 + g, q0:q0 + nq])]) for g in range(2)]),
                      reads=[mk(8), mk(9), mk(10), mk(11)], writes=pk(bs))
                S.add("dve", tt(SM[0:nq, :, 0:nq], PS[0:nq, bs, 0:256].rearrange("p (g q) -> p g q", g=2)[:, :, 0:nq],
                                TR[0:nq, 0:nq].unsqueeze(1).broadcast_to([nq, 2, nq]), ALU.mult), reads=pk(bs) + CK, writes=[("A", "sm" + sfx)])
                sub(4)
                nbk = (8 * nq) // 512
                SLB = STRSEGB[0:64, 0:64] if smp else STRB[:, :]
                grp_state = []
                for g in range(2):
                    BIGg, WWg = SETS[g][0], SETS[g][1]
                    gs = "g%d" % g
                    B3 = BIGg[0:nq, 0:8 * nq].rearrange("p (r q) -> p r q", q=nq)
                    W3 = WWg[0:nq, 0:8 * nq].rearrange("p (r q) -> p r q", q=nq)
                    BH = BIGg[0:nq, 0:512].bitcast(BF16)
                    BL = BIGg[0:nq, 512:1024].bitcast(BF16)
                    for hl, BX in ((0, BH), (1, BL)):
                        S.add("pool", tt(BX[:, 0:8 * nq].rearrange("p (r q) -> p r q", q=nq), TR[0:nq, 0:nq].unsqueeze(1).broadcast_to([nq, 8, nq]),
                                         DAHL[:, hl * 16 + 8 * g: hl * 16 + 8 * g + 8].unsqueeze(2).broadcast_to([nq, 8, nq]), ALU.mult),
                              reads=[("A", "dahl" + sfx)] + CK, writes=[("A", "big" + gs)])
                    br = alloc(nbk)
                    S.add("pe", mms([(PS[0:nq, br + i, :], [(SLB[0:nq, 0:nq], BH[:, i * 512:(i + 1) * 512]), (SLB[0:nq, 0:nq], BL[:, i * 512:(i + 1) * 512])])
                                     for i in range(nbk)]),
                          reads=[("A", "big" + gs)] + CK, writes=pk(br, nbk))
                    if nbk == 2:
                        cr3 = PS[0:nq, br:br + 2, :].rearrange("p a (r q) -> p (a r) q", q=nq)
                    else:
                        cr3 = PS[0:nq, br, :].rearrange("p (r q) -> p r q", q=nq)
                    grp_state.append((g, gs, B3, W3, br, cr3))
                for (g, gs, B3, W3, br, cr3) in grp_state:
                    S.add("act", act(B3, cr3, AF.Exp), reads=pk(br, nbk), writes=[("A", "big" + gs)])
                    if not smp:
                        S.add("dve", cp(DOUT[0:nq, 8 * g:8 * g + 8], B3[:, :, nq - 1]), reads=[("A", "big" + gs)], writes=[("A", "dout" + sfx)])
                    S.add("dve", tt(W3, B3, SM[0:nq, g, 0:nq].unsqueeze(1).broadcast_to([nq, 8, nq]), ALU.mult),
                          reads=[("A", "big" + gs), ("A", "sm" + sfx)], writes=[("A", "ww" + gs)])
                by = alloc(2, hold=True)
                for (g, gs, B3, W3, br, cr3) in grp_state:
                    def ydiag(g=g, W3=W3, by=by, nq=nq, XSD=XSD, XDT=XDT):
                        def fn(e):
                            e.matmul(PS[0:nq, by + g, :], IDB[0:nq, 0:nq], XSD[0:nq, g * 512:(g + 1) * 512], start=True, stop=False)
                            for r in range(8):
                                m_ = e.matmul(PS[0:nq, by + g, r * 64:(r + 1) * 64], W3[:, r, :],
                                              XDT[0:nq, (8 * g + r) * 64:(8 * g + r + 1) * 64], start=False, stop=(r == 7))
                            return m_
                        return fn
                    S.add("pe", ydiag(), reads=[("A", "ww" + gs), ("A", "xdt" + sfx), ("A", "xsd" + sfx)] + CK, writes=pk(by + g))
                sub(5)
                S.add("pool", tt(XDTD[0:nq, :].rearrange("p (r q) -> p r q", q=64), XDT[0:nq, :].rearrange("p (r q) -> p r q", q=64),
                                DOUT[0:nq, :].unsqueeze(2).broadcast_to([nq, 16, 64]), ALU.mult), reads=[("A", "xdt" + sfx), ("A", "dout" + sfx)], writes=[("A", "xdtd" + sfx)])
                bo = alloc(2, hold=True)
                if not smp:
                    S.add("pe", mms([(PS[0:nq, bo + g, :], [(MIX[:, 10 + g, q0:q0 + nq], HB[:, g * 512:(g + 1) * 512])]) for g in range(2)]),
                          reads=[mk(10), mk(11), ("hb",)], writes=pk(bo, 2))
                else:
                    S.add("dve", ts(SML[0:16, 10:14, :].rearrange("p a b -> p (a b)"), DTT[0:16, 0:64], ACOL[:, l:l + 1], 0.0, ALU.mult, ALU.add),
                          reads=[("A", "dtt", tg)] + CK, writes=[("A", "dat")])
                    S.add("dve", lambda e: e.tensor_reduce(out=TOTT[0:16, :], in_=SML[0:16, 10:14, :].rearrange("p a b -> p (a b)").rearrange("p (s q) -> p s q", q=4),
                                                            axis=AX.X, op=ALU.add), reads=[("A", "dat")], writes=[("A", "tott")])
                    bcd = alloc()
                    S.add("pe", mms([(PS[:, bcd, j * 16:(j + 1) * 16], [(E16[0:16, j * 128:(j + 1) * 128], TOTT[0:16, :])]) for j in range(8)]),
                          reads=[("A", "e16"), ("A", "tott")], writes=pk(bcd))
                    S.add("act", act(CDF[:, :, :], PS[:, bcd, 0:128].rearrange("p (j s) -> p j s", j=8), AF.Exp), reads=pk(bcd), writes=[("A", "cdf")])
                    hbufs = [(HST, ("hst",)), (H0X, ("A", "h0x")), (av(8192, [128, 1024]), ("A", "h0y")), (av(12288, [128, 1024]), ("A", "h0z"))]

                    def hload(sq_):
                        H0_, hkey_ = hbufs[sq_ % 4]
                        S.add("sp", dma(H0_[:, :].rearrange("p (j n) -> p j n", j=8), sssm[l, sq_].rearrange("(j p) n -> p j n", p=128)),
                              writes=[hkey_], chan="ld_h%d" % (sq_ % 4))
                    hload(0)
                    hload(1)
                    for sq in range(16):
                        H0, hkey = hbufs[sq % 4]
                        H03 = H0[:, :].rearrange("p (j n) -> p j n", j=8)
                        if sq + 2 < 16:
                            hload(sq + 2)
                        bt = alloc(2)
                        S.add("pe", mms([(PS[:, bt + j // 4, (j % 4) * 128:(j % 4 + 1) * 128], [(H03[:, j, :], IDF)]) for j in range(8)]),
                              reads=[hkey] + CK, writes=pk(bt, 2))
                        S.add("act", act(HB[:, :].rearrange("p (a b) -> p a b", a=2), PS[:, bt:bt + 2, :], AF.Copy), reads=pk(bt, 2), writes=[("hb",)])
                        cs_ = sq % 2
                        S.add("dve", ms(CMS[:, cs_, :, :], 0.0), writes=[("A", "cms", cs_)])
                        S.add("dve", cp(CMS[:, cs_, :, 4 * sq:4 * sq + 4], MIX[:, 10:12, 4 * sq:4 * sq + 4]), reads=[mk(10), mk(11)], writes=[("A", "cms", cs_)])
                        S.add("pe", (lambda sq=sq, cs_=cs_, bo=bo: (lambda e: [e.matmul(PS[0:64, bo + g, :], CMS[:, cs_, g, :], HB[:, g * 512:(g + 1) * 512],
                                                                                    start=(sq == 0), stop=(sq == 15)) for g in range(2)][-1]))(),
                              reads=[("A", "cms", cs_), ("hb",)], writes=pk(bo, 2))
                        S.add("dve", ts(BMS[0:64, cs_, :], BTOK[0:64, :], MASKCOL[0:64, sq:sq + 1], 0.0, ALU.mult, ALU.add),
                              reads=[("A", "btok" + sfx)] + CK, writes=[("A", "bms", cs_)])
                        bn = alloc(2)
                        S.add("pe", mms([(PS[:, bn + j // 4, (j % 4) * 128:(j % 4 + 1) * 128],
                                          [(XDTD[0:64, j * 128:(j + 1) * 128], BMS[0:64, cs_, (j // 4) * 128:(j // 4 + 1) * 128])]) for j in range(8)]),
                              reads=[("A", "xdtd" + sfx), ("A", "bms", cs_)], writes=pk(bn, 2))
                        S.add("dve", tt(H03, H03, CDF[:, :, sq].unsqueeze(2).broadcast_to([128, 8, 128]), ALU.mult), reads=[hkey, ("A", "cdf")], writes=[hkey])
                        S.add("dve", tt(H03, H03, PS[:, bn:bn + 2, :].rearrange("p a (j n) -> p (a j) n", n=128), ALU.add), reads=[hkey] + pk(bn, 2), writes=[hkey])
                        S.add("sp", dma(ssm_s_o[l, sq].rearrange("(j p) n -> p j n", p=128), H03), reads=[hkey], chan="st_h%d" % (sq % 4))
                if not smp:
                    bsp = alloc(2)
                    S.add("pe", mms([(PS[:, bsp + g, :], [(BTOK[:, g * 128:(g + 1) * 128], XDTD[:, g * 512:(g + 1) * 512])]) for g in range(2)]),
                          reads=[("A", "btok" + sfx), ("A", "xdtd" + sfx)], writes=pk(bsp, 2))
                    S.add("pool", tt(HST[:, :].rearrange("p (r q) -> p r q", q=64), HST[:, :].rearrange("p (r q) -> p r q", q=64),
                                    CDB[:, :].unsqueeze(2).broadcast_to([128, 16, 64]), ALU.mult), reads=[("hst",), ("A", "cdb" + sfx, 0), ("A", "cdb" + sfx, 1)], writes=[("hst",)])
                    S.add("dve", tt(HST[:, :].rearrange("p (a b) -> p a b", a=2), HST[:, :].rearrange("p (a b) -> p a b", a=2), PS[:, bsp:bsp + 2, :], ALU.add),
                          reads=[("hst",)] + pk(bsp, 2), writes=[("hst",)])
                    S.add("act", act(HB[:], HST[:], AF.Copy), reads=[("hst",)], writes=[("hb",)])
                sub(6)
                S.add("dve", tt(YT[0:nq, :].rearrange("p (r q) -> p r q", q=64), PS[0:nq, bo:bo + 2, :].rearrange("p a (r q) -> p (a r) q", q=64),
                                EXPA[0:nq, :].unsqueeze(2).broadcast_to([nq, 16, 64]), ALU.mult), reads=pk(bo, 2) + [("A", "expa" + sfx)], writes=[("A", "yt" + sfx)])
                release(bo, 2)
                S.add("dve", tt(YT[0:nq, :].rearrange("p (a b) -> p a b", a=2), YT[0:nq, :].rearrange("p (a b) -> p a b", a=2), PS[0:nq, by:by + 2, :], ALU.add),
                      reads=pk(by, 2) + [("A", "yt" + sfx)], writes=[("A", "yt" + sfx)])
                release(by, 2)
                yield
                bz = alloc(2)
                S.add("pe", mms([(PS[0:nq, bz + jj // 4, (jj % 4) * 128:(jj % 4 + 1) * 128], [(ZB[:, jj, q0:q0 + nq], IDB[:])]) for jj in range(8)]),
                      reads=[("A", "zb", c, tg) for c in range(8)] + CK, writes=pk(bz, 2))
                S.add("dve", tt(YT[0:nq, :].rearrange("p (a b) -> p a b", a=2), YT[0:nq, :].rearrange("p (a b) -> p a b", a=2), PS[0:nq, bz:bz + 2, :], ALU.mult),
                      reads=pk(bz, 2) + [("A", "yt" + sfx)], writes=[("A", "yt" + sfx)])
                S.add("dve", ms(SSQ[0:nq, 0:1], 0.0), writes=[("A", "ssq" + sfx)])
                S.add("act", act(GN[0:nq, :], YT[0:nq, :], AF.Square, accum_out=SSQ[0:nq, 0:1]), reads=[("A", "yt" + sfx), ("A", "ssq" + sfx)], writes=[("A", "gn" + sfx), ("A", "ssq" + sfx)])
                S.add("act", act(RSTD[0:nq, 0:1], SSQ[0:nq, 0:1], AF.Ln, bias=EPS[0:nq, 0:1], scale=1.0 / 1024.0), reads=[("A", "ssq" + sfx)] + CK, writes=[("A", "rstd" + sfx)])
                S.add("act", act(RSTD[0:nq, 0:1], RSTD[0:nq, 0:1], AF.Exp, scale=-0.5), reads=[("A", "rstd" + sfx)], writes=[("A", "rstd" + sfx)])
                S.add("act", act(GN[0:nq, :], YT[0:nq, :], AF.Copy, scale=RSTD[0:nq, 0:1]), reads=[("A", "yt" + sfx), ("A", "rstd" + sfx)], writes=[("A", "gn" + sfx)])
                sub(7)
                bk = alloc(2)
                S.add("pe", mms([(PS[:, bk + jj // 4, (jj % 4) * 128:(jj % 4) * 128 + nq], [(GN[0:nq, jj * 128:(jj + 1) * 128], IDB[0:nq, 0:nq])]) for jj in range(8)]),
                      reads=[("A", "gn" + sfx)] + CK, writes=pk(bk, 2))
                for a in range(2):
                    S.add("dve", tt(MIX[:, 4 * a:4 * a + 4, q0:q0 + nq], PS[:, bk + a, :].rearrange("p (j q) -> p j q", j=4)[:, :, 0:nq],
                                    P(l, 32 + 4 * a, 4).unsqueeze(2).broadcast_to([128, 4, nq]), ALU.mult),
                          reads=pk(bk + a) + CK, writes=[mk(c) for c in range(4 * a, 4 * a + 4)])
            prev = None
            for ci in range(nch):
                gcur = chunk_gen(ci)
                next(gcur)
                if prev is not None:
                    for _ in prev:
                        pass
                prev = gcur
            for _ in prev:
                pass
            if (not smp) and sidx == 1:
                YT = SETS[0][5]
                sfx = "0"
                bf = alloc(2)
                S.add("pe", mms([(PS[:, bf + j // 4, (j % 4) * 128:(j % 4 + 1) * 128], [(HST[:, j * 128:(j + 1) * 128], IDF)]) for j in range(8)]),
                      reads=[("hst",)] + CK, writes=pk(bf, 2))
                S.add("act", act(YT[:, :].rearrange("p (a b) -> p a b", a=2), PS[:, bf:bf + 2, :], AF.Copy), reads=pk(bf, 2), writes=[("A", "yt" + sfx)])
                S.add("sp", dma(ssm_p_o[l].rearrange("(j p) n -> p j n", p=128), YT[:, :].rearrange("p (j n) -> p j n", j=8)), reads=[("A", "yt" + sfx)], chan="st_ssmp")
            ckpt()
            for jb in range(4):
                s = ring_next()
                _, wv = colblock_load(s, w_out[l, 0:1024], jb * 256, 256, 8, "gu0", 0)
                for m in range(2):
                    j = jb * 2 + m
                    for (lc, n, t) in ltile:
                        b = alloc()
                        S.add("pe", mm(PS[:, b, 0:n], [(wv[:, k, m * 128:(m + 1) * 128], MIX[:, k, lc:lc + n]) for k in range(8)]),
                              reads=[("ring", s, "gu0")] + [("A", "mix", k, t) for k in range(8)], writes=pk(b))
                        g0 = TILES[t][0]
                        S.add("dve", tt(X[:, j, g0:g0 + n], X[:, j, g0:g0 + n], PS[:, b, 0:n], ALU.add), reads=pk(b) + [xk(j, t)], writes=[xk(j, t)])

        def ple(l):
            S.fence()
            for blk in range(17):
                r0 = blk * 128
                n = 128 if blk < 16 else 64
                t = blk // 4
                sl = blk % 2
                S.add("sp", dma(PSTG[0:n, sl, :], pin[l, r0:r0 + n, :]), writes=[("A", "pstg", sl)], chan="ld_p%d" % sl)
                b = alloc()
                S.add("pe", mms([(PS[:, b, k * 128:k * 128 + n], [(PSTG[0:n, sl, k * 128:(k + 1) * 128], IDF[0:n, 0:n])]) for k in range(2)]),
                      reads=[("A", "pstg", sl)] + CK, writes=pk(b))
                S.add("act", act(PT[:, :, r0:r0 + n], PS[:, b, 0:256].rearrange("p (k q) -> p k q", k=2)[:, :, 0:n], AF.Copy), reads=pk(b), writes=[("A", "pt", blk)])
            norm(range(5), l, 24, u_full, ukey)
            for jb in range(4):
                s = ring_next()
                _, wv = colblock_load(s, ple_gate[l], jb * 256, 256, 8, "gu0", 0)
                wload(s, "gu1", RING[:, s, 2048:2560].rearrange("p (k n) -> p k n", k=2),
                      ple_proj[l][:, jb * 256:(jb + 1) * 256].rearrange("(k p) n -> p k n", p=128))
                pv = RING[:, s, 2048:2560].rearrange("p (k n) -> p k n", k=2)
                for m in range(2):
                    j = jb * 2 + m
                    for t in range(5):
                        t0, n = TILES[t]
                        b = alloc(2)
                        S.add("pe", mms([(PS[:, b, 0:n], [(wv[:, k, m * 128:(m + 1) * 128], U[:, k, t0:t0 + n]) for k in range(8)]),
                                         (PS[:, b + 1, 0:n], [(pv[:, k, m * 128:(m + 1) * 128], PT[:, k, t0:t0 + n]) for k in range(2)])]),
                              reads=[("ring", s, "gu0"), ("ring", s, "gu1")] + [ukey(k, t) for k in range(8)] + [("A", "pt", bb_) for bb_ in range(17)], writes=pk(b, 2))
                        S.add("act", act(SGT[:, 0:n], PS[:, b, 0:n], AF.Sigmoid), reads=pk(b), writes=[("A", "sgt")])
                        S.add("dve", tt(TMP2[:, 0:n], SGT[:, 0:n], PS[:, b + 1, 0:n], ALU.mult), reads=[("A", "sgt")] + pk(b + 1), writes=[("A", "tmp2")])
                        S.add("dve", tt(X[:, j, t0:t0 + n], X[:, j, t0:t0 + n], TMP2[:, 0:n], ALU.add), reads=[("A", "tmp2"), xk(j, t)], writes=[xk(j, t)])

        SEGS = [("P", 0, 0, 1024, [0, 1]), ("P", 1, 1024, 1024, [2, 3]), ("S", 2, 2048, 64, [4])]
        try:
            ckpt()
            for l in range(nlayers):
                ffn(l, 0)
                ckpt()
                for seg in SEGS:
                    mixer(l, seg)
                    ckpt()
                ffn(l, 1)
                ckpt()
                ple(l)
                ckpt()
        except _Stop:
            pass
        S.fence()
        norm(range(5), None, 0, lambda k, t, n: X[:, k, TILES[t][0]:TILES[t][0] + n], lambda k, t: xk(k, t))
        S.fence()
        for blk in range(17):
            r0 = blk * 128
            n = 128 if blk < 16 else 64
            t = blk // 4
            sl = blk % 2
            b = alloc(2)
            S.add("pe", mms([(PS[0:n, b + j // 4, (j % 4) * 128:(j % 4 + 1) * 128], [(X[:, j, r0:r0 + n], IDF)]) for j in range(8)]),
                  reads=[xk(j, t) for j in range(8)] + CK, writes=pk(b, 2))
            if blk % 2 == 0:
                S.add("act", act(XSTG[0:n, sl, :].rearrange("p (a b) -> p a b", a=2), PS[0:n, b:b + 2, :], AF.Copy), reads=pk(b, 2), writes=[("A", "xstg", sl)])
            else:
                S.add("dve", cp(XSTG[0:n, sl, :].rearrange("p (a b) -> p a b", a=2), PS[0:n, b:b + 2, :]), reads=pk(b, 2), writes=[("A", "xstg", sl)])
            S.add("sp", dma(y_o[r0:r0 + n, :], XSTG[0:n, sl, :]), reads=[("A", "xstg", sl)], chan="st_y%d" % sl)
        S.emit(nc, es)
    return nc


def _host_consts():
    cst = np.zeros((128, NCST), np.float32)
    cst[:, 0:128] = np.eye(128, dtype=np.float32)
    s = np.arange(128)[:, None]
    q = np.arange(128)[None, :]
    cst[:, 128:256] = (s <= q).astype(np.float32)
    s6 = np.arange(64)[:, None]
    q6 = np.arange(64)[None, :]
    cst[0:64, 256:320] = ((s6 <= q6) & (s6 // 4 == q6 // 4)).astype(np.float32)
    cst[0:64, 320:384] = (s6 // 4 == q6 // 4).astype(np.float32)
    cst[0:64, 384:400] = (s6 // 4 == np.arange(16)[None, :]).astype(np.float32)
    for gi, w in enumerate((2, 4, 8, 16)):
        cst[:, 400 + gi * 16: 400 + (gi + 1) * 16] = (1.0 / np.minimum(np.arange(16) + 1, w)).astype(np.float32)[None, :]
    cst[:, 464:592] = (s > q).astype(np.float32)
    cst[0:64, 592:656] = ((s6 > q6) & (s6 // 4 == q6 // 4)).astype(np.float32)
    e16 = np.zeros((16, 1024), np.float32)
    for r in range(16):
        e16[r, r * 64:(r + 1) * 64] = 1.0
    return cst, e16


def _host_params(inp):
    prm = np.zeros((128, NPRM), np.float32)

    def cols(v):
        return np.asarray(v, np.float32).reshape(8, 128).T

    for l in range(L):
        b = l * PL
        prm[:, b + 0:b + 8] = cols(inp["norm_ffn1"][l])
        prm[:, b + 8:b + 16] = cols(inp["norm_mix"][l])
        prm[:, b + 16:b + 24] = cols(inp["norm_ffn2"][l])
        prm[:, b + 24:b + 32] = cols(inp["norm_ple"][l])
        prm[:, b + 32:b + 40] = cols(inp["ssd_norm_w"][l])
        prm[:, b + 40:b + 48] = cols(inp["pool_scale"][l])
        cw = np.asarray(inp["conv_w"][l], np.float32)
        prm[:, b + 48:b + 96] = cw.reshape(4, 12, 128).transpose(2, 1, 0).reshape(128, 48)
        prm[:, b + 96:b + 108] = np.asarray(inp["conv_b"][l], np.float32).reshape(12, 128).T
        prm[:, b + 108:b + 124] = np.asarray(inp["a_log"][l], np.float32)[None, :]
        prm[:, b + 124:b + 140] = np.asarray(inp["d_skip"][l], np.float32)[None, :]
        prm[0:16, b + 140] = np.asarray(inp["dt_bias"][l], np.float32)
        prm[0:16, b + 141] = np.asarray(inp["a_log"][l], np.float32)
    prm[:, L * PL:L * PL + 8] = cols(inp["final_norm"])
    return prm


_NC_CACHE = {}


def kernel(**inp):
    inp = {k: np.asarray(v) for k, v in inp.items()}
    nlayers = int(os.environ.get("MK_LAYERS", L))
    if nlayers not in _NC_CACHE:
        _NC_CACHE[nlayers] = build_nc(nlayers)
    nc = _NC_CACHE[nlayers]
    cst, e16 = _host_consts()
    prm = _host_params(inp)
    shared = {k: np.ascontiguousarray(inp[k], dtype=np.float32) for k in
              ("w_in", "w_out", "pool_w", "ffn1_gate", "ffn2_gate", "ffn1_up", "ffn2_up", "ffn1_down", "ffn2_down", "ple_gate", "ple_proj")}
    in_maps = []
    for c in range(NCORE):
        sl = slice(16 * c, 16 * c + 16)
        m = dict(shared)
        m["xin"] = np.ascontiguousarray(np.concatenate([inp["x_prompt"][c], inp["x_sample"][sl].reshape(NSM, D)], axis=0), dtype=np.float32)
        m["pin"] = np.ascontiguousarray(np.concatenate([inp["p_prompt"][:, c], inp["p_sample"][:, sl].reshape(L, NSM, 256)], axis=1), dtype=np.float32)
        m["sssm"] = np.ascontiguousarray(inp["state_ssm"][:, sl].reshape(L, 16, 1024, 128), dtype=np.float32)
        m["sconv"] = np.ascontiguousarray(inp["state_conv"][:, sl].reshape(L, 48, CONV), dtype=np.float32)
        m["spool"] = np.ascontiguousarray(inp["state_pool"][:, sl].reshape(L, 240, 1024), dtype=np.float32)
        m["prm"] = prm
        m["cst"] = cst
        m["e16"] = e16
        in_maps.append(m)
    ncr = int(os.environ.get("MK_CORES", NCORE))
    res = run_bass_kernel_spmd(nc, in_maps[:ncr], core_ids=list(range(ncr)))
    R = list(res.results) + [res.results[0]] * (NCORE - ncr)
    y_prompt = np.stack([R[c]["y"][0:NPR] for c in range(NCORE)], axis=0)
    y_sample = np.concatenate([R[c]["y"][NPR:].reshape(16, 4, D) for c in range(NCORE)], axis=0)
    ssm_prompt = np.stack([R[c]["ssm_p"].reshape(L, 16, 64, 128) for c in range(NCORE)], axis=1)
    conv_prompt = np.stack([R[c]["conv_p"] for c in range(NCORE)], axis=1)
    pool_prompt = np.stack([R[c]["pool_p"] for c in range(NCORE)], axis=1)
    ssm_sample = np.concatenate([R[c]["ssm_s"].reshape(L, 16, 16, 64, 128) for c in range(NCORE)], axis=1)
    conv_sample = np.concatenate([R[c]["conv_s"].reshape(L, 16, 3, CONV) for c in range(NCORE)], axis=1)
    pool_sample = np.concatenate([R[c]["pool_s"].reshape(L, 16, 15, 1024) for c in range(NCORE)], axis=1)
    f = lambda a: np.ascontiguousarray(a, dtype=np.float32)
    return (f(y_prompt), f(y_sample), f(ssm_prompt), f(conv_prompt), f(pool_prompt), f(ssm_sample), f(conv_sample), f(pool_sample))
```
